# Optimizing a Trainium2 kernel written in Bass

```python
import math
import jax, jax.numpy as jnp
from jax import lax
import numpy as np

D_MODEL = 1024
BATCH = 4
SEQ = 4096
DEPTH = 2

CTX_LEN = 256
GRID_W = 64

SSD_D_INNER = 2 * D_MODEL
SSD_HEADDIM = 64
SSD_HEADS = SSD_D_INNER // SSD_HEADDIM
SSD_GROUPS = 8
SSD_HPG = SSD_HEADS // SSD_GROUPS
SSD_STATE = 128
SSD_CONV = 5
SSD_CHUNK = 128
SSD_GN = SSD_GROUPS * SSD_STATE
XBC_WIDTH = SSD_D_INNER + 2 * SSD_GN

POOL_WIDTH = D_MODEL
POOL_WINDOWS = (2, 4, 8, 16)
POOL_GROUP = POOL_WIDTH // len(POOL_WINDOWS)

FFN_HIDDEN = -(-(8 * D_MODEL) // (3 * 256)) * 256

IN_COLS = SSD_D_INNER + XBC_WIDTH + 2 * SSD_HEADS + POOL_WIDTH + 2 * D_MODEL
IN_SPLITS = (SSD_D_INNER,
             SSD_D_INNER + XBC_WIDTH,
             SSD_D_INNER + XBC_WIDTH + 2 * SSD_HEADS,
             SSD_D_INNER + XBC_WIDTH + 2 * SSD_HEADS + POOL_WIDTH)
EPS = 1e-6

kernel_name = "hybrid_ssd_pool_dit_block"


def rmsnorm(x, g):
    xf = x.astype(jnp.float32)
    y = xf * lax.rsqrt(jnp.mean(xf * xf, axis=-1, keepdims=True) + EPS)
    return (y * g).astype(x.dtype)


def modulate(h, shift, scale):
    return h * (1.0 + scale) + shift


def adaln(cond, w, b):
    m = jax.nn.silu(cond) @ w + b
    return jnp.split(m[:, None, :], 6, axis=-1)


def dwconv_centred(u, w, b):
    k = w.shape[0]
    pad = k // 2
    L = u.shape[1]
    up = jnp.pad(u, ((0, 0), (pad, pad), (0, 0)))
    out = b
    for i in range(k):
        out = out + w[i] * up[:, i:i + L]
    return out


def pool_minus_self(u):
    L = u.shape[-2]
    uf = u.astype(jnp.float32)
    cs = jnp.cumsum(uf, axis=-2)
    cs = jnp.concatenate([jnp.zeros_like(cs[..., :1, :]), cs], axis=-2)
    t = jnp.arange(L)
    outs = []
    for gi, k in enumerate(POOL_WINDOWS):
        lo = jnp.clip(t - k // 2, 0, L)
        hi = jnp.clip(t + k // 2, 0, L)
        seg = cs[..., gi * POOL_GROUP:(gi + 1) * POOL_GROUP]
        s = jnp.take(seg, hi, axis=-2) - jnp.take(seg, lo, axis=-2)
        outs.append(s / (hi - lo).astype(jnp.float32)[:, None])
    mean = jnp.concatenate(outs, axis=-1)
    return (mean - uf).astype(u.dtype)


def ssd_scan(xh, dt, a_log, bm, cm, init_state, return_y):
    Bsz, L, H, P = xh.shape
    G, N = bm.shape[2], bm.shape[3]
    R = H // G
    Q = SSD_CHUNK
    nc = L // Q
    A = -jnp.exp(a_log.astype(jnp.float32))
    acum = jnp.cumsum((dt * A).reshape(Bsz, nc, Q, H), axis=2)
    xc = (xh.astype(jnp.float32) * dt[..., None]).reshape(Bsz, nc, Q, G, R, P)
    bc = bm.astype(jnp.float32).reshape(Bsz, nc, Q, G, N)
    cc = cm.astype(jnp.float32).reshape(Bsz, nc, Q, G, N)
    a_last = acum[:, :, -1]
    decay_to_end = jnp.exp(a_last[:, :, None, :] - acum).reshape(Bsz, nc, Q, G, R)
    chunk_states = jnp.einsum('bcjgn,bcjgr,bcjgrp->bcgrpn', bc, decay_to_end, xc)

    def step(state, inp):
        cs, al = inp
        return state * jnp.exp(al)[..., None, None] + cs, state

    final, entering = lax.scan(
        step, init_state.reshape(Bsz, G, R, P, N),
        (jnp.moveaxis(chunk_states, 1, 0), jnp.moveaxis(a_last.reshape(Bsz, nc, G, R), 1, 0)))
    final = final.reshape(Bsz, H, P, N)
    if not return_y:
        return None, final
    entering = jnp.moveaxis(entering, 0, 1)
    mask = jnp.tril(jnp.ones((Q, Q), dtype=bool))[None, None, :, :, None]
    diff = acum[:, :, :, None, :] - acum[:, :, None, :, :]
    decay = jnp.exp(jnp.where(mask, diff, -jnp.inf)).reshape(Bsz, nc, Q, Q, G, R)
    cb = jnp.einsum('bcign,bcjgn->bcijg', cc, bc)
    y_intra = jnp.einsum('bcijgr,bcjgrp->bcigrp', cb[..., None] * decay, xc)
    y_inter = (jnp.einsum('bcign,bcgrpn->bcigrp', cc, entering)
               * jnp.exp(acum).reshape(Bsz, nc, Q, G, R)[..., None])
    y = (y_intra + y_inter).reshape(Bsz, L, H, P)
    return y.astype(xh.dtype), final


def ssd_bidir(xs, dt, bm, cm, a_log, d_skip, init_states, return_y):
    ys = []
    finals = []
    for d in range(2):
        flip = (lambda a: jnp.flip(a, axis=1)) if d == 1 else (lambda a: a)
        y, s = ssd_scan(flip(xs), flip(dt[:, :, d]), a_log[d], flip(bm), flip(cm), init_states[d], return_y)
        finals.append(s)
        if return_y:
            ys.append(flip(y) + d_skip[d][:, None] * xs)
    y = (ys[0] + ys[1]) if return_y else None
    return y, finals


def prepare(h, w_in, conv_w, conv_b, dt_bias):
    Bsz, L = h.shape[0], h.shape[1]
    proj = h @ w_in
    z, xbc, dt_raw, u_pool, gate_logits = jnp.split(proj, IN_SPLITS, axis=-1)
    xbc = jax.nn.silu(dwconv_centred(xbc, conv_w, conv_b))
    xs, bm, cm = jnp.split(xbc, (SSD_D_INNER, SSD_D_INNER + SSD_GN), axis=-1)
    xs = xs.reshape(Bsz, L, SSD_HEADS, SSD_HEADDIM)
    bm = bm.reshape(Bsz, L, SSD_GROUPS, SSD_STATE)
    cm = cm.reshape(Bsz, L, SSD_GROUPS, SSD_STATE)
    dt = jax.nn.softplus((dt_raw.reshape(Bsz, L, 2, SSD_HEADS) + dt_bias).astype(jnp.float32))
    return z, xs, bm, cm, dt, u_pool, gate_logits


def mixer_out(z, y, u_pool, gate_logits, ssd_norm_w, w_ssd_out, pool_w, pool_scale, w_pool_out, w_out, on_grid):
    Bsz, L = z.shape[0], z.shape[1]
    yz = y.reshape(Bsz, L, SSD_D_INNER) * jax.nn.silu(z)
    yn = rmsnorm(yz.reshape(Bsz, L, SSD_GROUPS, SSD_D_INNER // SSD_GROUPS), 1.0).reshape(Bsz, L, SSD_D_INNER)
    o_ssd = (yn * ssd_norm_w) @ w_ssd_out
    if on_grid:
        rows = L // GRID_W
        pm = pool_minus_self(u_pool.reshape(Bsz, rows, GRID_W, POOL_WIDTH)).reshape(Bsz, L, POOL_WIDTH)
    else:
        pm = pool_minus_self(u_pool)
    pm = jnp.einsum('blgi,gio->blgo', pm.reshape(Bsz, L, len(POOL_WINDOWS), POOL_GROUP), pool_w)
    o_pool = (pm.reshape(Bsz, L, POOL_WIDTH) * pool_scale) @ w_pool_out
    g_ssd, g_pool = jnp.split(jax.nn.sigmoid(gate_logits), 2, axis=-1)
    return (g_ssd * o_ssd + g_pool * o_pool) @ w_out


def swiglu(h, w_gate_up, w_down):
    a, b = jnp.split(h @ w_gate_up, 2, axis=-1)
    return (jax.nn.silu(a) * b) @ w_down


def setup_inputs(seed: int = 0) -> dict:
    key = jax.random.key(seed)
    ks = jax.random.split(key, 24)
    f32 = jnp.float32

    def nrm(k, shape, scale):
        return jax.random.normal(k, shape, f32) * scale

    H = SSD_HEADS
    dt0 = jnp.exp(jax.random.uniform(ks[10], (DEPTH, 2, H), f32, minval=math.log(1e-3), maxval=math.log(1e-1)))
    return {
        "x": nrm(ks[0], (BATCH, SEQ, D_MODEL), 1.0),
        "c": nrm(ks[1], (BATCH, D_MODEL), 1.0),
        "ctx": nrm(ks[2], (BATCH, CTX_LEN, D_MODEL), 1.0),
        "c_ctx": nrm(ks[3], (D_MODEL,), 1.0),
        "w_ada": nrm(ks[4], (DEPTH, D_MODEL, 6 * D_MODEL), 0.5 * D_MODEL ** -0.5),
        "b_ada": nrm(ks[5], (DEPTH, 6 * D_MODEL), 0.02),
        "g_mix": 1.0 + nrm(ks[6], (DEPTH, D_MODEL), 0.05),
        "w_in": nrm(ks[7], (DEPTH, D_MODEL, IN_COLS), D_MODEL ** -0.5),
        "conv_w": nrm(ks[8], (DEPTH, SSD_CONV, XBC_WIDTH), SSD_CONV ** -0.5),
        "conv_b": nrm(ks[9], (DEPTH, XBC_WIDTH), 0.02),
        "dt_bias": dt0 + jnp.log(-jnp.expm1(-dt0)),
        "a_log": jnp.log(jax.random.uniform(ks[11], (DEPTH, 2, H), f32, minval=1.0, maxval=16.0)),
        "d_skip": 1.0 + nrm(ks[12], (DEPTH, 2, H), 0.05),
        "ssd_norm_w": 1.0 + nrm(ks[13], (DEPTH, SSD_D_INNER), 0.05),
        "w_ssd_out": nrm(ks[14], (DEPTH, SSD_D_INNER, D_MODEL), SSD_D_INNER ** -0.5),
        "pool_w": nrm(ks[15], (DEPTH, len(POOL_WINDOWS), POOL_GROUP, POOL_GROUP), POOL_GROUP ** -0.5),
        "pool_scale": 1.0 + nrm(ks[16], (DEPTH, POOL_WIDTH), 0.1),
        "w_pool_out": nrm(ks[17], (DEPTH, POOL_WIDTH, D_MODEL), POOL_WIDTH ** -0.5),
        "w_out": nrm(ks[18], (DEPTH, D_MODEL, D_MODEL), D_MODEL ** -0.5),
        "g_ffn": 1.0 + nrm(ks[19], (DEPTH, D_MODEL), 0.05),
        "w_gate_up": nrm(ks[20], (DEPTH, D_MODEL, 2 * FFN_HIDDEN), D_MODEL ** -0.5),
        "w_down": nrm(ks[21], (DEPTH, FFN_HIDDEN, D_MODEL), FFN_HIDDEN ** -0.5),
        "g_final": 1.0 + nrm(ks[22], (D_MODEL,), 0.05),
    }


def reference(x, c, ctx, c_ctx, w_ada, b_ada, g_mix, w_in, conv_w, conv_b, dt_bias, a_log, d_skip,
              ssd_norm_w, w_ssd_out, pool_w, pool_scale, w_pool_out, w_out, g_ffn, w_gate_up, w_down,
              g_final):
    cx = ctx
    for l in range(DEPTH):
        last = l == DEPTH - 1
        sh1, sc1, ga1, sh2, sc2, ga2 = adaln(c, w_ada[l], b_ada[l])
        csh1, csc1, cga1, csh2, csc2, cga2 = adaln(c_ctx[None, :], w_ada[l], b_ada[l])

        h_lat = modulate(rmsnorm(x, g_mix[l]), sh1, sc1)
        h_ctx = modulate(rmsnorm(cx, g_mix[l]), csh1, csc1)
        zc, xsc, bmc, cmc, dtc, upc, glc = prepare(h_ctx, w_in[l], conv_w[l], conv_b[l], dt_bias[l])
        zl, xsl, bml, cml, dtl, upl, gll = prepare(h_lat, w_in[l], conv_w[l], conv_b[l], dt_bias[l])
        zero = jnp.zeros((xsc.shape[0], SSD_HEADS, SSD_HEADDIM, SSD_STATE), jnp.float32)
        y_ctx, ctx_states = ssd_bidir(xsc, dtc, bmc, cmc, a_log[l], d_skip[l], (zero, zero), not last)
        y_lat, _ = ssd_bidir(xsl, dtl, bml, cml, a_log[l], d_skip[l], ctx_states, True)
        x = x + ga1 * mixer_out(zl, y_lat, upl, gll, ssd_norm_w[l], w_ssd_out[l], pool_w[l], pool_scale[l],
                                w_pool_out[l], w_out[l], True)
        x = x + ga2 * swiglu(modulate(rmsnorm(x, g_ffn[l]), sh2, sc2), w_gate_up[l], w_down[l])

        if not last:
            cx = cx + cga1 * mixer_out(zc, y_ctx, upc, glc, ssd_norm_w[l], w_ssd_out[l], pool_w[l],
                                       pool_scale[l], w_pool_out[l], w_out[l], False)
            cx = cx + cga2 * swiglu(modulate(rmsnorm(cx, g_ffn[l]), csh2, csc2), w_gate_up[l], w_down[l])
    return rmsnorm(x, g_final)
```

```python
from contextlib import ExitStack
import numpy as np
import concourse.bass as bass
import concourse.mybir as mybir
from concourse.bass_utils import run_bass_kernel_spmd

F32 = mybir.dt.float32
BF16 = mybir.dt.bfloat16
AF = mybir.ActivationFunctionType
ALU = mybir.AluOpType

D = 1024
KD = 8
DEPTH = 2
NCTX = 256
NLAT = 2048
TT = NCTX + NLAT
TTH = TT + 2
NCH = TT // 128
DIN = 2048
NH = 32
HP = 64
NG = 8
NS = 128
XBCW = 4096
FFH = 2816
INCOLS = 9280
EPS = 1e-6
CZ, CX, CGT, CPL, CDT = 0, 2048, 6144, 8192, 9216
XW = 2308
LOFF = 260

ENGS = ['pe', 'act', 'dve', 'pool', 'sp']
SAME_ENGINE_SYNC = True


class Buf:
    __slots__ = ('name', 'w', 'r')

    def __init__(self, name=''):
        self.name = name
        self.w = None
        self.r = []


class Prog:
    def __init__(self, nc, n_dma_sems=60):
        self.nc = nc
        self.ops = {e: [] for e in ENGS}
        self.sems = {}
        self.cnt = {}
        for e in ENGS:
            self.sems[e] = nc.alloc_semaphore(name=f"c_{e}")
            self.cnt[e] = 0
        self.known = {e: {} for e in ENGS}
        self.dma_keys = {}
        self.n_dma_sems = n_dma_sems

    def _dma_sem(self, key):
        if key not in self.dma_keys:
            assert len(self.dma_keys) < self.n_dma_sems, "too many dma keys"
            s = self.nc.alloc_semaphore(name=f"d_{len(self.dma_keys)}")
            k = ('dma', key)
            self.sems[k] = s
            self.cnt[k] = 0
            self.dma_keys[key] = k
        return self.dma_keys[key]

    def _need(self, eng, ev, waits):
        if ev is None:
            return
        k, v = ev
        if self.known[eng].get(k, 0) >= v:
            return
        waits[k] = max(waits.get(k, 0), v)

    def _deps(self, eng, reads, writes):
        waits = {}
        for b in reads:
            if SAME_ENGINE_SYNC or b.w is None or b.w[0] != eng:
                self._need(eng, b.w, waits)
        for b in writes:
            if SAME_ENGINE_SYNC or b.w is None or b.w[0] != eng:
                self._need(eng, b.w, waits)
            for ev in b.r:
                if ev[0] == eng:
                    continue
                self._need(eng, ev, waits)
        for k, v in waits.items():
            self.known[eng][k] = max(self.known[eng].get(k, 0), v)
        return waits

    def _commit(self, ev, reads, writes):
        for b in reads:
            b.r.append(ev)
        for b in writes:
            b.w = ev
            b.r = []

    def op(self, eng, fn, reads=(), writes=()):
        waits = self._deps(eng, reads, writes)
        self.cnt[eng] += 1
        ev = (eng, self.cnt[eng])
        self.ops[eng].append((waits, fn, (eng, 1)))
        self._commit(ev, reads, writes)
        return ev

    def dma(self, queue, key, fn, reads=(), writes=(), inc=16):
        k = self._dma_sem(key)
        waits = self._deps(queue, reads, writes)
        prev = self.cnt[k]
        if prev > 0 and self.known[queue].get(k, 0) < prev:
            waits[k] = max(waits.get(k, 0), prev)
            self.known[queue][k] = prev
        self.cnt[k] += inc
        ev = (k, self.cnt[k])
        self.ops[queue].append((waits, fn, (k, inc)))
        self._commit(ev, reads, writes)
        return ev

    def barrier(self):
        for e in ENGS:
            waits = {}
            for k, v in self.cnt.items():
                if v > 0 and self.known[e].get(k, 0) < v:
                    waits[k] = v
                    self.known[e][k] = v
            self.ops[e].append((waits, None, None))

    def flush(self):
        self.barrier()
        nc = self.nc
        sems = self.sems
        ops = self.ops

        def run(engname):
            def f(eng):
                for waits, fn, inc in ops[engname]:
                    for k, v in waits.items():
                        eng.wait_ge(sems[k], v)
                    if fn is not None:
                        ins = fn(eng)
                        ins.then_inc(sems[inc[0]], inc[1])
            return f

        with nc.Block() as block:
            block.tensor(run('pe'))
            block.scalar(run('act'))
            block.vector(run('dve'))
            block.gpsimd(run('pool'))
            block.sync(run('sp'))
        self.ops = {e: [] for e in ENGS}


def bc(ap, dims):
    return bass.AP(ap.tensor, ap.offset, [list(ap.ap[0])] + [list(d) for d in dims])


def build(n_cores=8, debug=False, stop_after=None, ext_scratch=False):
    nc = bass.Bass("TRN2", target_bir_lowering=False)
    P = Prog(nc)
    dbg_kind = "ExternalOutput" if (debug or ext_scratch) else "Internal"

    def din(name, shape):
        return nc.dram_tensor(name, list(shape), F32, kind="ExternalInput").ap()

    xT_in = din("xT", [D, TTH])
    cc_in = din("cc", [128, KD, 2])
    w_ada = din("w_ada", [DEPTH, D, 6 * D])
    b_ada = din("b_ada", [DEPTH, 128, 48])
    g_mix = din("g_mix", [DEPTH, 128, KD])
    g_ffn = din("g_ffn", [DEPTH, 128, KD])
    w_in = din("w_in", [DEPTH, D, INCOLS])
    conv_w = din("conv_w", [DEPTH, 128, 32, 5])
    conv_b = din("conv_b", [DEPTH, 128, 32])
    dt_bias = din("dt_bias", [DEPTH, 128, 64])
    a_log = din("a_log", [DEPTH, 128, 64])
    dskA = din("dskA", [DEPTH, 128, 16])
    dskB = din("dskB", [DEPTH, 128, 16])
    norm_w = din("norm_w", [DEPTH, 128, 16])
    w_ssd_out = din("w_ssd_out", [DEPTH, DIN, D])
    pool_w = din("pool_w", [DEPTH, 4, 256, 256])
    pool_scale = din("pool_scale", [DEPTH, 128, 8])
    w_pool_out = din("w_pool_out", [DEPTH, D, D])
    w_out = din("w_out", [DEPTH, D, D])
    w_gate_up = din("w_gate_up", [DEPTH, D, 2 * FFH])
    w_down = din("w_down", [DEPTH, FFH, D])
    g_final = din("g_final", [128, KD])
    cmat = din("cmat", [128, 5, 128])
    pm_lat = din("pm_lat", [128, 4, 128])
    pm_ctx = din("pm_ctx", [128, 2, 4, 2, 128])
    invc = din("invc", [128, 4, 384])
    selm = din("selm", [128, 2])

    yT_out = nc.dram_tensor("yT", [D, NLAT], F32, kind="ExternalOutput").ap()

    def dscr(name, shape, dt):
        return nc.dram_tensor(name, list(shape), dt, kind=dbg_kind).ap()

    XT_d = dscr("XT_d", [D, TTH], F32)
    ZS_d = dscr("ZS_d", [DIN, TT], BF16)
    GS_d = dscr("GS_d", [DIN, TT], BF16)
    XBC_d = dscr("XBC_d", [XBCW, XW], BF16)
    PM2_d = dscr("PM2_d", [D, TT], BF16)
    Y_d = dscr("Y_d", [DIN, TT], F32)
    SB_d = dscr("SB_d", [NCH, 128, DIN], F32)
    YN_d = dscr("YN_d", [DIN, TT], BF16)
    G_d = dscr("G_d", [FFH, TT], BF16)
    CCI_d = nc.dram_tensor("CCI_d", [128, DIN], F32).ap()
    CCO_d = nc.dram_tensor("CCO_d", [256, DIN], F32).ap()
    HCI_d = nc.dram_tensor("HCI_d", [128, 16], F32).ap()
    HCO_d = nc.dram_tensor("HCO_d", [256, 16], F32).ap()
    DBG = {}
    if debug:
        DBG["dtt"] = nc.dram_tensor("DBG_dtt", [128, NCH, 64], F32, kind="ExternalOutput").ap()
        DBG["mod"] = nc.dram_tensor("DBG_mod", [DEPTH, 128, 48, 2], F32, kind="ExternalOutput").ap()
        DBG["ht"] = nc.dram_tensor("DBG_ht", [128, KD, TTH], BF16, kind="ExternalOutput").ap()
        DBG["sa"] = nc.dram_tensor("DBG_sa", [128, DIN], F32, kind="ExternalOutput").ap()
        DBG["yt"] = nc.dram_tensor("DBG_yt", [DIN, TT], F32, kind="ExternalOutput").ap()

    pairs = [[2 * i, 2 * i + 1] for i in range(n_cores // 2)]

    with ExitStack() as top:
        _uid = [0]

        def sb(name, shape, dt, es=top):
            _uid[0] += 1
            return es.enter_context(nc.sbuf_tensor(f"{name}_{_uid[0]}", list(shape), dt)).ap()

        PS = [nc.alloc_psum_tensor(f"ps{i}", [128, 512], F32).ap() for i in range(8)]

        CM = sb("CM", [128, 5, 128], F32)
        CMB = sb("CMB", [128, 5, 128], BF16)
        ONES = sb("ONES", [128, 128], F32)
        ONESB = sb("ONESB", [128, 128], BF16)
        MOD = [sb(f"MOD{l}", [128, 48, 2], F32) for l in range(DEPTH)]
        A1 = [sb(f"A1_{l}", [128, KD, 2], F32) for l in range(DEPTH)]
        A2 = [sb(f"A2_{l}", [128, KD, 2], F32) for l in range(DEPTH)]
        SELM = sb("SELM", [128, 2], F32)
        DTT = sb("DTT", [128, NCH, 64], F32)
        DECB = sb("DECB", [128, NCH, 32], F32)
        SA = sb("SA", [128, DIN], F32)
        SAb = sb("SAb", [128, DIN], BF16)
        ident_f = CM[:, 0, :]
        UA_f = CM[:, 1, :]
        UB_f = CM[:, 2, :]
        TLE_f = CM[:, 3, :]
        TGE_f = CM[:, 4, :]
        ident_b = CMB[:, 0, :]

        def B_(n=''):
            return Buf(n)

        with ExitStack() as es:
            b_cm, b_cmb, b_ones, b_sel = B_(), B_(), B_(), B_()
            P.dma('sp', 'ld0', lambda e: e.dma_start(out=CM, in_=cmat), writes=[b_cm])
            P.dma('pool', 'w0', lambda e: e.dma_start(out=CMB, in_=cmat), writes=[b_cmb])
            P.dma('sp', 'ld1', lambda e: e.dma_start(out=SELM, in_=selm), writes=[b_sel])
            P.op('dve', lambda e: e.memset(ONES, 1.0), writes=[b_ones])
            P.op('dve', lambda e: e.memset(ONESB, 1.0), writes=[b_ones])
            xb = [sb(f"xb{i}", [128, KD, 512], F32, es) for i in range(2)]
            bxb = [B_(), B_()]
            bxt = B_()
            xin_v = xT_in.rearrange("(k p) t -> p k t", p=128)
            xtd_v = XT_d.rearrange("(k p) t -> p k t", p=128)
            tiles0 = [(0, 256)] + [(256 + 512 * i, 512) for i in range(4)] + [(TT, 2)]
            for i, (c0, w) in enumerate(tiles0):
                j = i % 2
                P.dma('sp', f'ld{j}', lambda e, j=j, c0=c0, w=w: e.dma_start(out=xb[j][:, :, :w], in_=xin_v[:, :, c0:c0 + w]),
                      writes=[bxb[j]])
                P.dma('sp', f'st{j}', lambda e, j=j, c0=c0, w=w: e.dma_start(out=xtd_v[:, :, c0:c0 + w], in_=xb[j][:, :, :w]),
                      reads=[bxb[j]], writes=[bxt])
            ccf = sb("ccf", [128, KD, 2], F32, es)
            ccb = sb("ccb", [128, KD, 2], BF16, es)
            b_ccf, b_ccb = B_(), B_()
            P.dma('sp', 'ld2', lambda e: e.dma_start(out=ccf, in_=cc_in), writes=[b_ccf])
            P.op('act', lambda e: e.activation(out=ccb, in_=ccf, func=AF.Silu), reads=[b_ccf], writes=[b_ccb])
            wa = [sb(f"wa{i}", [128, KD, 512], BF16, es) for i in range(2)]
            bwa = [B_(), B_()]
            bada = sb("bada", [128, 48], F32, es)
            gm = sb("gm", [128, KD], F32, es)
            gf = sb("gf", [128, KD], F32, es)
            b_bada, b_gm, b_gf, b_ps = B_(), B_(), B_(), B_()
            for l in range(DEPTH):
                b_mod, b_a = B_(), B_()
                P.dma('sp', 'ld3', lambda e, l=l: e.dma_start(out=bada, in_=b_ada[l]), writes=[b_bada])
                P.dma('sp', 'ld4', lambda e, l=l: e.dma_start(out=gm, in_=g_mix[l]), writes=[b_gm])
                P.dma('sp', 'ld5', lambda e, l=l: e.dma_start(out=gf, in_=g_ffn[l]), writes=[b_gf])
                wv = w_ada[l].rearrange("(k p) n -> p k n", p=128)
                for jb in range(12):
                    j = jb % 2
                    P.dma('pool', f'w{j}', lambda e, j=j, jb=jb, wv=wv: e.dma_start(out=wa[j], in_=wv[:, :, jb * 512:(jb + 1) * 512]),
                          writes=[bwa[j]])
                    for q in range(4):
                        cb = jb * 4 + q

                        def mm(e, j=j, q=q, cb=cb):
                            ins = None
                            for k in range(KD):
                                ins = e.matmul(PS[0][:, cb * 2:cb * 2 + 2], lhsT=wa[j][:, k, q * 128:(q + 1) * 128],
                                               rhs=ccb[:, k, :], start=(k == 0), stop=(k == KD - 1))
                            return ins
                        P.op('pe', mm, reads=[bwa[j], b_ccb], writes=[b_ps])
                P.op('dve', lambda e, l=l: e.tensor_tensor(out=MOD[l], in0=PS[0][:, 0:96].rearrange("p (c t) -> p c t", t=2),
                                                           in1=bc(bada, [[1, 48], [0, 2]]), op=ALU.add),
                     reads=[b_ps, b_bada], writes=[b_mod])
                P.op('dve', lambda e, l=l: e.scalar_tensor_tensor(out=A1[l], in0=MOD[l][:, 8:16, :], scalar=1.0,
                                                                  in1=bc(gm, [[1, KD], [0, 2]]), op0=ALU.add, op1=ALU.mult),
                     reads=[b_mod, b_gm], writes=[b_a])
                P.op('dve', lambda e, l=l: e.scalar_tensor_tensor(out=A2[l], in0=MOD[l][:, 32:40, :], scalar=1.0,
                                                                  in1=bc(gf, [[1, KD], [0, 2]]), op0=ALU.add, op1=ALU.mult),
                     reads=[b_mod, b_gf], writes=[b_a])
                if debug:
                    P.dma('sp', 'st2', lambda e, l=l: e.dma_start(out=DBG["mod"][l], in_=MOD[l]), reads=[b_mod])
            P.flush()
        if stop_after == 0:
            return nc

        def norm_mod(l, which, HT, es):
            A = A1[l] if which == 1 else A2[l]
            boff = 0 if which == 1 else 24
            xt = [sb(f"nx{i}", [128, KD, 512], F32, es) for i in range(2)]
            sq = [sb(f"nq{i}", [128, KD, 512], BF16, es) for i in range(2)]
            xn = [sb(f"nn{i}", [128, KD, 512], F32, es) for i in range(2)]
            rs = [sb(f"nr{i}", [128, 512], F32, es) for i in range(2)]
            ONESb = sb("nob", [128, 128], BF16, es)
            bob = B_()
            P.op('dve', lambda e: e.memset(ONESb, 1.0), writes=[bob])
            bx, bq, bn, br, bp = [B_(), B_()], [B_(), B_()], [B_(), B_()], [B_(), B_()], [B_(), B_()]
            bht = B_()
            tiles = [(0, 256, 1)] + [(256 + 512 * i, 512, 0) for i in range(4)]
            if which == 1:
                tiles.append((TT, 2, 0))

            def front(i):
                c0, w, mc = tiles[i]
                j = i % 2
                ps = PS[j]
                P.dma('sp', f'ld{j}', lambda e, j=j, c0=c0, w=w: e.dma_start(out=xt[j][:, :, :w], in_=xtd_v[:, :, c0:c0 + w]),
                      writes=[bx[j]])
                P.op('act', lambda e, j=j, w=w: e.activation(out=sq[j][:, :, :w], in_=xt[j][:, :, :w], func=AF.Square),
                     reads=[bx[j]], writes=[bq[j]])

                def mm(e, j=j, w=w, ps=ps):
                    ins = None
                    for k in range(KD):
                        ins = e.matmul(ps[:, :w], lhsT=ONESb, rhs=sq[j][:, k, :w], start=(k == 0), stop=(k == KD - 1))
                    return ins
                P.op('pe', mm, reads=[bq[j], bob], writes=[bp[j]])

            def back(i):
                c0, w, mc = tiles[i]
                j = i % 2
                ps = PS[j]
                P.op('act', lambda e, j=j, w=w, ps=ps: e.activation(out=rs[j][:, :w], in_=ps[:, :w], func=AF.Ln, scale=1.0 / D, bias=EPS),
                     reads=[bp[j]], writes=[br[j]])
                P.op('act', lambda e, j=j, w=w: e.activation(out=rs[j][:, :w], in_=rs[j][:, :w], func=AF.Exp, scale=-0.5),
                     reads=[br[j]], writes=[br[j]])
                P.op('dve', lambda e, j=j, w=w: e.tensor_tensor(out=xn[j][:, :, :w], in0=xt[j][:, :, :w],
                                                                in1=bc(rs[j], [[0, KD], [1, w]]), op=ALU.mult),
                     reads=[bx[j], br[j]], writes=[bn[j]])
                for k in range(KD):
                    if k % 3 == 2:
                        P.op('dve', lambda e, j=j, w=w, k=k, c0=c0, mc=mc: e.tensor_scalar(
                            out=HT[:, k, c0:c0 + w], in0=xn[j][:, k, :w], scalar1=A[:, k, mc:mc + 1],
                            scalar2=MOD[l][:, boff + k, mc:mc + 1], op0=ALU.mult, op1=ALU.add),
                            reads=[bn[j]], writes=[bht])
                    else:
                        P.op('act', lambda e, j=j, w=w, k=k, c0=c0, mc=mc: e.activation(
                            out=HT[:, k, c0:c0 + w], in_=xn[j][:, k, :w], func=AF.Identity,
                            scale=A[:, k, mc:mc + 1], bias=MOD[l][:, boff + k, mc:mc + 1]),
                            reads=[bn[j]], writes=[bht])

            front(0)
            for i in range(len(tiles)):
                if i + 1 < len(tiles):
                    front(i + 1)
                back(i)

        def inproj(l, HT, es):
            W = [sb(f"ipW{i}", [128, KD, 512], BF16, es) for i in range(2)]
            bW = [B_(), B_()]
            U = [sb(f"ipU{i}", [128, XW + 4], F32, es) for i in range(2)]
            bU = [B_(), B_()]
            ACC = [sb(f"ipA{i}", [128, XW], F32, es) for i in range(2)]
            bACC = [B_(), B_()]
            STG = [sb(f"ipS{i}", [128, XW], BF16, es) for i in range(2)]
            bSTG = [B_(), B_()]
            cw = sb("ipcw", [128, 32, 5], F32, es)
            cb = sb("ipcb", [128, 32], F32, es)
            b_cw, b_cb = B_(), B_()
            bPS = [B_() for _ in range(8)]
            wv = w_in[l].rearrange("(k p) n -> p k n", p=128)
            P.dma('sp', 'ld0', lambda e: e.dma_start(out=cw, in_=conv_w[l]), writes=[b_cw])
            P.dma('sp', 'ld1', lambda e: e.dma_start(out=cb, in_=conv_b[l]), writes=[b_cb])
            for i in range(2):
                P.op('dve', lambda e, i=i: e.memset(U[i], 0.0), writes=[bU[i]])
                P.op('dve', lambda e, i=i: e.memset(ACC[i], 0.0), writes=[bACC[i]])
            bHT = B_()
            nw = [0]

            def load_w(c0, ncol):
                j = nw[0] % 2
                nw[0] += 1
                P.dma('pool', f'w{j}', lambda e, j=j, c0=c0, ncol=ncol: e.dma_start(out=W[j][:, :, :ncol], in_=wv[:, :, c0:c0 + ncol]),
                      writes=[bW[j]])
                return j

            dtb = sb("ipdtb", [128, 64], F32, es)
            dx = sb("ipdx", [128, NCH * 64], F32, es)
            d1 = sb("ipd1", [128, NCH * 64], F32, es)
            b_dtb, b_dx, b_d1, b_dtt = B_(), B_(), B_(), B_()
            P.dma('sp', 'ld2', lambda e: e.dma_start(out=dtb, in_=dt_bias[l]), writes=[b_dtb])
            j = load_w(CDT, 64)
            for c in range(NCH):
                pb = 5 + c // 8
                po = (c % 8) * 64

                def mm(e, j=j, c=c, pb=pb, po=po):
                    ins = None
                    for k in range(KD):
                        ins = e.matmul(PS[pb][:, po:po + 64], lhsT=HT[:, k, c * 128:(c + 1) * 128], rhs=W[j][:, k, 0:64],
                                       start=(k == 0), stop=(k == KD - 1))
                    return ins
                P.op('pe', mm, reads=[bW[j], bHT], writes=[bPS[pb]])
            for pb, nchk in ((5, 8), (6, 8), (7, 2)):
                c0 = (pb - 5) * 8
                P.op('dve', lambda e, pb=pb, nchk=nchk, c0=c0: e.tensor_tensor(
                    out=dx[:, c0 * 64:(c0 + nchk) * 64].rearrange("p (c h) -> p c h", h=64),
                    in0=PS[pb][:, 0:nchk * 64].rearrange("p (c h) -> p c h", h=64),
                    in1=bc(dtb, [[0, nchk], [1, 64]]), op=ALU.add), reads=[bPS[pb], b_dtb], writes=[b_dx])
            P.op('act', lambda e: e.activation(out=d1, in_=dx, func=AF.Abs), reads=[b_dx], writes=[b_d1])
            P.op('act', lambda e: e.activation(out=d1, in_=d1, func=AF.Exp, scale=-1.0), reads=[b_d1], writes=[b_d1])
            P.op('act', lambda e: e.activation(out=d1, in_=d1, func=AF.Ln, bias=1.0), reads=[b_d1], writes=[b_d1])
            P.op('dve', lambda e: e.tensor_scalar_max(out=dx, in0=dx, scalar1=0.0), reads=[b_dx], writes=[b_dx])
            P.op('dve', lambda e: e.tensor_tensor(out=DTT.rearrange("p c h -> p (c h)"), in0=dx, in1=d1, op=ALU.add),
                 reads=[b_dx, b_d1], writes=[b_dtt])
            if debug and l == 0:
                P.dma('sp', 'st3', lambda e: e.dma_start(out=DBG["dtt"], in_=DTT), reads=[b_dtt])

            PMl = sb("ipPMl", [128, 4, 128], BF16, es)
            PMc = sb("ipPMc", [128, 2, 4, 2, 128], BF16, es)
            INV = sb("ipINV", [128, 4, 384], F32, es)
            PW = sb("ipPW", [128, 4, 2, 256], BF16, es)
            PSC = sb("ipPSC", [128, 8], F32, es)
            UT = [sb(f"ipUT{i}", [128, 2, 512], BF16, es) for i in range(2)]
            bUT = [B_(), B_()]
            PMT = [sb(f"ipPMT{i}", [128, 4, 256], BF16, es) for i in range(2)]
            bPMT = [B_(), B_()]
            PM2S = sb("ipPM2S", [128, 4, TT], BF16, es)
            b_pm2s = B_()
            b_pml, b_pmc, b_inv, b_pw, b_psc = B_(), B_(), B_(), B_(), B_()
            P.dma('pool', 'w2', lambda e: e.dma_start(out=PMl, in_=pm_lat), writes=[b_pml])
            P.dma('pool', 'w3', lambda e: e.dma_start(out=PMc, in_=pm_ctx), writes=[b_pmc])
            P.dma('sp', 'ld3', lambda e: e.dma_start(out=INV, in_=invc), writes=[b_inv])
            P.dma('pool', 'w4', lambda e: e.dma_start(out=PW, in_=pool_w[l].rearrange("g (ib p) o -> p g ib o", p=128)), writes=[b_pw])
            P.dma('sp', 'ld4', lambda e: e.dma_start(out=PSC, in_=pool_scale[l]), writes=[b_psc])
            it = 0
            for jp in range(2):
                j = load_w(CPL + jp * 512, 512)
                units = [(0, 2)] + [(c, 1) for c in range(2, NCH)]
                for (c0, nck) in units:
                    u = it % 2
                    it += 1
                    wtok = nck * 128
                    for cc in range(nck):
                        def mm(e, j=j, c=c0 + cc):
                            ins = None
                            for k in range(KD):
                                ins = e.matmul(PS[4], lhsT=HT[:, k, c * 128:(c + 1) * 128], rhs=W[j][:, k, :],
                                               start=(k == 0), stop=(k == KD - 1))
                            return ins
                        P.op('pe', mm, reads=[bW[j], bHT], writes=[bPS[4]])
                        P.op('act', lambda e, u=u, cc=cc: e.activation(out=UT[u][:, cc, :], in_=PS[4], func=AF.Copy),
                             reads=[bPS[4]], writes=[bUT[u]])
                    def pbk(banks, q, wtok=wtok):
                        if wtok == 128:
                            return banks[0], q * 128
                        return banks[q // 2], (q % 2) * 256
                    for q in range(4):
                        g = 2 * jp + q // 2
                        pbi, po = pbk((5, 7), q)

                        def mmp(e, u=u, q=q, g=g, nck=nck, pbi=pbi, po=po):
                            ins = None
                            if nck == 1:
                                ins = e.matmul(PS[pbi][:, po:po + 128], lhsT=UT[u][:, 0, q * 128:(q + 1) * 128],
                                               rhs=PMl[:, g, :], start=True, stop=True)
                            else:
                                for tb in range(2):
                                    for tb_ in range(2):
                                        ins = e.matmul(PS[pbi][:, po + tb * 128:po + (tb + 1) * 128],
                                                       lhsT=UT[u][:, tb_, q * 128:(q + 1) * 128],
                                                       rhs=PMc[:, tb_, g, tb, :], start=(tb_ == 0), stop=(tb_ == 1))
                            return ins
                        P.op('pe', mmp, reads=[bUT[u], b_pml, b_pmc], writes=[bPS[pbi]])
                    for gg in range(2):
                        g = 2 * jp + gg
                        ioff = 0 if nck == 2 else 256
                        pbi, po = pbk((5, 7), 2 * gg)
                        P.op('dve', lambda e, u=u, gg=gg, g=g, wtok=wtok, ioff=ioff, pbi=pbi, po=po: e.tensor_tensor(
                            out=PMT[u][:, 2 * gg:2 * gg + 2, :wtok],
                            in0=PS[pbi][:, po:po + 2 * wtok].rearrange("p (q t) -> p q t", t=wtok),
                            in1=bc(INV[:, g, ioff:ioff + wtok], [[0, 2], [1, wtok]]), op=ALU.mult),
                            reads=[bPS[pbi], b_inv], writes=[bPMT[u]])
                    for gg in range(2):
                        g = 2 * jp + gg
                        for ob in range(2):
                            q = 2 * gg + ob
                            pbi, po = pbk((6, 3), q)

                            def mmw(e, u=u, g=g, gg=gg, ob=ob, wtok=wtok, pbi=pbi, po=po):
                                ins = None
                                for ib in range(2):
                                    ins = e.matmul(PS[pbi][:, po:po + wtok], lhsT=PW[:, g, ib, ob * 128:(ob + 1) * 128],
                                                   rhs=PMT[u][:, 2 * gg + ib, :wtok], start=(ib == 0), stop=(ib == 1))
                                return ins
                            P.op('pe', mmw, reads=[bPMT[u], b_pw], writes=[bPS[pbi]])
                    for q in range(4):
                        blk = jp * 4 + q
                        pbi, po = pbk((6, 3), q)
                        P.op('act', lambda e, q=q, blk=blk, wtok=wtok, c0=c0, pbi=pbi, po=po: e.activation(
                            out=PM2S[:, q, c0 * 128:c0 * 128 + wtok], in_=PS[pbi][:, po:po + wtok],
                            func=AF.Copy, scale=PSC[:, blk:blk + 1]), reads=[bPS[pbi], b_psc], writes=[b_pm2s])
                P.dma('sp', 'st0', lambda e, jp=jp: e.dma_start(
                    out=PM2_d[jp * 512:(jp + 1) * 512, :].rearrange("(q p) t -> p q t", p=128), in_=PM2S),
                    reads=[b_pm2s])

            tiles = [(0, 256)] + [(256 + 512 * i, 512) for i in range(4)]
            nps = [0]
            nstg = [0]
            nxb = [0]
            for jb in range(16):
                c0w = jb * 512
                fam = 'z' if jb < 4 else ('x' if jb < 12 else 'g')
                j = load_w(c0w, 512)
                for q in range(4):
                    if fam == 'x':
                        blk = (jb - 4) * 4 + q
                        ui = nxb[0] % 2
                        nxb[0] += 1
                        tl = tiles + [(TT, 2)]
                    else:
                        si = nstg[0] % 2
                        nstg[0] += 1
                        tl = tiles
                    for (t0, w) in tl:
                        pb = nps[0] % 4
                        nps[0] += 1

                        def mm(e, j=j, q=q, t0=t0, w=w, pb=pb):
                            ins = None
                            for k in range(KD):
                                ins = e.matmul(PS[pb][:, :w], lhsT=W[j][:, k, q * 128:(q + 1) * 128], rhs=HT[:, k, t0:t0 + w],
                                               start=(k == 0), stop=(k == KD - 1))
                            return ins
                        P.op('pe', mm, reads=[bW[j], bHT], writes=[bPS[pb]])
                        if fam == 'x':
                            uo = 2 if t0 == 0 else (LOFF + 2 + (t0 - NCTX))
                            P.op('act', lambda e, ui=ui, uo=uo, w=w, pb=pb: e.activation(out=U[ui][:, uo:uo + w], in_=PS[pb][:, :w], func=AF.Copy),
                                 reads=[bPS[pb]], writes=[bU[ui]])
                            if w > 2:
                                P.op('act', lambda e, ui=ui, uo=uo, w=w, pb=pb, blk=blk: e.activation(
                                    out=ACC[ui][:, uo - 2:uo - 2 + w], in_=PS[pb][:, :w], func=AF.Copy, scale=cw[:, blk, 2:3]),
                                    reads=[bPS[pb], b_cw], writes=[bACC[ui]])
                        else:
                            fn = AF.Silu if fam == 'z' else AF.Sigmoid
                            P.op('act', lambda e, si=si, t0=t0, w=w, pb=pb, fn=fn: e.activation(out=STG[si][:, t0:t0 + w], in_=PS[pb][:, :w], func=fn),
                                 reads=[bPS[pb]], writes=[bSTG[si]])
                    if fam == 'x':
                        ai = ui
                        for tp in (0, 1, 3, 4):
                            P.op('dve', lambda e, ui=ui, ai=ai, blk=blk, tp=tp: e.scalar_tensor_tensor(
                                out=ACC[ai], in0=U[ui][:, tp:tp + XW], scalar=cw[:, blk, tp:tp + 1], in1=ACC[ai],
                                op0=ALU.mult, op1=ALU.add), reads=[bU[ui], b_cw, bACC[ai]], writes=[bACC[ai]])
                        si = nstg[0] % 2
                        nstg[0] += 1
                        P.op('act', lambda e, si=si, ai=ai, blk=blk: e.activation(out=STG[si], in_=ACC[ai], func=AF.Silu, bias=cb[:, blk:blk + 1]),
                             reads=[bACC[ai], b_cb], writes=[bSTG[si]])
                        P.dma('sp', f'st{si}', lambda e, si=si, blk=blk: e.dma_start(out=XBC_d[blk * 128:(blk + 1) * 128, :], in_=STG[si]),
                              reads=[bSTG[si]])
                    else:
                        dst = ZS_d if fam == 'z' else GS_d
                        blk = (jb * 4 + q) if fam == 'z' else ((jb - 12) * 4 + q)
                        P.dma('sp', f'st{si}', lambda e, si=si, blk=blk, dst=dst: e.dma_start(out=dst[blk * 128:(blk + 1) * 128, :], in_=STG[si][:, 0:TT]),
                              reads=[bSTG[si]])


        xbc_v = XBC_d.rearrange("(blk p) t -> p blk t", p=128)
        yd_v = Y_d.rearrange("(blk p) t -> p blk t", p=128)
        zs_v = ZS_d.rearrange("(blk p) t -> p blk t", p=128)
        ynd_v = YN_d.rearrange("(blk p) t -> p blk t", p=128)

        def chunk_col(c):
            return c * 128 if c < 2 else LOFF + (c - 2) * 128

        def pad_ap(X, h0, nh):
            return bass.AP(X.tensor, X.offset + h0 * 128, [list(X.ap[0]), [256, nh // 2], [192, 2], [1, 64]])

        def sweepA(l, es):
            XB = [sb(f"aXB{i}", [128, 32, 128], BF16, es) for i in range(2)]
            bXB = [B_(), B_()]
            XDA = sb("aXDA", [128, 32, 128], BF16, es)
            XDB = sb("aXDB", [128, 32, 128], BF16, es)
            XWA = sb("aXWA", [128, DIN], BF16, es)
            XWB = sb("aXWB", [128, DIN], BF16, es)
            BT = sb("aBT", [128, 1024], BF16, es)
            AN = sb("aAN", [128, 64], F32, es)
            DTA = sb("aDTA", [128, 64], F32, es)
            EXPO = sb("aEXPO", [128, 128], F32, es)
            FAC = sb("aFAC", [128, 64], F32, es)
            E = [sb(f"aE{i}", [128, 4, 128], F32, es) for i in range(2)]
            MP = [[sb(f"aMP{i}{d}", [128, 4, 128], BF16, es) for d in range(2)] for i in range(2)]
            CBTm = sb("aCBT", [128, 2, 8, 128], F32, es)
            DBH = sb("aDBH", [128, 2, DIN], BF16, es)
            DTH = sb("aDTH", [128, 2, 32], BF16, es)
            bDTH = B_()
            EA = sb("aEA", [128, DIN], F32, es)
            T1 = sb("aT1", [128, DIN], F32, es)
            YP = [sb(f"aYP{i}", [128, DIN], F32, es) for i in range(2)]
            SBL = [sb(f"aSBL{i}", [128, DIN], F32, es) for i in range(2)]
            DSK = sb("aDSK", [128, 16, 128], BF16, es)
            dsa = sb("adsa", [128, 16], F32, es)
            dsb = sb("adsb", [128, 16], F32, es)
            bXD, bXW, bBT, bAN, bDTA, bEXPO, bFAC = B_(), B_(), B_(), B_(), B_(), B_(), B_()
            bE = [B_(), B_()]
            bMP = [[B_(), B_()], [B_(), B_()]]
            bCBT, bDBC, bEA, bT1, bDSK, bds, bSA, bSAb, bDECB = B_(), B_(), B_(), B_(), B_(), B_(), B_(), B_(), B_()
            bYP, bSBL = [B_(), B_()], [B_(), B_()]
            bPS = [B_() for _ in range(8)]
            PSb = [PS[i].bitcast(BF16) for i in range(3)]
            P.dma('sp', 'ld0', lambda e: e.dma_start(out=AN, in_=a_log[l]), writes=[bAN])
            P.op('act', lambda e: e.activation(out=AN, in_=AN, func=AF.Exp), reads=[bAN], writes=[bAN])
            P.op('dve', lambda e: e.tensor_scalar_mul(out=AN, in0=AN, scalar1=-1.0), reads=[bAN], writes=[bAN])
            P.dma('sp', 'ld1', lambda e: e.dma_start(out=dsa, in_=dskA[l]), writes=[bds])
            P.dma('sp', 'ld2', lambda e: e.dma_start(out=dsb, in_=dskB[l]), writes=[bds])
            P.op('dve', lambda e: e.tensor_tensor(out=dsa, in0=dsa, in1=dsb, op=ALU.add), reads=[bds], writes=[bds])
            for pp in range(16):
                P.op('dve', lambda e, pp=pp: e.tensor_scalar(out=DSK[:, pp, :], in0=ident_f, scalar1=dsa[:, pp:pp + 1], scalar2=None, op0=ALU.mult),
                     reads=[bds], writes=[bDSK])
            P.op('dve', lambda e: e.memset(XDA, 0.0), writes=[bXD])
            P.op('dve', lambda e: e.memset(XDB, 0.0), writes=[bXD])
            P.op('dve', lambda e: e.memset(SA, 0.0), writes=[bSA])
            P.op('dve', lambda e: e.memset(SAb, 0.0), writes=[bSAb])

            R2 = [sb(f"aR2x{i}", [128, 4, 128], F32, es) for i in range(3)]
            bR2 = [B_(), B_(), B_()]
            P.dma('sp', 'ld3', lambda e: e.dma_start(out=XB[0], in_=xbc_v[:, :, 0:128]), writes=[bXB[0]])
            for c in range(NCH):
                i = c % 2
                co = chunk_col(c)
                yi = c % 2
                if c + 1 < NCH:
                    con = chunk_col(c + 1)
                    P.dma('sp', f'ld{3 + (1 - i)}', lambda e, i=i, con=con: e.dma_start(out=XB[1 - i], in_=xbc_v[:, :, con:con + 128]), writes=[bXB[1 - i]])
                P.op('dve', lambda e, c=c: e.tensor_tensor(out=DTA, in0=DTT[:, c, :], in1=AN, op=ALU.mult), reads=[bAN], writes=[bDTA])
                P.op('dve', lambda e: e.tensor_copy(out=DTH[:, 0, :], in_=DTA[:, 0:32]), reads=[bDTA], writes=[bDTH])
                P.op('dve', lambda e: e.tensor_tensor(out=DTH[:, 1, :], in0=DTA[:, 0:32], in1=DTH[:, 0, :], op=ALU.subtract), reads=[bDTA, bDTH], writes=[bDTH])
                for hl in range(2):
                    P.op('dve', lambda e, hl=hl: e.tensor_copy(out=DBH[:, hl, :].rearrange("p (h q) -> p h q", q=64), in_=bc(DTH[:, hl, :], [[1, 32], [0, 64]])),
                         reads=[bDTH], writes=[bDBC])

                def emit_R2(n):
                    g, d = n // 2, n % 2
                    Mk = TLE_f if d == 0 else TGE_f
                    rr = n % 3
                    P.op('pool' if n % 4 == 1 else 'dve', lambda e, rr=rr, d=d, g=g, Mk=Mk: e.tensor_tensor(
                        out=R2[rr], in0=bc(DTA[:, d * 32 + 4 * g:d * 32 + 4 * g + 4], [[1, 4], [0, 128]]),
                        in1=bc(Mk, [[0, 4], [1, 128]]), op=ALU.mult), reads=[bDTA], writes=[bR2[rr]])

                def emit_D(n):
                    d = n % 2
                    Uf = UA_f if d == 0 else UB_f
                    rr = n % 3
                    pb = 4 + n % 2
                    P.op('pe', lambda e, rr=rr, Uf=Uf, pb=pb: e.matmul(PS[pb], lhsT=Uf, rhs=R2[rr].rearrange("p h t -> p (h t)"), start=True, stop=True),
                         reads=[bR2[rr]], writes=[bPS[pb]])
                    P.op('act', lambda e, pb=pb, n=n: e.activation(out=E[n % 2].rearrange("p h t -> p (h t)"), in_=PS[pb], func=AF.Exp),
                         reads=[bPS[pb]], writes=[bE[n % 2]])

                def emit_M(n):
                    g, d = n // 2, n % 2
                    r = g % 2
                    P.op('pool' if d == 1 else 'dve', lambda e, n=n, r=r, d=d, g=g: e.tensor_tensor(
                        out=MP[r][d], in0=E[n % 2], in1=bc(CBTm[:, d, g, :], [[0, 4], [1, 128]]), op=ALU.mult),
                        reads=[bE[n % 2], bCBT], writes=[bMP[r][d]])

                def emit_y(g):
                    r = g % 2
                    for pe_ in range(2):
                        pp = 2 * g + pe_
                        bank = pp // 4
                        col = (pp % 4) * 128

                        def mmy(e, pp=pp, bank=bank, col=col, r=r, g=g, i=i):
                            o = PS[bank][:, col:col + 128]
                            e.matmul(o, lhsT=DSK[:, pp, :], rhs=XB[i][:, pp, :], start=True, stop=False)
                            ins = None
                            for ee in range(2):
                                h = 2 * pp + ee
                                hh = h - 4 * g
                                e.matmul(o, lhsT=XDA[:, h, :], rhs=MP[r][0][:, hh, :], start=False, stop=False)
                                ins = e.matmul(o, lhsT=XDB[:, h, :], rhs=MP[r][1][:, hh, :], start=False, stop=(ee == 1))
                            return ins
                        P.op('pe', mmy, reads=[bDSK, bXB[i], bXD, bMP[r][0], bMP[r][1]], writes=[bPS[bank]])

                emit_R2(0)
                emit_R2(1)

                def mm_small(e):
                    e.matmul(PS[3][:, 0:32], lhsT=UA_f, rhs=DTA[:, 0:32], start=True, stop=True)
                    e.matmul(PS[3][:, 32:64], lhsT=UB_f, rhs=DTA[:, 32:64], start=True, stop=True)
                    return e.matmul(PS[3][:, 64:128], lhsT=ONES, rhs=DTA[:, 0:64], start=True, stop=True)
                P.op('pe', mm_small, reads=[bDTA], writes=[bPS[3]])
                P.op('act', lambda e: e.activation(out=EXPO, in_=PS[3][:, 0:128], func=AF.Exp), reads=[bPS[3]], writes=[bEXPO])
                for b2 in range(2):
                    def tr(e, b2=b2, i=i):
                        ins = None
                        for q in range(8):
                            ins = e.transpose(PSb[b2][:, q * 128:(q + 1) * 128], XB[i][:, b2 * 8 + q, :], ident_b)
                        return ins
                    P.op('pe', tr, reads=[bXB[i]], writes=[bPS[b2]])

                def trb(e, i=i):
                    ins = None
                    for q in range(8):
                        ins = e.transpose(PSb[2][:, q * 128:(q + 1) * 128], XB[i][:, 16 + q, :], ident_b)
                    return ins
                P.op('pe', trb, reads=[bXB[i]], writes=[bPS[2]])
                for b2 in range(2):
                    def mmc(e, b2=b2, i=i):
                        ins = None
                        for gq in range(4):
                            g = b2 * 4 + gq
                            ins = e.matmul(PS[4 + b2][:, gq * 128:(gq + 1) * 128], lhsT=XB[i][:, 16 + g, :], rhs=XB[i][:, 24 + g, :],
                                           start=True, stop=True)
                        return ins
                    P.op('pe', mmc, reads=[bXB[i]], writes=[bPS[4 + b2]])
                for rd in range(2):
                    for b2 in range(2):
                        b4 = rd * 2 + b2

                        def mme(e, b4=b4, b2=b2):
                            ins = None
                            for q in range(4):
                                pp = b4 * 4 + q
                                e.matmul(PS[6 + b2][:, q * 128:(q + 1) * 128], lhsT=DBH[:, 0, pp * 128:(pp + 1) * 128], rhs=CMB[:, 3, :],
                                         start=True, stop=False)
                                ins = e.matmul(PS[6 + b2][:, q * 128:(q + 1) * 128], lhsT=DBH[:, 1, pp * 128:(pp + 1) * 128], rhs=CMB[:, 3, :],
                                               start=False, stop=True)
                            return ins
                        P.op('pe', mme, reads=[bDBC], writes=[bPS[6 + b2]])
                        P.op('act', lambda e, b4=b4, b2=b2: e.activation(out=EA[:, b4 * 512:(b4 + 1) * 512], in_=PS[6 + b2], func=AF.Exp),
                             reads=[bPS[6 + b2]], writes=[bEA])
                    if rd == 0:
                        for b2 in range(2):
                            for d, Mk in ((0, TLE_f), (1, TGE_f)):
                                P.op('dve', lambda e, b2=b2, d=d, Mk=Mk: e.tensor_tensor(
                                    out=CBTm[:, d, b2 * 4:(b2 + 1) * 4, :], in0=PS[4 + b2].rearrange("p (g t) -> p g t", t=128),
                                    in1=bc(Mk, [[0, 4], [1, 128]]), op=ALU.mult), reads=[bPS[4 + b2]], writes=[bCBT])
                        emit_D(0)
                        emit_D(1)
                P.op('dve', lambda e, c=c: e.tensor_tensor(out=FAC, in0=DTT[:, c, :], in1=EXPO[:, 0:64], op=ALU.mult), reads=[bEXPO], writes=[bFAC])
                P.op('dve', lambda e, c=c: e.tensor_copy(out=DECB[:, c, :], in_=EXPO[:, 96:128]), reads=[bEXPO], writes=[bDECB])
                for b2 in range(2):
                    src3 = bass.AP(PSb[b2].tensor, PSb[b2].offset, [list(PSb[b2].ap[0]), [128, 8], [64, 2], [1, 64]])
                    for d, X in ((0, XDA), (1, XDB)):
                        fac = bass.AP(DTT.tensor, DTT[:, c, :].offset + d * 32 + 16 * b2, [list(DTT.ap[0]), [2, 8], [1, 2], [0, 64]])
                        P.op('dve', lambda e, X=X, b2=b2, src3=src3, fac=fac: e.tensor_tensor(out=pad_ap(X, 16 * b2, 16), in0=src3, in1=fac, op=ALU.mult),
                             reads=[bPS[b2]], writes=[bXD])
                    for d, X in ((0, XWA), (1, XWB)):
                        P.op('dve', lambda e, X=X, b2=b2, d=d: e.tensor_tensor(
                            out=X[:, b2 * 1024:(b2 + 1) * 1024].rearrange("p (h q) -> p h q", q=64),
                            in0=PSb[b2].rearrange("p (h q) -> p h q", q=64),
                            in1=bc(FAC[:, d * 32 + 16 * b2:d * 32 + 16 * b2 + 16], [[1, 16], [0, 64]]), op=ALU.mult),
                            reads=[bPS[b2], bFAC], writes=[bXW])
                P.op('act', lambda e: e.activation(out=BT, in_=PSb[2], func=AF.Copy), reads=[bPS[2]], writes=[bBT])
                for n in range(2, 18):
                    emit_M(n - 2)
                    if n < 16:
                        emit_R2(n)
                        emit_D(n)
                    if (n - 2) % 2 == 1:
                        emit_y((n - 2) // 2)
                for hb in range(2):
                    for b2 in range(2):
                        def mmi(e, hb=hb, b2=b2, i=i):
                            ins = None
                            for q in range(4):
                                pp = hb * 8 + b2 * 4 + q
                                ins = e.matmul(PS[6 + b2][:, q * 128:(q + 1) * 128], lhsT=SAb[:, pp * 128:(pp + 1) * 128],
                                               rhs=XB[i][:, 24 + pp // 2, :], start=True, stop=True)
                            return ins
                        P.op('pe', mmi, reads=[bSAb, bXB[i]], writes=[bPS[6 + b2]])
                        cs = (hb * 2 + b2) * 512
                        P.op('dve', lambda e, b2=b2, cs=cs: e.tensor_tensor(out=T1[:, cs:cs + 512], in0=PS[6 + b2], in1=EA[:, cs:cs + 512], op=ALU.mult),
                             reads=[bPS[6 + b2], bEA], writes=[bT1])
                        yb = hb * 2 + b2
                        P.op('dve', lambda e, yb=yb, cs=cs, yi=yi: e.tensor_tensor(out=YP[yi][:, cs:cs + 512], in0=PS[yb], in1=T1[:, cs:cs + 512], op=ALU.add),
                             reads=[bPS[yb], bT1], writes=[bYP[yi]])
                P.dma('sp', f'st{yi}', lambda e, yi=yi, c=c: e.dma_start(out=yd_v[:, :, c * 128:(c + 1) * 128], in_=YP[yi].rearrange("p (b t) -> p b t", t=128)),
                      reads=[bYP[yi]])
                for b4 in range(4):
                    def mms(e, b4=b4):
                        ins = None
                        for q in range(2):
                            g = b4 * 2 + q
                            ins = e.matmul(PS[b4][:, q * 256:(q + 1) * 256], lhsT=BT[:, g * 128:(g + 1) * 128], rhs=XWA[:, g * 256:(g + 1) * 256],
                                           start=True, stop=True)
                        return ins
                    P.op('pe', mms, reads=[bBT, bXW], writes=[bPS[b4]])
                P.op('dve', lambda e: e.tensor_tensor(out=SA.rearrange("p (h q) -> p h q", q=64), in0=SA.rearrange("p (h q) -> p h q", q=64),
                                                      in1=bc(EXPO[:, 64:96], [[1, 32], [0, 64]]), op=ALU.mult), reads=[bEXPO, bSA], writes=[bSA])
                for b4 in range(4):
                    def mmsb(e, b4=b4):
                        ins = None
                        for q in range(2):
                            g = b4 * 2 + q
                            ins = e.matmul(PS[4 + b4][:, q * 256:(q + 1) * 256], lhsT=BT[:, g * 128:(g + 1) * 128], rhs=XWB[:, g * 256:(g + 1) * 256],
                                           start=True, stop=True)
                        return ins
                    P.op('pe', mmsb, reads=[bBT, bXW], writes=[bPS[4 + b4]])
                for b4 in range(4):
                    P.op('dve', lambda e, b4=b4: e.tensor_tensor(out=SA[:, b4 * 512:(b4 + 1) * 512], in0=SA[:, b4 * 512:(b4 + 1) * 512], in1=PS[b4], op=ALU.add),
                         reads=[bPS[b4], bSA], writes=[bSA])
                    P.op('act', lambda e, b4=b4, yi=yi: e.activation(out=SBL[yi][:, b4 * 512:(b4 + 1) * 512], in_=PS[4 + b4], func=AF.Copy),
                         reads=[bPS[4 + b4]], writes=[bSBL[yi]])
                P.op('act', lambda e: e.activation(out=SAb, in_=SA, func=AF.Copy), reads=[bSA], writes=[bSAb])
                P.dma('sp', f'st{2 + yi}', lambda e, yi=yi, c=c: e.dma_start(out=SB_d[c], in_=SBL[yi]), reads=[bSBL[yi]])
                if debug and l == 0 and c == 1:
                    P.dma('sp', 'st4', lambda e: e.dma_start(out=DBG["sa"], in_=SA), reads=[bSA])


        def sweepB(l, es, last):
            CT = [sb(f"bCT{i}", [128, 8, 128], BF16, es) for i in range(2)]
            YPl = [sb(f"bYP{i}", [128, DIN], F32, es) for i in range(2)]
            SBLl = [sb(f"bSBL{i}", [128, DIN], F32, es) for i in range(2)]
            ZSl = [sb(f"bZS{i}", [128, 16, 128], BF16, es) for i in range(2)]
            YNs = [sb(f"bYN{i}", [128, 16, 128], BF16, es) for i in range(2)]
            SBr = sb("bSBr", [128, DIN], F32, es)
            SBb = sb("bSBb", [128, DIN], BF16, es)
            CC2 = sb("bCC2", [128, 2, DIN], F32, es)
            AN = sb("bAN", [128, 32], F32, es)
            DTA = sb("bDTA", [128, 32], F32, es)
            DBH = sb("bDBH", [128, 2, DIN], BF16, es)
            DTH = sb("bDTH", [128, 2, 32], BF16, es)
            bDTH = B_()
            EA = sb("bEA", [128, DIN], F32, es)
            T1 = sb("bT1", [128, DIN], F32, es)
            YZ2 = [sb(f"bYZ{i}", [128, DIN], F32, es) for i in range(2)]
            SQ = sb("bSQ", [128, DIN], BF16, es)
            RST2 = [sb(f"bRST{i}", [128, 8, 128], F32, es) for i in range(2)]
            bYZ2, bRST2 = [B_(), B_()], [B_(), B_()]
            NW = sb("bNW", [128, 16], F32, es)
            bCT, bYPl, bSBLl, bZSl, bYNs = [B_(), B_()], [B_(), B_()], [B_(), B_()], [B_(), B_()], [B_(), B_()]
            bSBr, bSBb, bCC2, bAN, bDTA, bDBC, bEA, bT1, bYZ_, bSQ, bRST_, bNW = [B_() for _ in range(12)]
            bCCI, bCCO = B_(), B_()
            bPS = [B_() for _ in range(8)]
            P.dma('sp', 'st0', lambda e: e.dma_start(out=CCI_d, in_=SA), writes=[bCCI])
            P.dma('pool', 'cc', lambda e: e.collective_compute("AllGather", ALU.bypass, replica_groups=pairs,
                                                               ins=[CCI_d.opt()], outs=[CCO_d.opt()]),
                  reads=[bCCI], writes=[bCCO], inc=1)
            P.dma('sp', 'ld0', lambda e: e.dma_start(out=AN, in_=a_log[l][:, 32:64]), writes=[bAN])
            P.op('act', lambda e: e.activation(out=AN, in_=AN, func=AF.Exp), reads=[bAN], writes=[bAN])
            P.op('dve', lambda e: e.tensor_scalar_mul(out=AN, in0=AN, scalar1=-1.0), reads=[bAN], writes=[bAN])
            P.dma('sp', 'ld1', lambda e: e.dma_start(out=NW, in_=norm_w[l]), writes=[bNW])
            P.op('dve', lambda e: e.memset(SBr, 0.0), writes=[bSBr])
            P.op('dve', lambda e: e.memset(SBb, 0.0), writes=[bSBb])
            order = ([] if last else [1, 0]) + list(range(NCH - 1, 1, -1))
            EA2 = [EA, sb("bEA2", [128, DIN], F32, es)]
            bEA2 = [B_(), B_()]

            def emit_loads(it, c):
                i = it % 2
                co = chunk_col(c)
                P.dma('sp', f'ld{3 + i}', lambda e, i=i, co=co: e.dma_start(out=CT[i], in_=xbc_v[:, 24:32, co:co + 128]), writes=[bCT[i]])
                P.dma('sp', f'ld{5 + i}', lambda e, i=i, c=c: e.dma_start(out=YPl[i].rearrange("p (b t) -> p b t", t=128), in_=yd_v[:, :, c * 128:(c + 1) * 128]),
                      writes=[bYPl[i]])
                P.dma('sp', f'ld{7 + i}', lambda e, i=i, c=c: e.dma_start(out=SBLl[i], in_=SB_d[c]), writes=[bSBLl[i]])
                P.dma('sp', f'ld{9 + i}', lambda e, i=i, c=c: e.dma_start(out=ZSl[i], in_=zs_v[:, :, c * 128:(c + 1) * 128]), writes=[bZSl[i]])

            def emit_EA(it, c):
                sl = it % 2
                P.op('dve', lambda e, c=c: e.tensor_tensor(out=DTA, in0=DTT[:, c, 32:64], in1=AN, op=ALU.mult), reads=[bAN], writes=[bDTA])
                P.op('dve', lambda e: e.tensor_copy(out=DTH[:, 0, :], in_=DTA), reads=[bDTA], writes=[bDTH])
                P.op('dve', lambda e: e.tensor_tensor(out=DTH[:, 1, :], in0=DTA, in1=DTH[:, 0, :], op=ALU.subtract), reads=[bDTA, bDTH], writes=[bDTH])
                for hl in range(2):
                    P.op('dve', lambda e, hl=hl: e.tensor_copy(out=DBH[:, hl, :].rearrange("p (h q) -> p h q", q=64), in_=bc(DTH[:, hl, :], [[1, 32], [0, 64]])),
                         reads=[bDTH], writes=[bDBC])
                for b4 in range(4):
                    def mme(e, b4=b4):
                        ins = None
                        for q in range(4):
                            pp = b4 * 4 + q
                            e.matmul(PS[b4][:, q * 128:(q + 1) * 128], lhsT=DBH[:, 0, pp * 128:(pp + 1) * 128], rhs=CMB[:, 4, :],
                                     start=True, stop=False)
                            ins = e.matmul(PS[b4][:, q * 128:(q + 1) * 128], lhsT=DBH[:, 1, pp * 128:(pp + 1) * 128], rhs=CMB[:, 4, :],
                                           start=False, stop=True)
                        return ins
                    P.op('pe', mme, reads=[bDBC], writes=[bPS[b4]])
                    P.op('act', lambda e, b4=b4, sl=sl: e.activation(out=EA2[sl][:, b4 * 512:(b4 + 1) * 512], in_=PS[b4], func=AF.Exp),
                         reads=[bPS[b4]], writes=[bEA2[sl]])

            emit_loads(0, order[0])
            emit_EA(0, order[0])
            for it, c in enumerate(order):
                i = it % 2
                sl = it % 2
                if c == NCH - 1:
                    P.dma('sp', 'ld2', lambda e: e.dma_start(out=CC2, in_=CCO_d.rearrange("(r p) n -> p r n", p=128)),
                          reads=[bCCO], writes=[bCC2])
                    P.op('dve', lambda e: e.tensor_scalar(out=SBr, in0=CC2[:, 0, :], scalar1=SELM[:, 0:1], scalar2=None, op0=ALU.mult),
                         reads=[bCC2], writes=[bSBr])
                    P.op('dve', lambda e: e.scalar_tensor_tensor(out=SBr, in0=CC2[:, 1, :], scalar=SELM[:, 1:2], in1=SBr,
                                                                 op0=ALU.mult, op1=ALU.add), reads=[bCC2, bSBr], writes=[bSBr])
                    P.op('act', lambda e: e.activation(out=SBb, in_=SBr, func=AF.Copy), reads=[bSBr], writes=[bSBb])
                if it + 1 < len(order):
                    emit_loads(it + 1, order[it + 1])
                for b4 in range(4):
                    def mmi(e, b4=b4, i=i):
                        ins = None
                        for q in range(4):
                            pp = b4 * 4 + q
                            ins = e.matmul(PS[4 + b4][:, q * 128:(q + 1) * 128], lhsT=SBb[:, pp * 128:(pp + 1) * 128],
                                           rhs=CT[i][:, pp // 2, :], start=True, stop=True)
                        return ins
                    P.op('pe', mmi, reads=[bSBb, bCT[i]], writes=[bPS[4 + b4]])
                    P.op('dve', lambda e, b4=b4, sl=sl: e.tensor_tensor(out=T1[:, b4 * 512:(b4 + 1) * 512], in0=PS[4 + b4], in1=EA2[sl][:, b4 * 512:(b4 + 1) * 512], op=ALU.mult),
                         reads=[bPS[4 + b4], bEA2[sl]], writes=[bT1])
                P.op('dve', lambda e, i=i: e.tensor_tensor(out=T1, in0=T1, in1=YPl[i], op=ALU.add), reads=[bT1, bYPl[i]], writes=[bT1])
                if debug and l == 0:
                    P.dma('sp', 'st5', lambda e, c=c: e.dma_start(out=DBG["yt"].rearrange("(blk p) t -> p blk t", p=128)[:, :, c * 128:(c + 1) * 128],
                                                                 in_=T1.rearrange("p (b t) -> p b t", t=128)), reads=[bT1])
                P.op('dve', lambda e, c=c: e.tensor_tensor(out=SBr.rearrange("p (h q) -> p h q", q=64), in0=SBr.rearrange("p (h q) -> p h q", q=64),
                                                           in1=bc(DECB[:, c, :], [[1, 32], [0, 64]]), op=ALU.mult), reads=[bSBr], writes=[bSBr])
                P.op('dve', lambda e, i=i: e.tensor_tensor(out=SBr, in0=SBr, in1=SBLl[i], op=ALU.add), reads=[bSBr, bSBLl[i]], writes=[bSBr])
                P.op('act', lambda e: e.activation(out=SBb, in_=SBr, func=AF.Copy), reads=[bSBr], writes=[bSBb])
                if it + 1 < len(order) and order[it + 1] != NCH - 1:
                    emit_EA(it + 1, order[it + 1])
                YZ, RST, bYZ, bRST = YZ2[i], RST2[i], bYZ2[i], bRST2[i]
                P.op('dve', lambda e, i=i, YZ=YZ: e.tensor_tensor(out=YZ, in0=T1, in1=ZSl[i].rearrange("p b t -> p (b t)"), op=ALU.mult),
                     reads=[bT1, bZSl[i]], writes=[bYZ])
                P.op('act', lambda e, YZ=YZ: e.activation(out=bass.AP(SQ.tensor, SQ.offset, [list(SQ.ap[0]), [128, 8], [1024, 2], [1, 128]]),
                                                          in_=YZ.rearrange("p (g r t) -> p g r t", r=2, t=128), func=AF.Square), reads=[bYZ], writes=[bSQ])
                for b2 in range(2):
                    def mmn(e, b2=b2):
                        e.matmul(PS[4 + b2], lhsT=ONESB, rhs=SQ[:, b2 * 512:(b2 + 1) * 512], start=True, stop=False)
                        return e.matmul(PS[4 + b2], lhsT=ONESB, rhs=SQ[:, 1024 + b2 * 512:1024 + (b2 + 1) * 512], start=False, stop=True)
                    P.op('pe', mmn, reads=[bSQ], writes=[bPS[4 + b2]])
                    P.op('act', lambda e, b2=b2, RST=RST: e.activation(out=RST[:, b2 * 4:(b2 + 1) * 4, :].rearrange("p g t -> p (g t)"), in_=PS[4 + b2],
                                                                       func=AF.Ln, scale=1.0 / 256, bias=EPS), reads=[bPS[4 + b2]], writes=[bRST])
                P.op('act', lambda e, RST=RST: e.activation(out=RST, in_=RST, func=AF.Exp, scale=-0.5), reads=[bRST], writes=[bRST])
                P.op('pool', lambda e, i=i, YZ=YZ, RST=RST: e.tensor_tensor(out=YNs[i].rearrange("p (g r) t -> p g r t", r=2),
                                                                            in0=YZ.rearrange("p (g r t) -> p g r t", r=2, t=128),
                                                                            in1=bc(RST, [[128, 8], [0, 2], [1, 128]]), op=ALU.mult),
                     reads=[bYZ, bRST], writes=[bYNs[i]])
                if it + 1 < len(order) and order[it + 1] == NCH - 1:
                    emit_EA(it + 1, order[it + 1])
                P.dma('sp', f'st{1 + i}', lambda e, i=i, c=c: e.dma_start(out=ynd_v[:, :, c * 128:(c + 1) * 128], in_=YNs[i]), reads=[bYNs[i]])


        gs_v = GS_d.rearrange("(blk p) t -> p blk t", p=128)
        pm2_v = PM2_d.rearrange("(blk p) t -> p blk t", p=128)
        gd_v = G_d.rearrange("(blk p) t -> p blk t", p=128)

        def load_weight(dst, src_v, nk, ncols, keybase, bdst):
            n = 0
            for k0 in range(0, nk, 8):
                k1 = min(nk, k0 + 8)
                for c0 in range(0, ncols, 512):
                    c1 = min(ncols, c0 + 512)
                    P.dma('pool', f'w{keybase + n % 4}', lambda e, k0=k0, k1=k1, c0=c0, c1=c1: e.dma_start(
                        out=dst[:, k0:k1, c0:c1], in_=src_v[:, k0:k1, c0:c1]), writes=[bdst])
                    n += 1

        def mixer_dense(l, es, last):
            WSO = sb("dWSO", [128, 16, D], BF16, es)
            WPO = sb("dWPO", [128, 8, D], BF16, es)
            WO = sb("dWO", [128, 8, D], BF16, es)
            YNt = [sb(f"dYN{i}", [128, 16, 512], BF16, es) for i in range(2)]
            PMt = [sb(f"dPM{i}", [128, 8, 512], BF16, es) for i in range(2)]
            GSt = sb("dGS", [128, 16, 512], BF16, es)
            Xt = sb("dX", [128, KD, 512], F32, es)
            MT = sb("dMT", [128, KD, 512], BF16, es)
            Ta = [sb(f"dTa{i}", [128, 512], F32, es) for i in range(2)]
            Tb = [sb(f"dTb{i}", [128, 512], F32, es) for i in range(2)]
            bWSO, bWPO, bWO, bGS, bX, bMT = B_(), B_(), B_(), B_(), B_(), B_()
            bYN, bPM, bTa, bTb = [B_(), B_()], [B_(), B_()], [B_(), B_()], [B_(), B_()]
            bPS = [B_() for _ in range(8)]
            load_weight(WSO, w_ssd_out[l].rearrange("(k p) n -> p k n", p=128), 16, D, 0, bWSO)
            NWm = sb("dNW", [128, 16], F32, es)
            bNWm = B_()
            P.dma('sp', 'ld6', lambda e: e.dma_start(out=NWm, in_=norm_w[l]), writes=[bNWm])
            for kb in range(16):
                P.op('dve', lambda e, kb=kb: e.tensor_scalar(out=WSO[:, kb, :], in0=WSO[:, kb, :], scalar1=NWm[:, kb:kb + 1], scalar2=None, op0=ALU.mult),
                     reads=[bWSO, bNWm], writes=[bWSO])
            load_weight(WPO, w_pool_out[l].rearrange("(k p) n -> p k n", p=128), 8, D, 0, bWPO)
            load_weight(WO, w_out[l].rearrange("(k p) n -> p k n", p=128), 8, D, 0, bWO)
            tiles = ([] if last else [(0, 256, 1)]) + [(256 + 512 * i, 512, 0) for i in range(4)]
            np_ = 0
            for it, (t0, w, mc) in enumerate(tiles):
                i = it % 2
                P.dma('sp', f'ld{i}', lambda e, i=i, t0=t0, w=w: e.dma_start(out=YNt[i][:, :, :w], in_=ynd_v[:, :, t0:t0 + w]), writes=[bYN[i]])
                P.dma('sp', f'ld{2 + i}', lambda e, i=i, t0=t0, w=w: e.dma_start(out=PMt[i][:, :, :w], in_=pm2_v[:, :, t0:t0 + w]), writes=[bPM[i]])
                P.dma('sp', 'ld4', lambda e, t0=t0, w=w: e.dma_start(out=GSt[:, :, :w], in_=gs_v[:, :, t0:t0 + w]), writes=[bGS])
                P.dma('sp', 'ld5', lambda e, t0=t0, w=w: e.dma_start(out=Xt[:, :, :w], in_=xtd_v[:, :, t0:t0 + w]), writes=[bX])
                for db in range(KD):
                    pa = np_ % 4
                    pb = 4 + np_ % 4
                    r = np_ % 2
                    np_ += 1

                    def mma(e, db=db, i=i, w=w, pa=pa):
                        ins = None
                        for kb in range(16):
                            ins = e.matmul(PS[pa][:, :w], lhsT=WSO[:, kb, db * 128:(db + 1) * 128], rhs=YNt[i][:, kb, :w],
                                           start=(kb == 0), stop=(kb == 15))
                        return ins
                    P.op('pe', mma, reads=[bWSO, bYN[i]], writes=[bPS[pa]])

                    def mmb(e, db=db, i=i, w=w, pb=pb):
                        ins = None
                        for kb in range(8):
                            ins = e.matmul(PS[pb][:, :w], lhsT=WPO[:, kb, db * 128:(db + 1) * 128], rhs=PMt[i][:, kb, :w],
                                           start=(kb == 0), stop=(kb == 7))
                        return ins
                    P.op('pe', mmb, reads=[bWPO, bPM[i]], writes=[bPS[pb]])
                    P.op('dve', lambda e, r=r, db=db, w=w, pa=pa: e.tensor_tensor(out=Ta[r][:, :w], in0=PS[pa][:, :w], in1=GSt[:, db, :w], op=ALU.mult),
                         reads=[bPS[pa], bGS], writes=[bTa[r]])
                    P.op('dve', lambda e, r=r, db=db, w=w, pb=pb: e.tensor_tensor(out=Tb[r][:, :w], in0=PS[pb][:, :w], in1=GSt[:, 8 + db, :w], op=ALU.mult),
                         reads=[bPS[pb], bGS], writes=[bTb[r]])
                    P.op('pool', lambda e, r=r, db=db, w=w: e.tensor_tensor(out=MT[:, db, :w], in0=Ta[r][:, :w], in1=Tb[r][:, :w], op=ALU.add),
                         reads=[bTa[r], bTb[r]], writes=[bMT])
                for db in range(KD):
                    pa = np_ % 4
                    np_ += 1

                    def mmo(e, db=db, w=w, pa=pa):
                        ins = None
                        for kb in range(8):
                            ins = e.matmul(PS[pa][:, :w], lhsT=WO[:, kb, db * 128:(db + 1) * 128], rhs=MT[:, kb, :w],
                                           start=(kb == 0), stop=(kb == 7))
                        return ins
                    P.op('pe', mmo, reads=[bWO, bMT], writes=[bPS[pa]])
                    P.op('dve', lambda e, db=db, w=w, pa=pa, mc=mc: e.scalar_tensor_tensor(
                        out=Xt[:, db, :w], in0=PS[pa][:, :w], scalar=MOD[l][:, 16 + db, mc:mc + 1], in1=Xt[:, db, :w],
                        op0=ALU.mult, op1=ALU.add), reads=[bPS[pa], bX], writes=[bX])
                P.dma('sp', 'st0', lambda e, t0=t0, w=w: e.dma_start(out=xtd_v[:, :, t0:t0 + w], in_=Xt[:, :, :w]), reads=[bX])

        def ffn_up(l, HT, es, last):
            Wa = [sb(f"fWa{i}", [128, KD, 512], BF16, es) for i in range(2)]
            Wb = [sb(f"fWb{i}", [128, KD, 512], BF16, es) for i in range(2)]
            SI = [sb(f"fSI{i}", [128, 512], F32, es) for i in range(2)]
            GST = [sb(f"fG{i}", [128, TT], BF16, es) for i in range(2)]
            bWa, bWb, bSI, bGST = [B_(), B_()], [B_(), B_()], [B_(), B_()], [B_(), B_()]
            bPS = [B_() for _ in range(8)]
            bHT = B_()
            wv = w_gate_up[l].rearrange("(k p) n -> p k n", p=128)
            tiles = ([] if last else [(0, 256)]) + [(256 + 512 * i, 512) for i in range(4)]
            np_ = 0
            ng = 0
            for jb in range(6):
                j = jb % 2
                ncol = 512 if jb < 5 else 256
                P.dma('pool', f'w{j}', lambda e, j=j, jb=jb, ncol=ncol: e.dma_start(out=Wa[j][:, :, :ncol], in_=wv[:, :, jb * 512:jb * 512 + ncol]),
                      writes=[bWa[j]])
                P.dma('pool', f'w{2 + j}', lambda e, j=j, jb=jb, ncol=ncol: e.dma_start(out=Wb[j][:, :, :ncol], in_=wv[:, :, FFH + jb * 512:FFH + jb * 512 + ncol]),
                      writes=[bWb[j]])
                for q in range(ncol // 128):
                    hb = jb * 4 + q
                    gi = ng % 2
                    ng += 1
                    for (t0, w) in tiles:
                        pa = np_ % 4
                        pb = 4 + np_ % 4
                        r = np_ % 2
                        np_ += 1

                        def mma(e, j=j, q=q, t0=t0, w=w, pa=pa):
                            ins = None
                            for k in range(KD):
                                ins = e.matmul(PS[pa][:, :w], lhsT=Wa[j][:, k, q * 128:(q + 1) * 128], rhs=HT[:, k, t0:t0 + w],
                                               start=(k == 0), stop=(k == KD - 1))
                            return ins
                        P.op('pe', mma, reads=[bWa[j], bHT], writes=[bPS[pa]])

                        def mmb(e, j=j, q=q, t0=t0, w=w, pb=pb):
                            ins = None
                            for k in range(KD):
                                ins = e.matmul(PS[pb][:, :w], lhsT=Wb[j][:, k, q * 128:(q + 1) * 128], rhs=HT[:, k, t0:t0 + w],
                                               start=(k == 0), stop=(k == KD - 1))
                            return ins
                        P.op('pe', mmb, reads=[bWb[j], bHT], writes=[bPS[pb]])
                        P.op('act', lambda e, r=r, w=w, pa=pa: e.activation(out=SI[r][:, :w], in_=PS[pa][:, :w], func=AF.Silu),
                             reads=[bPS[pa]], writes=[bSI[r]])
                        P.op('dve', lambda e, r=r, w=w, pb=pb, gi=gi, t0=t0: e.tensor_tensor(out=GST[gi][:, t0:t0 + w], in0=PS[pb][:, :w], in1=SI[r][:, :w], op=ALU.mult),
                             reads=[bPS[pb], bSI[r]], writes=[bGST[gi]])
                    clo = NCTX if last else 0
                    P.dma('sp', f'st{gi}', lambda e, gi=gi, hb=hb, clo=clo: e.dma_start(out=G_d[hb * 128:(hb + 1) * 128, clo:TT], in_=GST[gi][:, clo:TT]),
                          reads=[bGST[gi]])

        def ffn_down(l, es, last):
            WD = sb("gWD", [128, 22, D], BF16, es)
            Gt = [sb(f"gG{i}", [128, 22, 512], BF16, es) for i in range(2)]
            Xt = [sb(f"gX{i}", [128, KD, 512], F32, es) for i in range(2)]
            SQ = sb("gSQ", [128, KD, 512], F32, es)
            RS = sb("gRS", [128, 512], F32, es)
            GF = sb("gGF", [128, KD], F32, es)
            HX = sb("gHX", [128, KD, 2], F32, es)
            HX2 = sb("gHX2", [128, 2, 16], F32, es)
            bWD, bSQ, bRS, bGF, bHX, bHX2, bHCI, bHCO = B_(), B_(), B_(), B_(), B_(), B_(), B_(), B_()
            bG, bX = [B_(), B_()], [B_(), B_()]
            bPS = [B_() for _ in range(8)]
            load_weight(WD, w_down[l].rearrange("(k p) n -> p k n", p=128), 22, D, 0, bWD)
            if last:
                P.dma('sp', 'ld6', lambda e: e.dma_start(out=GF, in_=g_final), writes=[bGF])
            tiles = ([] if last else [(0, 256, 1)]) + [(256 + 512 * i, 512, 0) for i in range(4)]
            np_ = 0
            for it, (t0, w, mc) in enumerate(tiles):
                i = it % 2
                P.dma('sp', f'ld{i}', lambda e, i=i, t0=t0, w=w: e.dma_start(out=Gt[i][:, :, :w], in_=gd_v[:, :, t0:t0 + w]), writes=[bG[i]])
                P.dma('sp', f'ld{2 + i}', lambda e, i=i, t0=t0, w=w: e.dma_start(out=Xt[i][:, :, :w], in_=xtd_v[:, :, t0:t0 + w]), writes=[bX[i]])
                for db in range(KD):
                    pa = np_ % 4
                    np_ += 1

                    def mmd(e, db=db, i=i, w=w, pa=pa):
                        ins = None
                        for kb in range(22):
                            ins = e.matmul(PS[pa][:, :w], lhsT=WD[:, kb, db * 128:(db + 1) * 128], rhs=Gt[i][:, kb, :w],
                                           start=(kb == 0), stop=(kb == 21))
                        return ins
                    P.op('pe', mmd, reads=[bWD, bG[i]], writes=[bPS[pa]])
                    P.op('dve', lambda e, db=db, i=i, w=w, pa=pa, mc=mc: e.scalar_tensor_tensor(
                        out=Xt[i][:, db, :w], in0=PS[pa][:, :w], scalar=MOD[l][:, 40 + db, mc:mc + 1], in1=Xt[i][:, db, :w],
                        op0=ALU.mult, op1=ALU.add), reads=[bPS[pa], bX[i]], writes=[bX[i]])
                if not last:
                    P.dma('sp', f'st{i}', lambda e, i=i, t0=t0, w=w: e.dma_start(out=xtd_v[:, :, t0:t0 + w], in_=Xt[i][:, :, :w]), reads=[bX[i]])
                    if t0 + w == TT:
                        for hh in range(2):
                            P.op('dve', lambda e, i=i, hh=hh: e.tensor_copy(out=HX[:, :, hh], in_=Xt[i][:, :, 511 - hh]), reads=[bX[i]], writes=[bHX])
                        P.dma('sp', 'st2', lambda e: e.dma_start(out=HCI_d, in_=HX.rearrange("p k t -> p (k t)")), reads=[bHX], writes=[bHCI])
                        P.dma('pool', 'cc', lambda e: e.collective_compute("AllGather", ALU.bypass, replica_groups=pairs,
                                                                           ins=[HCI_d.opt()], outs=[HCO_d.opt()]),
                              reads=[bHCI], writes=[bHCO], inc=1)
                        P.dma('sp', 'ld4', lambda e: e.dma_start(out=HX2, in_=HCO_d.rearrange("(r p) n -> p r n", p=128)), reads=[bHCO], writes=[bHX2])
                        P.op('dve', lambda e: e.tensor_scalar(out=HX.rearrange("p k t -> p (k t)"), in0=HX2[:, 0, :], scalar1=SELM[:, 0:1], scalar2=None, op0=ALU.mult),
                             reads=[bHX2], writes=[bHX])
                        P.op('dve', lambda e: e.scalar_tensor_tensor(out=HX.rearrange("p k t -> p (k t)"), in0=HX2[:, 1, :], scalar=SELM[:, 1:2],
                                                                     in1=HX.rearrange("p k t -> p (k t)"), op0=ALU.mult, op1=ALU.add),
                             reads=[bHX2, bHX], writes=[bHX])
                        P.dma('sp', 'st3', lambda e: e.dma_start(out=xtd_v[:, :, TT:TT + 2], in_=HX), reads=[bHX])
                else:
                    P.op('act', lambda e, i=i: e.activation(out=SQ, in_=Xt[i], func=AF.Square), reads=[bX[i]], writes=[bSQ])

                    def mmq(e):
                        ins = None
                        for k in range(KD):
                            ins = e.matmul(PS[4], lhsT=ONES, rhs=SQ[:, k, :], start=(k == 0), stop=(k == KD - 1))
                        return ins
                    P.op('pe', mmq, reads=[bSQ], writes=[bPS[4]])
                    P.op('act', lambda e: e.activation(out=RS, in_=PS[4], func=AF.Ln, scale=1.0 / D, bias=EPS), reads=[bPS[4]], writes=[bRS])
                    P.op('act', lambda e: e.activation(out=RS, in_=RS, func=AF.Exp, scale=-0.5), reads=[bRS], writes=[bRS])
                    for k in range(KD):
                        P.op('dve', lambda e, k=k, i=i: e.scalar_tensor_tensor(out=SQ[:, k, :], in0=Xt[i][:, k, :], scalar=GF[:, k:k + 1], in1=RS,
                                                                               op0=ALU.mult, op1=ALU.mult), reads=[bX[i], bRS, bGF, bSQ], writes=[bSQ])
                    P.dma('sp', f'st{i}', lambda e, t0=t0: e.dma_start(out=yT_out.rearrange("(k p) t -> p k t", p=128)[:, :, t0 - NCTX:t0 - NCTX + 512], in_=SQ),
                          reads=[bSQ])

        for l in range(DEPTH):
            last = (l == DEPTH - 1)
            with ExitStack() as es:
                HT = sb("HT", [128, KD, TTH], BF16, es)
                with ExitStack() as es1:
                    norm_mod(l, 1, HT, es1)
                    P.flush()
                if debug and l == 0:
                    P.dma('sp', 'st0', lambda e: e.dma_start(out=DBG["ht"], in_=HT))
                    P.flush()
                if stop_after == 1:
                    return nc
                with ExitStack() as es2:
                    inproj(l, HT, es2)
                    P.flush()
            if stop_after == 2:
                return nc
            with ExitStack() as es:
                sweepA(l, es)
                P.flush()
            if stop_after == 3:
                return nc
            with ExitStack() as es:
                sweepB(l, es, last)
                P.flush()
            if stop_after == 4:
                return nc
            with ExitStack() as es:
                mixer_dense(l, es, last)
                P.flush()
            if stop_after == 5:
                return nc
            with ExitStack() as es:
                HT = sb("HT2", [128, KD, TTH], BF16, es)
                with ExitStack() as es1:
                    norm_mod(l, 2, HT, es1)
                    P.flush()
                with ExitStack() as es1:
                    ffn_up(l, HT, es1, last)
                    P.flush()
            with ExitStack() as es:
                ffn_down(l, es, last)
                P.flush()
            if stop_after == 6:
                return nc
    return nc


def _pool_mats(L, k, rowlen):
    Pm = np.zeros((L, L), np.float32)
    inv = np.zeros(L, np.float32)
    for t in range(L):
        r0 = (t // rowlen) * rowlen
        tt = t - r0
        lo = min(max(tt - k // 2, 0), rowlen)
        hi = min(max(tt + k // 2, 0), rowlen)
        Pm[r0 + lo:r0 + hi, t] = 1.0
        Pm[t, t] -= (hi - lo)
        inv[t] = 1.0 / (hi - lo)
    return Pm, inv


def _consts(rev):
    k = np.arange(128)[:, None]
    j = np.arange(128)[None, :]
    cm = np.stack([(k == j), (k > j), (k < j), (k <= j), (k >= j)], axis=1).astype(np.float32)
    pml = np.zeros((128, 4, 128), np.float32)
    pmc = np.zeros((128, 2, 4, 2, 128), np.float32)
    invc = np.zeros((4, 384), np.float32)
    for g, kk in enumerate((2, 4, 8, 16)):
        Pm, inv = _pool_mats(128, kk, 64)
        Pc, invcx = _pool_mats(256, kk, 256)
        if rev:
            Pm = Pm[::-1, ::-1]
            inv = inv[::-1]
            Pc = Pc[::-1, ::-1]
            invcx = invcx[::-1]
        pml[:, g, :] = Pm
        for tb_ in range(2):
            for tb in range(2):
                pmc[:, tb_, g, tb, :] = Pc[tb_ * 128:(tb_ + 1) * 128, tb * 128:(tb + 1) * 128]
        invc[g, :NCTX] = invcx
        invc[g, NCTX:] = inv
    invc = np.ascontiguousarray(np.broadcast_to(invc[None], (128, 4, 384)))
    return cm, pml, pmc, invc


def prep_inputs(inputs, n_cores=8):
    f = lambda a: np.ascontiguousarray(np.asarray(a, dtype=np.float32))
    inp = {k: np.asarray(v) for k, v in inputs.items()}
    pm = lambda v: f(v.reshape(-1, 128).T)

    def stack(fn):
        return f(np.stack([fn(l) for l in range(DEPTH)]))
    shared = dict(
        w_ada=f(inp["w_ada"]), w_ssd_out=f(inp["w_ssd_out"]), pool_w=f(inp["pool_w"]),
        w_pool_out=f(inp["w_pool_out"]), w_out=f(inp["w_out"]), w_gate_up=f(inp["w_gate_up"]),
        w_down=f(inp["w_down"]),
        b_ada=stack(lambda l: pm(inp["b_ada"][l])), g_mix=stack(lambda l: pm(inp["g_mix"][l])),
        g_ffn=stack(lambda l: pm(inp["g_ffn"][l])), conv_b=stack(lambda l: pm(inp["conv_b"][l])),
        norm_w=stack(lambda l: pm(inp["ssd_norm_w"][l])), pool_scale=stack(lambda l: pm(inp["pool_scale"][l])),
        g_final=pm(inp["g_final"]),
    )
    per_hf = []
    for hf in range(2):
        rev = hf == 1
        dA, dB = (1, 0) if rev else (0, 1)
        dtc = lambda d: np.arange(6144 + d * 32, 6144 + d * 32 + 32)
        idx = np.concatenate([np.arange(0, 2048), np.arange(2048, 6144), np.arange(7232, 9280),
                              np.arange(6208, 7232), dtc(dA), dtc(dB)])
        cm, pml, pmc, invc = _consts(rev)

        def cw(l):
            w = inp["conv_w"][l]
            if rev:
                w = w[::-1]
            return w.T.reshape(32, 128, 5).transpose(1, 0, 2)

        def bc64(v):
            r = np.concatenate([v[dA], v[dB]])
            return np.broadcast_to(r[None], (128, 64))

        def dsk(l, d):
            v = inp["d_skip"][l][d]
            return np.repeat(v.reshape(16, 2, 1), 64, axis=2).reshape(16, 128).T
        sel = np.zeros((128, 2), np.float32)
        sel[:, 1 - hf] = 1.0
        per_hf.append(dict(
            w_in=f(inp["w_in"][:, :, idx]),
            conv_w=stack(cw), dt_bias=stack(lambda l: bc64(inp["dt_bias"][l])),
            a_log=stack(lambda l: bc64(inp["a_log"][l])),
            dskA=stack(lambda l: dsk(l, dA)), dskB=stack(lambda l: dsk(l, dB)),
            cmat=f(cm), pm_lat=f(pml), pm_ctx=f(pmc), invc=f(invc), selm=f(sel),
        ))
    in_maps = []
    for core in range(n_cores):
        b, hf = core // 2, core % 2
        xb = inp["x"][b]
        cx = inp["ctx"][b]
        if hf == 0:
            lat, halo, cl = xb[:NLAT], xb[NLAT:NLAT + 2], cx
        else:
            lat, halo, cl = xb[NLAT:][::-1], xb[NLAT - 2:NLAT][::-1], cx[::-1]
        xT = f(np.concatenate([cl, lat, halo], axis=0).T)
        cc = f(np.stack([pm(inp["c"][b]), pm(inp["c_ctx"])], axis=-1))
        m = dict(xT=xT, cc=cc)
        m.update(shared)
        m.update(per_hf[hf])
        in_maps.append(m)
    return in_maps


def assemble(results, n_cores=8):
    B = n_cores // 2
    out = np.zeros((B, 2 * NLAT, D), np.float32)
    for core in range(n_cores):
        b, hf = core // 2, core % 2
        y = np.asarray(results[core]["yT"]).T
        if hf == 0:
            out[b, :NLAT] = y
        else:
            out[b, NLAT:] = y[::-1]
    return out


_NC_CACHE = {}


def kernel(**inputs):
    n = 8
    if "nc" not in _NC_CACHE:
        _NC_CACHE["nc"] = build(n)
    nc = _NC_CACHE["nc"]
    in_maps = prep_inputs(inputs, n)
    res = run_bass_kernel_spmd(nc, in_maps, core_ids=list(range(n)))
    return assemble(res.results, n)
```

```python
from contextlib import ExitStack
import numpy as np
import concourse.bass as bass
import concourse.mybir as mybir
from concourse.bass_utils import run_bass_kernel_spmd

F32 = mybir.dt.float32
BF16 = mybir.dt.bfloat16
AF = mybir.ActivationFunctionType
ALU = mybir.AluOpType

D = 1024
KD = 8
DEPTH = 2
NCTX = 256
NLAT = 2048
TT = NCTX + NLAT
TTH = TT + 2
NCH = TT // 128
DIN = 2048
NH = 32
HP = 64
NG = 8
NS = 128
XBCW = 4096
FFH = 2816
INCOLS = 9280
EPS = 1e-6
CZ, CX, CGT, CPL, CDT = 0, 2048, 6144, 8192, 9216
XW = 2308
LOFF = 260

ENGS = ['pe', 'act', 'dve', 'pool', 'sp']
SAME_ENGINE_SYNC = True


class Buf:
    __slots__ = ('name', 'w', 'r')

    def __init__(self, name=''):
        self.name = name
        self.w = None
        self.r = []


class Prog:
    def __init__(self, nc, n_dma_sems=60):
        self.nc = nc
        self.ops = {e: [] for e in ENGS}
        self.sems = {}
        self.cnt = {}
        for e in ENGS:
            self.sems[e] = nc.alloc_semaphore(name=f"c_{e}")
            self.cnt[e] = 0
        self.known = {e: {} for e in ENGS}
        self.dma_keys = {}
        self.n_dma_sems = n_dma_sems

    def _dma_sem(self, key):
        if key not in self.dma_keys:
            assert len(self.dma_keys) < self.n_dma_sems, "too many dma keys"
            s = self.nc.alloc_semaphore(name=f"d_{len(self.dma_keys)}")
            k = ('dma', key)
            self.sems[k] = s
            self.cnt[k] = 0
            self.dma_keys[key] = k
        return self.dma_keys[key]

    def _need(self, eng, ev, waits):
        if ev is None:
            return
        k, v = ev
        if self.known[eng].get(k, 0) >= v:
            return
        waits[k] = max(waits.get(k, 0), v)

    def _deps(self, eng, reads, writes):
        waits = {}
        for b in reads:
            if SAME_ENGINE_SYNC or b.w is None or b.w[0] != eng:
                self._need(eng, b.w, waits)
        for b in writes:
            if SAME_ENGINE_SYNC or b.w is None or b.w[0] != eng:
                self._need(eng, b.w, waits)
            for ev in b.r:
                if ev[0] == eng:
                    continue
                self._need(eng, ev, waits)
        for k, v in waits.items():
            self.known[eng][k] = max(self.known[eng].get(k, 0), v)
        return waits

    def _commit(self, ev, reads, writes):
        for b in reads:
            b.r.append(ev)
        for b in writes:
            b.w = ev
            b.r = []

    def op(self, eng, fn, reads=(), writes=()):
        waits = self._deps(eng, reads, writes)
        self.cnt[eng] += 1
        ev = (eng, self.cnt[eng])
        self.ops[eng].append((waits, fn, (eng, 1)))
        self._commit(ev, reads, writes)
        return ev

    def dma(self, queue, key, fn, reads=(), writes=(), inc=16):
        k = self._dma_sem(key)
        waits = self._deps(queue, reads, writes)
        prev = self.cnt[k]
        if prev > 0 and self.known[queue].get(k, 0) < prev:
            waits[k] = max(waits.get(k, 0), prev)
            self.known[queue][k] = prev
        self.cnt[k] += inc
        ev = (k, self.cnt[k])
        self.ops[queue].append((waits, fn, (k, inc)))
        self._commit(ev, reads, writes)
        return ev

    def barrier(self):
        for e in ENGS:
            waits = {}
            for k, v in self.cnt.items():
                if v > 0 and self.known[e].get(k, 0) < v:
                    waits[k] = v
                    self.known[e][k] = v
            self.ops[e].append((waits, None, None))

    def flush(self):
        self.barrier()
        nc = self.nc
        sems = self.sems
        ops = self.ops

        def run(engname):
            def f(eng):
                for waits, fn, inc in ops[engname]:
                    for k, v in waits.items():
                        eng.wait_ge(sems[k], v)
                    if fn is not None:
                        ins = fn(eng)
                        ins.then_inc(sems[inc[0]], inc[1])
            return f

        with nc.Block() as block:
            block.tensor(run('pe'))
            block.scalar(run('act'))
            block.vector(run('dve'))
            block.gpsimd(run('pool'))
            block.sync(run('sp'))
        self.ops = {e: [] for e in ENGS}


def bc(ap, dims):
    return bass.AP(ap.tensor, ap.offset, [list(ap.ap[0])] + [list(d) for d in dims])


def build(n_cores=8, debug=False, stop_after=None, ext_scratch=False):
    nc = bass.Bass("TRN2", target_bir_lowering=False)
    P = Prog(nc)
    dbg_kind = "ExternalOutput" if (debug or ext_scratch) else "Internal"

    def din(name, shape):
        return nc.dram_tensor(name, list(shape), F32, kind="ExternalInput").ap()

    xT_in = din("xT", [D, TTH])
    cc_in = din("cc", [128, KD, 2])
    w_ada = din("w_ada", [DEPTH, D, 6 * D])
    b_ada = din("b_ada", [DEPTH, 128, 48])
    g_mix = din("g_mix", [DEPTH, 128, KD])
    g_ffn = din("g_ffn", [DEPTH, 128, KD])
    w_in = din("w_in", [DEPTH, D, INCOLS])
    conv_w = din("conv_w", [DEPTH, 128, 32, 5])
    conv_b = din("conv_b", [DEPTH, 128, 32])
    dt_bias = din("dt_bias", [DEPTH, 128, 64])
    a_log = din("a_log", [DEPTH, 128, 64])
    dskA = din("dskA", [DEPTH, 128, 16])
    dskB = din("dskB", [DEPTH, 128, 16])
    norm_w = din("norm_w", [DEPTH, 128, 16])
    w_ssd_out = din("w_ssd_out", [DEPTH, DIN, D])
    pool_w = din("pool_w", [DEPTH, 4, 256, 256])
    pool_scale = din("pool_scale", [DEPTH, 128, 8])
    w_pool_out = din("w_pool_out", [DEPTH, D, D])
    w_out = din("w_out", [DEPTH, D, D])
    w_gate_up = din("w_gate_up", [DEPTH, D, 2 * FFH])
    w_down = din("w_down", [DEPTH, FFH, D])
    g_final = din("g_final", [128, KD])
    cmat = din("cmat", [128, 5, 128])
    pm_lat = din("pm_lat", [128, 4, 128])
    pm_ctx = din("pm_ctx", [128, 2, 4, 2, 128])
    invc = din("invc", [128, 4, 384])
    selm = din("selm", [128, 2])

    yT_out = nc.dram_tensor("yT", [D, NLAT], F32, kind="ExternalOutput").ap()

    def dscr(name, shape, dt):
        return nc.dram_tensor(name, list(shape), dt, kind=dbg_kind).ap()

    XT_d = dscr("XT_d", [D, TTH], F32)
    ZS_d = dscr("ZS_d", [DIN, TT], BF16)
    GS_d = dscr("GS_d", [DIN, TT], BF16)
    XBC_d = dscr("XBC_d", [XBCW, XW], BF16)
    PM2_d = dscr("PM2_d", [D, TT], BF16)
    Y_d = dscr("Y_d", [DIN, TT], F32)
    SB_d = dscr("SB_d", [NCH, 128, DIN], F32)
    YN_d = dscr("YN_d", [DIN, TT], BF16)
    G_d = dscr("G_d", [FFH, TT], BF16)
    CCI_d = nc.dram_tensor("CCI_d", [128, DIN], F32).ap()
    CCO_d = nc.dram_tensor("CCO_d", [256, DIN], F32).ap()
    HCI_d = nc.dram_tensor("HCI_d", [128, 16], F32).ap()
    HCO_d = nc.dram_tensor("HCO_d", [256, 16], F32).ap()
    DBG = {}
    if debug:
        DBG["dtt"] = nc.dram_tensor("DBG_dtt", [128, NCH, 64], F32, kind="ExternalOutput").ap()
        DBG["mod"] = nc.dram_tensor("DBG_mod", [DEPTH, 128, 48, 2], F32, kind="ExternalOutput").ap()
        DBG["ht"] = nc.dram_tensor("DBG_ht", [128, KD, TTH], BF16, kind="ExternalOutput").ap()
        DBG["sa"] = nc.dram_tensor("DBG_sa", [128, DIN], F32, kind="ExternalOutput").ap()
        DBG["yt"] = nc.dram_tensor("DBG_yt", [DIN, TT], F32, kind="ExternalOutput").ap()

    pairs = [[2 * i, 2 * i + 1] for i in range(n_cores // 2)]

    with ExitStack() as top:
        _uid = [0]

        def sb(name, shape, dt, es=top):
            _uid[0] += 1
            return es.enter_context(nc.sbuf_tensor(f"{name}_{_uid[0]}", list(shape), dt)).ap()

        PS = [nc.alloc_psum_tensor(f"ps{i}", [128, 512], F32).ap() for i in range(8)]

        CM = sb("CM", [128, 5, 128], F32)
        CMB = sb("CMB", [128, 5, 128], BF16)
        ONES = sb("ONES", [128, 128], F32)
        ONESB = sb("ONESB", [128, 128], BF16)
        MOD = [sb(f"MOD{l}", [128, 48, 2], F32) for l in range(DEPTH)]
        A1 = [sb(f"A1_{l}", [128, KD, 2], F32) for l in range(DEPTH)]
        A2 = [sb(f"A2_{l}", [128, KD, 2], F32) for l in range(DEPTH)]
        SELM = sb("SELM", [128, 2], F32)
        DTT = sb("DTT", [128, NCH, 64], F32)
        DECB = sb("DECB", [128, NCH, 32], F32)
        SA = sb("SA", [128, DIN], F32)
        SAb = sb("SAb", [128, DIN], BF16)
        ident_f = CM[:, 0, :]
        UA_f = CM[:, 1, :]
        UB_f = CM[:, 2, :]
        TLE_f = CM[:, 3, :]
        TGE_f = CM[:, 4, :]
        ident_b = CMB[:, 0, :]

        def B_(n=''):
            return Buf(n)

        with ExitStack() as es:
            b_cm, b_cmb, b_ones, b_sel = B_(), B_(), B_(), B_()
            P.dma('sp', 'ld0', lambda e: e.dma_start(out=CM, in_=cmat), writes=[b_cm])
            P.dma('pool', 'w0', lambda e: e.dma_start(out=CMB, in_=cmat), writes=[b_cmb])
            P.dma('sp', 'ld1', lambda e: e.dma_start(out=SELM, in_=selm), writes=[b_sel])
            P.op('dve', lambda e: e.memset(ONES, 1.0), writes=[b_ones])
            P.op('dve', lambda e: e.memset(ONESB, 1.0), writes=[b_ones])
            xb = [sb(f"xb{i}", [128, KD, 512], F32, es) for i in range(2)]
            bxb = [B_(), B_()]
            bxt = B_()
            xin_v = xT_in.rearrange("(k p) t -> p k t", p=128)
            xtd_v = XT_d.rearrange("(k p) t -> p k t", p=128)
            tiles0 = [(0, 256)] + [(256 + 512 * i, 512) for i in range(4)] + [(TT, 2)]
            for i, (c0, w) in enumerate(tiles0):
                j = i % 2
                P.dma('sp', f'ld{j}', lambda e, j=j, c0=c0, w=w: e.dma_start(out=xb[j][:, :, :w], in_=xin_v[:, :, c0:c0 + w]),
                      writes=[bxb[j]])
                P.dma('sp', f'st{j}', lambda e, j=j, c0=c0, w=w: e.dma_start(out=xtd_v[:, :, c0:c0 + w], in_=xb[j][:, :, :w]),
                      reads=[bxb[j]], writes=[bxt])
            ccf = sb("ccf", [128, KD, 2], F32, es)
            ccb = sb("ccb", [128, KD, 2], BF16, es)
            b_ccf, b_ccb = B_(), B_()
            P.dma('sp', 'ld2', lambda e: e.dma_start(out=ccf, in_=cc_in), writes=[b_ccf])
            P.op('act', lambda e: e.activation(out=ccb, in_=ccf, func=AF.Silu), reads=[b_ccf], writes=[b_ccb])
            wa = [sb(f"wa{i}", [128, KD, 512], BF16, es) for i in range(2)]
            bwa = [B_(), B_()]
            bada = sb("bada", [128, 48], F32, es)
            gm = sb("gm", [128, KD], F32, es)
            gf = sb("gf", [128, KD], F32, es)
            b_bada, b_gm, b_gf, b_ps = B_(), B_(), B_(), B_()
            for l in range(DEPTH):
                b_mod, b_a = B_(), B_()
                P.dma('sp', 'ld3', lambda e, l=l: e.dma_start(out=bada, in_=b_ada[l]), writes=[b_bada])
                P.dma('sp', 'ld4', lambda e, l=l: e.dma_start(out=gm, in_=g_mix[l]), writes=[b_gm])
                P.dma('sp', 'ld5', lambda e, l=l: e.dma_start(out=gf, in_=g_ffn[l]), writes=[b_gf])
                wv = w_ada[l].rearrange("(k p) n -> p k n", p=128)
                for jb in range(12):
                    j = jb % 2
                    P.dma('pool', f'w{j}', lambda e, j=j, jb=jb, wv=wv: e.dma_start(out=wa[j], in_=wv[:, :, jb * 512:(jb + 1) * 512]),
                          writes=[bwa[j]])
                    for q in range(4):
                        cb = jb * 4 + q

                        def mm(e, j=j, q=q, cb=cb):
                            ins = None
                            for k in range(KD):
                                ins = e.matmul(PS[0][:, cb * 2:cb * 2 + 2], lhsT=wa[j][:, k, q * 128:(q + 1) * 128],
                                               rhs=ccb[:, k, :], start=(k == 0), stop=(k == KD - 1))
                            return ins
                        P.op('pe', mm, reads=[bwa[j], b_ccb], writes=[b_ps])
                P.op('dve', lambda e, l=l: e.tensor_tensor(out=MOD[l], in0=PS[0][:, 0:96].rearrange("p (c t) -> p c t", t=2),
                                                           in1=bc(bada, [[1, 48], [0, 2]]), op=ALU.add),
                     reads=[b_ps, b_bada], writes=[b_mod])
                P.op('dve', lambda e, l=l: e.scalar_tensor_tensor(out=A1[l], in0=MOD[l][:, 8:16, :], scalar=1.0,
                                                                  in1=bc(gm, [[1, KD], [0, 2]]), op0=ALU.add, op1=ALU.mult),
                     reads=[b_mod, b_gm], writes=[b_a])
                P.op('dve', lambda e, l=l: e.scalar_tensor_tensor(out=A2[l], in0=MOD[l][:, 32:40, :], scalar=1.0,
                                                                  in1=bc(gf, [[1, KD], [0, 2]]), op0=ALU.add, op1=ALU.mult),
                     reads=[b_mod, b_gf], writes=[b_a])
                if debug:
                    P.dma('sp', 'st2', lambda e, l=l: e.dma_start(out=DBG["mod"][l], in_=MOD[l]), reads=[b_mod])
            P.flush()
        if stop_after == 0:
            return nc

        def norm_mod(l, which, HT, es):
            A = A1[l] if which == 1 else A2[l]
            boff = 0 if which == 1 else 24
            xt = [sb(f"nx{i}", [128, KD, 512], F32, es) for i in range(2)]
            sq = [sb(f"nq{i}", [128, KD, 512], BF16, es) for i in range(2)]
            xn = [sb(f"nn{i}", [128, KD, 512], F32, es) for i in range(2)]
            rs = [sb(f"nr{i}", [128, 512], F32, es) for i in range(2)]
            ONESb = sb("nob", [128, 128], BF16, es)
            bob = B_()
            P.op('dve', lambda e: e.memset(ONESb, 1.0), writes=[bob])
            bx, bq, bn, br, bp = [B_(), B_()], [B_(), B_()], [B_(), B_()], [B_(), B_()], [B_(), B_()]
            bht = B_()
            tiles = [(0, 256, 1)] + [(256 + 512 * i, 512, 0) for i in range(4)]
            if which == 1:
                tiles.append((TT, 2, 0))

            def front(i):
                c0, w, mc = tiles[i]
                j = i % 2
                ps = PS[j]
                P.dma('sp', f'ld{j}', lambda e, j=j, c0=c0, w=w: e.dma_start(out=xt[j][:, :, :w], in_=xtd_v[:, :, c0:c0 + w]),
                      writes=[bx[j]])
                P.op('act', lambda e, j=j, w=w: e.activation(out=sq[j][:, :, :w], in_=xt[j][:, :, :w], func=AF.Square),
                     reads=[bx[j]], writes=[bq[j]])

                def mm(e, j=j, w=w, ps=ps):
                    ins = None
                    for k in range(KD):
                        ins = e.matmul(ps[:, :w], lhsT=ONESb, rhs=sq[j][:, k, :w], start=(k == 0), stop=(k == KD - 1))
                    return ins
                P.op('pe', mm, reads=[bq[j], bob], writes=[bp[j]])

            def back(i):
                c0, w, mc = tiles[i]
                j = i % 2
                ps = PS[j]
                P.op('act', lambda e, j=j, w=w, ps=ps: e.activation(out=rs[j][:, :w], in_=ps[:, :w], func=AF.Ln, scale=1.0 / D, bias=EPS),
                     reads=[bp[j]], writes=[br[j]])
                P.op('act', lambda e, j=j, w=w: e.activation(out=rs[j][:, :w], in_=rs[j][:, :w], func=AF.Exp, scale=-0.5),
                     reads=[br[j]], writes=[br[j]])
                P.op('dve', lambda e, j=j, w=w: e.tensor_tensor(out=xn[j][:, :, :w], in0=xt[j][:, :, :w],
                                                                in1=bc(rs[j], [[0, KD], [1, w]]), op=ALU.mult),
                     reads=[bx[j], br[j]], writes=[bn[j]])
                for k in range(KD):
                    if k % 3 == 2:
                        P.op('dve', lambda e, j=j, w=w, k=k, c0=c0, mc=mc: e.tensor_scalar(
                            out=HT[:, k, c0:c0 + w], in0=xn[j][:, k, :w], scalar1=A[:, k, mc:mc + 1],
                            scalar2=MOD[l][:, boff + k, mc:mc + 1], op0=ALU.mult, op1=ALU.add),
                            reads=[bn[j]], writes=[bht])
                    else:
                        P.op('act', lambda e, j=j, w=w, k=k, c0=c0, mc=mc: e.activation(
                            out=HT[:, k, c0:c0 + w], in_=xn[j][:, k, :w], func=AF.Identity,
                            scale=A[:, k, mc:mc + 1], bias=MOD[l][:, boff + k, mc:mc + 1]),
                            reads=[bn[j]], writes=[bht])

            front(0)
            for i in range(len(tiles)):
                if i + 1 < len(tiles):
                    front(i + 1)
                back(i)

        def inproj(l, HT, es):
            W = [sb(f"ipW{i}", [128, KD, 512], BF16, es) for i in range(2)]
            bW = [B_(), B_()]
            U = [sb(f"ipU{i}", [128, XW + 4], F32, es) for i in range(2)]
            bU = [B_(), B_()]
            ACC = [sb(f"ipA{i}", [128, XW], F32, es) for i in range(2)]
            bACC = [B_(), B_()]
            STG = [sb(f"ipS{i}", [128, XW], BF16, es) for i in range(2)]
            bSTG = [B_(), B_()]
            cw = sb("ipcw", [128, 32, 5], F32, es)
            cb = sb("ipcb", [128, 32], F32, es)
            b_cw, b_cb = B_(), B_()
            bPS = [B_() for _ in range(8)]
            wv = w_in[l].rearrange("(k p) n -> p k n", p=128)
            P.dma('sp', 'ld0', lambda e: e.dma_start(out=cw, in_=conv_w[l]), writes=[b_cw])
            P.dma('sp', 'ld1', lambda e: e.dma_start(out=cb, in_=conv_b[l]), writes=[b_cb])
            for i in range(2):
                P.op('dve', lambda e, i=i: e.memset(U[i], 0.0), writes=[bU[i]])
                P.op('dve', lambda e, i=i: e.memset(ACC[i], 0.0), writes=[bACC[i]])
            bHT = B_()
            nw = [0]

            def load_w(c0, ncol):
                j = nw[0] % 2
                nw[0] += 1
                P.dma('pool', f'w{j}', lambda e, j=j, c0=c0, ncol=ncol: e.dma_start(out=W[j][:, :, :ncol], in_=wv[:, :, c0:c0 + ncol]),
                      writes=[bW[j]])
                return j

            dtb = sb("ipdtb", [128, 64], F32, es)
            dx = sb("ipdx", [128, NCH * 64], F32, es)
            d1 = sb("ipd1", [128, NCH * 64], F32, es)
            b_dtb, b_dx, b_d1, b_dtt = B_(), B_(), B_(), B_()
            P.dma('sp', 'ld2', lambda e: e.dma_start(out=dtb, in_=dt_bias[l]), writes=[b_dtb])
            j = load_w(CDT, 64)
            for c in range(NCH):
                pb = 5 + c // 8
                po = (c % 8) * 64

                def mm(e, j=j, c=c, pb=pb, po=po):
                    ins = None
                    for k in range(KD):
                        ins = e.matmul(PS[pb][:, po:po + 64], lhsT=HT[:, k, c * 128:(c + 1) * 128], rhs=W[j][:, k, 0:64],
                                       start=(k == 0), stop=(k == KD - 1))
                    return ins
                P.op('pe', mm, reads=[bW[j], bHT], writes=[bPS[pb]])
            for pb, nchk in ((5, 8), (6, 8), (7, 2)):
                c0 = (pb - 5) * 8
                P.op('dve', lambda e, pb=pb, nchk=nchk, c0=c0: e.tensor_tensor(
                    out=dx[:, c0 * 64:(c0 + nchk) * 64].rearrange("p (c h) -> p c h", h=64),
                    in0=PS[pb][:, 0:nchk * 64].rearrange("p (c h) -> p c h", h=64),
                    in1=bc(dtb, [[0, nchk], [1, 64]]), op=ALU.add), reads=[bPS[pb], b_dtb], writes=[b_dx])
            P.op('act', lambda e: e.activation(out=d1, in_=dx, func=AF.Abs), reads=[b_dx], writes=[b_d1])
            P.op('act', lambda e: e.activation(out=d1, in_=d1, func=AF.Exp, scale=-1.0), reads=[b_d1], writes=[b_d1])
            P.op('act', lambda e: e.activation(out=d1, in_=d1, func=AF.Ln, bias=1.0), reads=[b_d1], writes=[b_d1])
            P.op('dve', lambda e: e.tensor_scalar_max(out=dx, in0=dx, scalar1=0.0), reads=[b_dx], writes=[b_dx])
            P.op('dve', lambda e: e.tensor_tensor(out=DTT.rearrange("p c h -> p (c h)"), in0=dx, in1=d1, op=ALU.add),
                 reads=[b_dx, b_d1], writes=[b_dtt])
            if debug and l == 0:
                P.dma('sp', 'st3', lambda e: e.dma_start(out=DBG["dtt"], in_=DTT), reads=[b_dtt])

            PMl = sb("ipPMl", [128, 4, 128], BF16, es)
            PMc = sb("ipPMc", [128, 2, 4, 2, 128], BF16, es)
            INV = sb("ipINV", [128, 4, 384], F32, es)
            PW = sb("ipPW", [128, 4, 2, 256], BF16, es)
            PSC = sb("ipPSC", [128, 8], F32, es)
            UT = [sb(f"ipUT{i}", [128, 2, 512], BF16, es) for i in range(2)]
            bUT = [B_(), B_()]
            PMT = [sb(f"ipPMT{i}", [128, 4, 256], BF16, es) for i in range(2)]
            bPMT = [B_(), B_()]
            PM2S = sb("ipPM2S", [128, 4, TT], BF16, es)
            b_pm2s = B_()
            b_pml, b_pmc, b_inv, b_pw, b_psc = B_(), B_(), B_(), B_(), B_()
            P.dma('pool', 'w2', lambda e: e.dma_start(out=PMl, in_=pm_lat), writes=[b_pml])
            P.dma('pool', 'w3', lambda e: e.dma_start(out=PMc, in_=pm_ctx), writes=[b_pmc])
            P.dma('sp', 'ld3', lambda e: e.dma_start(out=INV, in_=invc), writes=[b_inv])
            P.dma('pool', 'w4', lambda e: e.dma_start(out=PW, in_=pool_w[l].rearrange("g (ib p) o -> p g ib o", p=128)), writes=[b_pw])
            P.dma('sp', 'ld4', lambda e: e.dma_start(out=PSC, in_=pool_scale[l]), writes=[b_psc])
            it = 0
            for jp in range(2):
                j = load_w(CPL + jp * 512, 512)
                units = [(0, 2)] + [(c, 1) for c in range(2, NCH)]
                for (c0, nck) in units:
                    u = it % 2
                    it += 1
                    wtok = nck * 128
                    for cc in range(nck):
                        def mm(e, j=j, c=c0 + cc):
                            ins = None
                            for k in range(KD):
                                ins = e.matmul(PS[4], lhsT=HT[:, k, c * 128:(c + 1) * 128], rhs=W[j][:, k, :],
                                               start=(k == 0), stop=(k == KD - 1))
                            return ins
                        P.op('pe', mm, reads=[bW[j], bHT], writes=[bPS[4]])
                        P.op('act', lambda e, u=u, cc=cc: e.activation(out=UT[u][:, cc, :], in_=PS[4], func=AF.Copy),
                             reads=[bPS[4]], writes=[bUT[u]])
                    def pbk(banks, q, wtok=wtok):
                        if wtok == 128:
                            return banks[0], q * 128
                        return banks[q // 2], (q % 2) * 256
                    for q in range(4):
                        g = 2 * jp + q // 2
                        pbi, po = pbk((5, 7), q)

                        def mmp(e, u=u, q=q, g=g, nck=nck, pbi=pbi, po=po):
                            ins = None
                            if nck == 1:
                                ins = e.matmul(PS[pbi][:, po:po + 128], lhsT=UT[u][:, 0, q * 128:(q + 1) * 128],
                                               rhs=PMl[:, g, :], start=True, stop=True)
                            else:
                                for tb in range(2):
                                    for tb_ in range(2):
                                        ins = e.matmul(PS[pbi][:, po + tb * 128:po + (tb + 1) * 128],
                                                       lhsT=UT[u][:, tb_, q * 128:(q + 1) * 128],
                                                       rhs=PMc[:, tb_, g, tb, :], start=(tb_ == 0), stop=(tb_ == 1))
                            return ins
                        P.op('pe', mmp, reads=[bUT[u], b_pml, b_pmc], writes=[bPS[pbi]])
                    for gg in range(2):
                        g = 2 * jp + gg
                        ioff = 0 if nck == 2 else 256
                        pbi, po = pbk((5, 7), 2 * gg)
                        P.op('dve', lambda e, u=u, gg=gg, g=g, wtok=wtok, ioff=ioff, pbi=pbi, po=po: e.tensor_tensor(
                            out=PMT[u][:, 2 * gg:2 * gg + 2, :wtok],
                            in0=PS[pbi][:, po:po + 2 * wtok].rearrange("p (q t) -> p q t", t=wtok),
                            in1=bc(INV[:, g, ioff:ioff + wtok], [[0, 2], [1, wtok]]), op=ALU.mult),
                            reads=[bPS[pbi], b_inv], writes=[bPMT[u]])
                    for gg in range(2):
                        g = 2 * jp + gg
                        for ob in range(2):
                            q = 2 * gg + ob
                            pbi, po = pbk((6, 3), q)

                            def mmw(e, u=u, g=g, gg=gg, ob=ob, wtok=wtok, pbi=pbi, po=po):
                                ins = None
                                for ib in range(2):
                                    ins = e.matmul(PS[pbi][:, po:po + wtok], lhsT=PW[:, g, ib, ob * 128:(ob + 1) * 128],
                                                   rhs=PMT[u][:, 2 * gg + ib, :wtok], start=(ib == 0), stop=(ib == 1))
                                return ins
                            P.op('pe', mmw, reads=[bPMT[u], b_pw], writes=[bPS[pbi]])
                    for q in range(4):
                        blk = jp * 4 + q
                        pbi, po = pbk((6, 3), q)
                        P.op('act', lambda e, q=q, blk=blk, wtok=wtok, c0=c0, pbi=pbi, po=po: e.activation(
                            out=PM2S[:, q, c0 * 128:c0 * 128 + wtok], in_=PS[pbi][:, po:po + wtok],
                            func=AF.Copy, scale=PSC[:, blk:blk + 1]), reads=[bPS[pbi], b_psc], writes=[b_pm2s])
                P.dma('sp', 'st0', lambda e, jp=jp: e.dma_start(
                    out=PM2_d[jp * 512:(jp + 1) * 512, :].rearrange("(q p) t -> p q t", p=128), in_=PM2S),
                    reads=[b_pm2s])

            tiles = [(0, 256)] + [(256 + 512 * i, 512) for i in range(4)]
            nps = [0]
            nstg = [0]
            nxb = [0]
            for jb in range(16):
                c0w = jb * 512
                fam = 'z' if jb < 4 else ('x' if jb < 12 else 'g')
                j = load_w(c0w, 512)
                for q in range(4):
                    if fam == 'x':
                        blk = (jb - 4) * 4 + q
                        ui = nxb[0] % 2
                        nxb[0] += 1
                        tl = tiles + [(TT, 2)]
                    else:
                        si = nstg[0] % 2
                        nstg[0] += 1
                        tl = tiles
                    for (t0, w) in tl:
                        pb = nps[0] % 4
                        nps[0] += 1

                        def mm(e, j=j, q=q, t0=t0, w=w, pb=pb):
                            ins = None
                            for k in range(KD):
                                ins = e.matmul(PS[pb][:, :w], lhsT=W[j][:, k, q * 128:(q + 1) * 128], rhs=HT[:, k, t0:t0 + w],
                                               start=(k == 0), stop=(k == KD - 1))
                            return ins
                        P.op('pe', mm, reads=[bW[j], bHT], writes=[bPS[pb]])
                        if fam == 'x':
                            uo = 2 if t0 == 0 else (LOFF + 2 + (t0 - NCTX))
                            P.op('act', lambda e, ui=ui, uo=uo, w=w, pb=pb: e.activation(out=U[ui][:, uo:uo + w], in_=PS[pb][:, :w], func=AF.Copy),
                                 reads=[bPS[pb]], writes=[bU[ui]])
                            if w > 2:
                                P.op('act', lambda e, ui=ui, uo=uo, w=w, pb=pb, blk=blk: e.activation(
                                    out=ACC[ui][:, uo - 2:uo - 2 + w], in_=PS[pb][:, :w], func=AF.Copy, scale=cw[:, blk, 2:3]),
                                    reads=[bPS[pb], b_cw], writes=[bACC[ui]])
                        else:
                            fn = AF.Silu if fam == 'z' else AF.Sigmoid
                            P.op('act', lambda e, si=si, t0=t0, w=w, pb=pb, fn=fn: e.activation(out=STG[si][:, t0:t0 + w], in_=PS[pb][:, :w], func=fn),
                                 reads=[bPS[pb]], writes=[bSTG[si]])
                    if fam == 'x':
                        ai = ui
                        for tp in (0, 1, 3, 4):
                            P.op('dve', lambda e, ui=ui, ai=ai, blk=blk, tp=tp: e.scalar_tensor_tensor(
                                out=ACC[ai], in0=U[ui][:, tp:tp + XW], scalar=cw[:, blk, tp:tp + 1], in1=ACC[ai],
                                op0=ALU.mult, op1=ALU.add), reads=[bU[ui], b_cw, bACC[ai]], writes=[bACC[ai]])
                        si = nstg[0] % 2
                        nstg[0] += 1
                        P.op('act', lambda e, si=si, ai=ai, blk=blk: e.activation(out=STG[si], in_=ACC[ai], func=AF.Silu, bias=cb[:, blk:blk + 1]),
                             reads=[bACC[ai], b_cb], writes=[bSTG[si]])
                        P.dma('sp', f'st{si}', lambda e, si=si, blk=blk: e.dma_start(out=XBC_d[blk * 128:(blk + 1) * 128, :], in_=STG[si]),
                              reads=[bSTG[si]])
                    else:
                        dst = ZS_d if fam == 'z' else GS_d
                        blk = (jb * 4 + q) if fam == 'z' else ((jb - 12) * 4 + q)
                        P.dma('sp', f'st{si}', lambda e, si=si, blk=blk, dst=dst: e.dma_start(out=dst[blk * 128:(blk + 1) * 128, :], in_=STG[si][:, 0:TT]),
                              reads=[bSTG[si]])


        xbc_v = XBC_d.rearrange("(blk p) t -> p blk t", p=128)
        yd_v = Y_d.rearrange("(blk p) t -> p blk t", p=128)
        zs_v = ZS_d.rearrange("(blk p) t -> p blk t", p=128)
        ynd_v = YN_d.rearrange("(blk p) t -> p blk t", p=128)

        def chunk_col(c):
            return c * 128 if c < 2 else LOFF + (c - 2) * 128

        def pad_ap(X, h0, nh):
            return bass.AP(X.tensor, X.offset + h0 * 128, [list(X.ap[0]), [256, nh // 2], [192, 2], [1, 64]])

        def sweepA(l, es):
            XB = [sb(f"aXB{i}", [128, 32, 128], BF16, es) for i in range(2)]
            bXB = [B_(), B_()]
            XDA = sb("aXDA", [128, 32, 128], BF16, es)
            XDB = sb("aXDB", [128, 32, 128], BF16, es)
            XWA = sb("aXWA", [128, DIN], BF16, es)
            XWB = sb("aXWB", [128, DIN], BF16, es)
            BT = sb("aBT", [128, 1024], BF16, es)
            AN = sb("aAN", [128, 64], F32, es)
            DTA = sb("aDTA", [128, 64], F32, es)
            EXPO = sb("aEXPO", [128, 128], F32, es)
            FAC = sb("aFAC", [128, 64], F32, es)
            E = [sb(f"aE{i}", [128, 4, 128], F32, es) for i in range(2)]
            MP = [[sb(f"aMP{i}{d}", [128, 4, 128], BF16, es) for d in range(2)] for i in range(2)]
            CBTm = sb("aCBT", [128, 2, 8, 128], F32, es)
            DBH = sb("aDBH", [128, 2, DIN], BF16, es)
            DTH = sb("aDTH", [128, 2, 32], BF16, es)
            bDTH = B_()
            EA = sb("aEA", [128, DIN], F32, es)
            T1 = sb("aT1", [128, DIN], F32, es)
            YP = [sb(f"aYP{i}", [128, DIN], F32, es) for i in range(2)]
            SBL = [sb(f"aSBL{i}", [128, DIN], F32, es) for i in range(2)]
            DSK = sb("aDSK", [128, 16, 128], BF16, es)
            dsa = sb("adsa", [128, 16], F32, es)
            dsb = sb("adsb", [128, 16], F32, es)
            bXD, bXW, bBT, bAN, bDTA, bEXPO, bFAC = B_(), B_(), B_(), B_(), B_(), B_(), B_()
            bE = [B_(), B_()]
            bMP = [[B_(), B_()], [B_(), B_()]]
            bCBT, bDBC, bEA, bT1, bDSK, bds, bSA, bSAb, bDECB = B_(), B_(), B_(), B_(), B_(), B_(), B_(), B_(), B_()
            bYP, bSBL = [B_(), B_()], [B_(), B_()]
            bPS = [B_() for _ in range(8)]
            PSb = [PS[i].bitcast(BF16) for i in range(3)]
            P.dma('sp', 'ld0', lambda e: e.dma_start(out=AN, in_=a_log[l]), writes=[bAN])
            P.op('act', lambda e: e.activation(out=AN, in_=AN, func=AF.Exp), reads=[bAN], writes=[bAN])
            P.op('dve', lambda e: e.tensor_scalar_mul(out=AN, in0=AN, scalar1=-1.0), reads=[bAN], writes=[bAN])
            P.dma('sp', 'ld1', lambda e: e.dma_start(out=dsa, in_=dskA[l]), writes=[bds])
            P.dma('sp', 'ld2', lambda e: e.dma_start(out=dsb, in_=dskB[l]), writes=[bds])
            P.op('dve', lambda e: e.tensor_tensor(out=dsa, in0=dsa, in1=dsb, op=ALU.add), reads=[bds], writes=[bds])
            for pp in range(16):
                P.op('dve', lambda e, pp=pp: e.tensor_scalar(out=DSK[:, pp, :], in0=ident_f, scalar1=dsa[:, pp:pp + 1], scalar2=None, op0=ALU.mult),
                     reads=[bds], writes=[bDSK])
            P.op('dve', lambda e: e.memset(XDA, 0.0), writes=[bXD])
            P.op('dve', lambda e: e.memset(XDB, 0.0), writes=[bXD])
            P.op('dve', lambda e: e.memset(SA, 0.0), writes=[bSA])
            P.op('dve', lambda e: e.memset(SAb, 0.0), writes=[bSAb])

            R2 = [sb(f"aR2x{i}", [128, 4, 128], F32, es) for i in range(3)]
            bR2 = [B_(), B_(), B_()]
            P.dma('sp', 'ld3', lambda e: e.dma_start(out=XB[0], in_=xbc_v[:, :, 0:128]), writes=[bXB[0]])
            for c in range(NCH):
                i = c % 2
                co = chunk_col(c)
                yi = c % 2
                if c + 1 < NCH:
                    con = chunk_col(c + 1)
                    P.dma('sp', f'ld{3 + (1 - i)}', lambda e, i=i, con=con: e.dma_start(out=XB[1 - i], in_=xbc_v[:, :, con:con + 128]), writes=[bXB[1 - i]])
                P.op('dve', lambda e, c=c: e.tensor_tensor(out=DTA, in0=DTT[:, c, :], in1=AN, op=ALU.mult), reads=[bAN], writes=[bDTA])
                P.op('dve', lambda e: e.tensor_copy(out=DTH[:, 0, :], in_=DTA[:, 0:32]), reads=[bDTA], writes=[bDTH])
                P.op('dve', lambda e: e.tensor_tensor(out=DTH[:, 1, :], in0=DTA[:, 0:32], in1=DTH[:, 0, :], op=ALU.subtract), reads=[bDTA, bDTH], writes=[bDTH])
                for hl in range(2):
                    P.op('dve', lambda e, hl=hl: e.tensor_copy(out=DBH[:, hl, :].rearrange("p (h q) -> p h q", q=64), in_=bc(DTH[:, hl, :], [[1, 32], [0, 64]])),
                         reads=[bDTH], writes=[bDBC])

                def emit_R2(n):
                    g, d = n // 2, n % 2
                    Mk = TLE_f if d == 0 else TGE_f
                    rr = n % 3
                    P.op('pool' if n % 2 == 1 else 'dve', lambda e, rr=rr, d=d, g=g, Mk=Mk: e.tensor_tensor(
                        out=R2[rr], in0=bc(DTA[:, d * 32 + 4 * g:d * 32 + 4 * g + 4], [[1, 4], [0, 128]]),
                        in1=bc(Mk, [[0, 4], [1, 128]]), op=ALU.mult), reads=[bDTA], writes=[bR2[rr]])

                def emit_D(n):
                    d = n % 2
                    Uf = UA_f if d == 0 else UB_f
                    rr = n % 3
                    pb = 4 + n % 2
                    P.op('pe', lambda e, rr=rr, Uf=Uf, pb=pb: e.matmul(PS[pb], lhsT=Uf, rhs=R2[rr].rearrange("p h t -> p (h t)"), start=True, stop=True),
                         reads=[bR2[rr]], writes=[bPS[pb]])
                    P.op('act', lambda e, pb=pb, n=n: e.activation(out=E[n % 2].rearrange("p h t -> p (h t)"), in_=PS[pb], func=AF.Exp),
                         reads=[bPS[pb]], writes=[bE[n % 2]])

                def emit_M(n):
                    g, d = n // 2, n % 2
                    r = g % 2
                    P.op('dve', lambda e, n=n, r=r, d=d, g=g: e.tensor_tensor(
                        out=MP[r][d], in0=E[n % 2], in1=bc(CBTm[:, d, g, :], [[0, 4], [1, 128]]), op=ALU.mult),
                        reads=[bE[n % 2], bCBT], writes=[bMP[r][d]])

                def emit_y(g):
                    r = g % 2
                    for pe_ in range(2):
                        pp = 2 * g + pe_
                        bank = pp // 4
                        col = (pp % 4) * 128

                        def mmy(e, pp=pp, bank=bank, col=col, r=r, g=g, i=i):
                            o = PS[bank][:, col:col + 128]
                            e.matmul(o, lhsT=DSK[:, pp, :], rhs=XB[i][:, pp, :], start=True, stop=False)
                            ins = None
                            for ee in range(2):
                                h = 2 * pp + ee
                                hh = h - 4 * g
                                e.matmul(o, lhsT=XDA[:, h, :], rhs=MP[r][0][:, hh, :], start=False, stop=False)
                                ins = e.matmul(o, lhsT=XDB[:, h, :], rhs=MP[r][1][:, hh, :], start=False, stop=(ee == 1))
                            return ins
                        P.op('pe', mmy, reads=[bDSK, bXB[i], bXD, bMP[r][0], bMP[r][1]], writes=[bPS[bank]])

                emit_R2(0)
                emit_R2(1)

                def mm_small(e):
                    e.matmul(PS[3][:, 0:32], lhsT=UA_f, rhs=DTA[:, 0:32], start=True, stop=True)
                    e.matmul(PS[3][:, 32:64], lhsT=UB_f, rhs=DTA[:, 32:64], start=True, stop=True)
                    return e.matmul(PS[3][:, 64:128], lhsT=ONES, rhs=DTA[:, 0:64], start=True, stop=True)
                P.op('pe', mm_small, reads=[bDTA], writes=[bPS[3]])
                P.op('act', lambda e: e.activation(out=EXPO, in_=PS[3][:, 0:128], func=AF.Exp), reads=[bPS[3]], writes=[bEXPO])
                for b2 in range(2):
                    def tr(e, b2=b2, i=i):
                        ins = None
                        for q in range(8):
                            ins = e.transpose(PSb[b2][:, q * 128:(q + 1) * 128], XB[i][:, b2 * 8 + q, :], ident_b)
                        return ins
                    P.op('pe', tr, reads=[bXB[i]], writes=[bPS[b2]])

                def trb(e, i=i):
                    ins = None
                    for q in range(8):
                        ins = e.transpose(PSb[2][:, q * 128:(q + 1) * 128], XB[i][:, 16 + q, :], ident_b)
                    return ins
                P.op('pe', trb, reads=[bXB[i]], writes=[bPS[2]])
                for b2 in range(2):
                    def mmc(e, b2=b2, i=i):
                        ins = None
                        for gq in range(4):
                            g = b2 * 4 + gq
                            ins = e.matmul(PS[4 + b2][:, gq * 128:(gq + 1) * 128], lhsT=XB[i][:, 16 + g, :], rhs=XB[i][:, 24 + g, :],
                                           start=True, stop=True)
                        return ins
                    P.op('pe', mmc, reads=[bXB[i]], writes=[bPS[4 + b2]])
                for rd in range(2):
                    for b2 in range(2):
                        b4 = rd * 2 + b2

                        def mme(e, b4=b4, b2=b2):
                            ins = None
                            for q in range(4):
                                pp = b4 * 4 + q
                                e.matmul(PS[6 + b2][:, q * 128:(q + 1) * 128], lhsT=DBH[:, 0, pp * 128:(pp + 1) * 128], rhs=CMB[:, 3, :],
                                         start=True, stop=False)
                                ins = e.matmul(PS[6 + b2][:, q * 128:(q + 1) * 128], lhsT=DBH[:, 1, pp * 128:(pp + 1) * 128], rhs=CMB[:, 3, :],
                                               start=False, stop=True)
                            return ins
                        P.op('pe', mme, reads=[bDBC], writes=[bPS[6 + b2]])
                        P.op('act', lambda e, b4=b4, b2=b2: e.activation(out=EA[:, b4 * 512:(b4 + 1) * 512], in_=PS[6 + b2], func=AF.Exp),
                             reads=[bPS[6 + b2]], writes=[bEA])
                    if rd == 0:
                        for b2 in range(2):
                            for d, Mk in ((0, TLE_f), (1, TGE_f)):
                                P.op('dve', lambda e, b2=b2, d=d, Mk=Mk: e.tensor_tensor(
                                    out=CBTm[:, d, b2 * 4:(b2 + 1) * 4, :], in0=PS[4 + b2].rearrange("p (g t) -> p g t", t=128),
                                    in1=bc(Mk, [[0, 4], [1, 128]]), op=ALU.mult), reads=[bPS[4 + b2]], writes=[bCBT])
                        emit_D(0)
                        emit_D(1)
                P.op('dve', lambda e, c=c: e.tensor_tensor(out=FAC, in0=DTT[:, c, :], in1=EXPO[:, 0:64], op=ALU.mult), reads=[bEXPO], writes=[bFAC])
                P.op('dve', lambda e, c=c: e.tensor_copy(out=DECB[:, c, :], in_=EXPO[:, 96:128]), reads=[bEXPO], writes=[bDECB])
                for b2 in range(2):
                    src3 = bass.AP(PSb[b2].tensor, PSb[b2].offset, [list(PSb[b2].ap[0]), [128, 8], [64, 2], [1, 64]])
                    for d, X in ((0, XDA), (1, XDB)):
                        fac = bass.AP(DTT.tensor, DTT[:, c, :].offset + d * 32 + 16 * b2, [list(DTT.ap[0]), [2, 8], [1, 2], [0, 64]])
                        P.op('dve', lambda e, X=X, b2=b2, src3=src3, fac=fac: e.tensor_tensor(out=pad_ap(X, 16 * b2, 16), in0=src3, in1=fac, op=ALU.mult),
                             reads=[bPS[b2]], writes=[bXD])
                    for d, X in ((0, XWA), (1, XWB)):
                        P.op('dve', lambda e, X=X, b2=b2, d=d: e.tensor_tensor(
                            out=X[:, b2 * 1024:(b2 + 1) * 1024].rearrange("p (h q) -> p h q", q=64),
                            in0=PSb[b2].rearrange("p (h q) -> p h q", q=64),
                            in1=bc(FAC[:, d * 32 + 16 * b2:d * 32 + 16 * b2 + 16], [[1, 16], [0, 64]]), op=ALU.mult),
                            reads=[bPS[b2], bFAC], writes=[bXW])
                P.op('act', lambda e: e.activation(out=BT, in_=PSb[2], func=AF.Copy), reads=[bPS[2]], writes=[bBT])
                for n in range(2, 18):
                    emit_M(n - 2)
                    if n < 16:
                        emit_R2(n)
                        emit_D(n)
                    if (n - 2) % 2 == 1:
                        emit_y((n - 2) // 2)
                for hb in range(2):
                    for b2 in range(2):
                        def mmi(e, hb=hb, b2=b2, i=i):
                            ins = None
                            for q in range(4):
                                pp = hb * 8 + b2 * 4 + q
                                ins = e.matmul(PS[6 + b2][:, q * 128:(q + 1) * 128], lhsT=SAb[:, pp * 128:(pp + 1) * 128],
                                               rhs=XB[i][:, 24 + pp // 2, :], start=True, stop=True)
                            return ins
                        P.op('pe', mmi, reads=[bSAb, bXB[i]], writes=[bPS[6 + b2]])
                        cs = (hb * 2 + b2) * 512
                        P.op('dve', lambda e, b2=b2, cs=cs: e.tensor_tensor(out=T1[:, cs:cs + 512], in0=PS[6 + b2], in1=EA[:, cs:cs + 512], op=ALU.mult),
                             reads=[bPS[6 + b2], bEA], writes=[bT1])
                        yb = hb * 2 + b2
                        P.op('dve', lambda e, yb=yb, cs=cs, yi=yi: e.tensor_tensor(out=YP[yi][:, cs:cs + 512], in0=PS[yb], in1=T1[:, cs:cs + 512], op=ALU.add),
                             reads=[bPS[yb], bT1], writes=[bYP[yi]])
                P.dma('sp', f'st{yi}', lambda e, yi=yi, c=c: e.dma_start(out=yd_v[:, :, c * 128:(c + 1) * 128], in_=YP[yi].rearrange("p (b t) -> p b t", t=128)),
                      reads=[bYP[yi]])
                for b4 in range(4):
                    def mms(e, b4=b4):
                        ins = None
                        for q in range(2):
                            g = b4 * 2 + q
                            ins = e.matmul(PS[b4][:, q * 256:(q + 1) * 256], lhsT=BT[:, g * 128:(g + 1) * 128], rhs=XWA[:, g * 256:(g + 1) * 256],
                                           start=True, stop=True)
                        return ins
                    P.op('pe', mms, reads=[bBT, bXW], writes=[bPS[b4]])
                P.op('dve', lambda e: e.tensor_tensor(out=SA.rearrange("p (h q) -> p h q", q=64), in0=SA.rearrange("p (h q) -> p h q", q=64),
                                                      in1=bc(EXPO[:, 64:96], [[1, 32], [0, 64]]), op=ALU.mult), reads=[bEXPO, bSA], writes=[bSA])
                for b4 in range(4):
                    def mmsb(e, b4=b4):
                        ins = None
                        for q in range(2):
                            g = b4 * 2 + q
                            ins = e.matmul(PS[4 + b4][:, q * 256:(q + 1) * 256], lhsT=BT[:, g * 128:(g + 1) * 128], rhs=XWB[:, g * 256:(g + 1) * 256],
                                           start=True, stop=True)
                        return ins
                    P.op('pe', mmsb, reads=[bBT, bXW], writes=[bPS[4 + b4]])
                for b4 in range(4):
                    P.op('dve', lambda e, b4=b4: e.tensor_tensor(out=SA[:, b4 * 512:(b4 + 1) * 512], in0=SA[:, b4 * 512:(b4 + 1) * 512], in1=PS[b4], op=ALU.add),
                         reads=[bPS[b4], bSA], writes=[bSA])
                    P.op('act', lambda e, b4=b4, yi=yi: e.activation(out=SBL[yi][:, b4 * 512:(b4 + 1) * 512], in_=PS[4 + b4], func=AF.Copy),
                         reads=[bPS[4 + b4]], writes=[bSBL[yi]])
                P.op('act', lambda e: e.activation(out=SAb, in_=SA, func=AF.Copy), reads=[bSA], writes=[bSAb])
                P.dma('sp', f'st{2 + yi}', lambda e, yi=yi, c=c: e.dma_start(out=SB_d[c], in_=SBL[yi]), reads=[bSBL[yi]])
                if debug and l == 0 and c == 1:
                    P.dma('sp', 'st4', lambda e: e.dma_start(out=DBG["sa"], in_=SA), reads=[bSA])


        def sweepB(l, es, last):
            CT = [sb(f"bCT{i}", [128, 8, 128], BF16, es) for i in range(2)]
            YPl = [sb(f"bYP{i}", [128, DIN], F32, es) for i in range(2)]
            SBLl = [sb(f"bSBL{i}", [128, DIN], F32, es) for i in range(2)]
            ZSl = [sb(f"bZS{i}", [128, 16, 128], BF16, es) for i in range(2)]
            YNs = [sb(f"bYN{i}", [128, 16, 128], BF16, es) for i in range(2)]
            SBr = sb("bSBr", [128, DIN], F32, es)
            SBb = sb("bSBb", [128, DIN], BF16, es)
            CC2 = sb("bCC2", [128, 2, DIN], F32, es)
            AN = sb("bAN", [128, 32], F32, es)
            DTA = sb("bDTA", [128, 32], F32, es)
            DBH = sb("bDBH", [128, 2, DIN], BF16, es)
            DTH = sb("bDTH", [128, 2, 32], BF16, es)
            bDTH = B_()
            EA = sb("bEA", [128, DIN], F32, es)
            T1 = sb("bT1", [128, DIN], F32, es)
            YZ2 = [sb(f"bYZ{i}", [128, DIN], F32, es) for i in range(2)]
            SQ = sb("bSQ", [128, DIN], BF16, es)
            RST2 = [sb(f"bRST{i}", [128, 8, 128], F32, es) for i in range(2)]
            bYZ2, bRST2 = [B_(), B_()], [B_(), B_()]
            NW = sb("bNW", [128, 16], F32, es)
            bCT, bYPl, bSBLl, bZSl, bYNs = [B_(), B_()], [B_(), B_()], [B_(), B_()], [B_(), B_()], [B_(), B_()]
            bSBr, bSBb, bCC2, bAN, bDTA, bDBC, bEA, bT1, bYZ_, bSQ, bRST_, bNW = [B_() for _ in range(12)]
            bCCI, bCCO = B_(), B_()
            bPS = [B_() for _ in range(8)]
            P.dma('sp', 'st0', lambda e: e.dma_start(out=CCI_d, in_=SA), writes=[bCCI])
            P.dma('pool', 'cc', lambda e: e.collective_compute("AllGather", ALU.bypass, replica_groups=pairs,
                                                               ins=[CCI_d.opt()], outs=[CCO_d.opt()]),
                  reads=[bCCI], writes=[bCCO], inc=1)
            P.dma('sp', 'ld0', lambda e: e.dma_start(out=AN, in_=a_log[l][:, 32:64]), writes=[bAN])
            P.op('act', lambda e: e.activation(out=AN, in_=AN, func=AF.Exp), reads=[bAN], writes=[bAN])
            P.op('dve', lambda e: e.tensor_scalar_mul(out=AN, in0=AN, scalar1=-1.0), reads=[bAN], writes=[bAN])
            P.dma('sp', 'ld1', lambda e: e.dma_start(out=NW, in_=norm_w[l]), writes=[bNW])
            P.op('dve', lambda e: e.memset(SBr, 0.0), writes=[bSBr])
            P.op('dve', lambda e: e.memset(SBb, 0.0), writes=[bSBb])
            order = ([] if last else [1, 0]) + list(range(NCH - 1, 1, -1))
            EA2 = [EA, sb("bEA2", [128, DIN], F32, es)]
            bEA2 = [B_(), B_()]

            def emit_loads(it, c):
                i = it % 2
                co = chunk_col(c)
                P.dma('sp', f'ld{3 + i}', lambda e, i=i, co=co: e.dma_start(out=CT[i], in_=xbc_v[:, 24:32, co:co + 128]), writes=[bCT[i]])
                P.dma('sp', f'ld{5 + i}', lambda e, i=i, c=c: e.dma_start(out=YPl[i].rearrange("p (b t) -> p b t", t=128), in_=yd_v[:, :, c * 128:(c + 1) * 128]),
                      writes=[bYPl[i]])
                P.dma('sp', f'ld{7 + i}', lambda e, i=i, c=c: e.dma_start(out=SBLl[i], in_=SB_d[c]), writes=[bSBLl[i]])
                P.dma('sp', f'ld{9 + i}', lambda e, i=i, c=c: e.dma_start(out=ZSl[i], in_=zs_v[:, :, c * 128:(c + 1) * 128]), writes=[bZSl[i]])

            def emit_EA(it, c):
                sl = it % 2
                P.op('dve', lambda e, c=c: e.tensor_tensor(out=DTA, in0=DTT[:, c, 32:64], in1=AN, op=ALU.mult), reads=[bAN], writes=[bDTA])
                P.op('dve', lambda e: e.tensor_copy(out=DTH[:, 0, :], in_=DTA), reads=[bDTA], writes=[bDTH])
                P.op('dve', lambda e: e.tensor_tensor(out=DTH[:, 1, :], in0=DTA, in1=DTH[:, 0, :], op=ALU.subtract), reads=[bDTA, bDTH], writes=[bDTH])
                for hl in range(2):
                    P.op('dve', lambda e, hl=hl: e.tensor_copy(out=DBH[:, hl, :].rearrange("p (h q) -> p h q", q=64), in_=bc(DTH[:, hl, :], [[1, 32], [0, 64]])),
                         reads=[bDTH], writes=[bDBC])
                for b4 in range(4):
                    def mme(e, b4=b4):
                        ins = None
                        for q in range(4):
                            pp = b4 * 4 + q
                            e.matmul(PS[b4][:, q * 128:(q + 1) * 128], lhsT=DBH[:, 0, pp * 128:(pp + 1) * 128], rhs=CMB[:, 4, :],
                                     start=True, stop=False)
                            ins = e.matmul(PS[b4][:, q * 128:(q + 1) * 128], lhsT=DBH[:, 1, pp * 128:(pp + 1) * 128], rhs=CMB[:, 4, :],
                                           start=False, stop=True)
                        return ins
                    P.op('pe', mme, reads=[bDBC], writes=[bPS[b4]])
                    P.op('act', lambda e, b4=b4, sl=sl: e.activation(out=EA2[sl][:, b4 * 512:(b4 + 1) * 512], in_=PS[b4], func=AF.Exp),
                         reads=[bPS[b4]], writes=[bEA2[sl]])

            emit_loads(0, order[0])
            emit_EA(0, order[0])
            for it, c in enumerate(order):
                i = it % 2
                sl = it % 2
                if c == NCH - 1:
                    P.dma('sp', 'ld2', lambda e: e.dma_start(out=CC2, in_=CCO_d.rearrange("(r p) n -> p r n", p=128)),
                          reads=[bCCO], writes=[bCC2])
                    P.op('dve', lambda e: e.tensor_scalar(out=SBr, in0=CC2[:, 0, :], scalar1=SELM[:, 0:1], scalar2=None, op0=ALU.mult),
                         reads=[bCC2], writes=[bSBr])
                    P.op('dve', lambda e: e.scalar_tensor_tensor(out=SBr, in0=CC2[:, 1, :], scalar=SELM[:, 1:2], in1=SBr,
                                                                 op0=ALU.mult, op1=ALU.add), reads=[bCC2, bSBr], writes=[bSBr])
                    P.op('act', lambda e: e.activation(out=SBb, in_=SBr, func=AF.Copy), reads=[bSBr], writes=[bSBb])
                if it + 1 < len(order):
                    emit_loads(it + 1, order[it + 1])
                for b4 in range(4):
                    def mmi(e, b4=b4, i=i):
                        ins = None
                        for q in range(4):
                            pp = b4 * 4 + q
                            ins = e.matmul(PS[4 + b4][:, q * 128:(q + 1) * 128], lhsT=SBb[:, pp * 128:(pp + 1) * 128],
                                           rhs=CT[i][:, pp // 2, :], start=True, stop=True)
                        return ins
                    P.op('pe', mmi, reads=[bSBb, bCT[i]], writes=[bPS[4 + b4]])
                    P.op('dve', lambda e, b4=b4, sl=sl: e.tensor_tensor(out=T1[:, b4 * 512:(b4 + 1) * 512], in0=PS[4 + b4], in1=EA2[sl][:, b4 * 512:(b4 + 1) * 512], op=ALU.mult),
                         reads=[bPS[4 + b4], bEA2[sl]], writes=[bT1])
                P.op('dve', lambda e, i=i: e.tensor_tensor(out=T1, in0=T1, in1=YPl[i], op=ALU.add), reads=[bT1, bYPl[i]], writes=[bT1])
                if debug and l == 0:
                    P.dma('sp', 'st5', lambda e, c=c: e.dma_start(out=DBG["yt"].rearrange("(blk p) t -> p blk t", p=128)[:, :, c * 128:(c + 1) * 128],
                                                                 in_=T1.rearrange("p (b t) -> p b t", t=128)), reads=[bT1])
                P.op('dve', lambda e, c=c: e.tensor_tensor(out=SBr.rearrange("p (h q) -> p h q", q=64), in0=SBr.rearrange("p (h q) -> p h q", q=64),
                                                           in1=bc(DECB[:, c, :], [[1, 32], [0, 64]]), op=ALU.mult), reads=[bSBr], writes=[bSBr])
                P.op('dve', lambda e, i=i: e.tensor_tensor(out=SBr, in0=SBr, in1=SBLl[i], op=ALU.add), reads=[bSBr, bSBLl[i]], writes=[bSBr])
                P.op('act', lambda e: e.activation(out=SBb, in_=SBr, func=AF.Copy), reads=[bSBr], writes=[bSBb])
                if it + 1 < len(order) and order[it + 1] != NCH - 1:
                    emit_EA(it + 1, order[it + 1])
                YZ, RST, bYZ, bRST = YZ2[i], RST2[i], bYZ2[i], bRST2[i]
                P.op('dve', lambda e, i=i, YZ=YZ: e.tensor_tensor(out=YZ, in0=T1, in1=ZSl[i].rearrange("p b t -> p (b t)"), op=ALU.mult),
                     reads=[bT1, bZSl[i]], writes=[bYZ])
                P.op('act', lambda e, YZ=YZ: e.activation(out=bass.AP(SQ.tensor, SQ.offset, [list(SQ.ap[0]), [128, 8], [1024, 2], [1, 128]]),
                                                          in_=YZ.rearrange("p (g r t) -> p g r t", r=2, t=128), func=AF.Square), reads=[bYZ], writes=[bSQ])
                for b2 in range(2):
                    def mmn(e, b2=b2):
                        e.matmul(PS[4 + b2], lhsT=ONESB, rhs=SQ[:, b2 * 512:(b2 + 1) * 512], start=True, stop=False)
                        return e.matmul(PS[4 + b2], lhsT=ONESB, rhs=SQ[:, 1024 + b2 * 512:1024 + (b2 + 1) * 512], start=False, stop=True)
                    P.op('pe', mmn, reads=[bSQ], writes=[bPS[4 + b2]])
                    P.op('act', lambda e, b2=b2, RST=RST: e.activation(out=RST[:, b2 * 4:(b2 + 1) * 4, :].rearrange("p g t -> p (g t)"), in_=PS[4 + b2],
                                                                       func=AF.Ln, scale=1.0 / 256, bias=EPS), reads=[bPS[4 + b2]], writes=[bRST])
                P.op('act', lambda e, RST=RST: e.activation(out=RST, in_=RST, func=AF.Exp, scale=-0.5), reads=[bRST], writes=[bRST])
                P.op('pool', lambda e, i=i, YZ=YZ, RST=RST: e.tensor_tensor(out=YNs[i].rearrange("p (g r) t -> p g r t", r=2),
                                                                            in0=YZ.rearrange("p (g r t) -> p g r t", r=2, t=128),
                                                                            in1=bc(RST, [[128, 8], [0, 2], [1, 128]]), op=ALU.mult),
                     reads=[bYZ, bRST], writes=[bYNs[i]])
                if it + 1 < len(order) and order[it + 1] == NCH - 1:
                    emit_EA(it + 1, order[it + 1])
                P.dma('sp', f'st{1 + i}', lambda e, i=i, c=c: e.dma_start(out=ynd_v[:, :, c * 128:(c + 1) * 128], in_=YNs[i]), reads=[bYNs[i]])


        gs_v = GS_d.rearrange("(blk p) t -> p blk t", p=128)
        pm2_v = PM2_d.rearrange("(blk p) t -> p blk t", p=128)
        gd_v = G_d.rearrange("(blk p) t -> p blk t", p=128)

        def load_weight(dst, src_v, nk, ncols, keybase, bdst):
            n = 0
            for k0 in range(0, nk, 8):
                k1 = min(nk, k0 + 8)
                for c0 in range(0, ncols, 512):
                    c1 = min(ncols, c0 + 512)
                    P.dma('pool', f'w{keybase + n % 4}', lambda e, k0=k0, k1=k1, c0=c0, c1=c1: e.dma_start(
                        out=dst[:, k0:k1, c0:c1], in_=src_v[:, k0:k1, c0:c1]), writes=[bdst])
                    n += 1

        def mixer_dense(l, es, last):
            WSO = sb("dWSO", [128, 16, D], BF16, es)
            WPO = sb("dWPO", [128, 8, D], BF16, es)
            WO = sb("dWO", [128, 8, D], BF16, es)
            YNt = [sb(f"dYN{i}", [128, 16, 512], BF16, es) for i in range(2)]
            PMt = [sb(f"dPM{i}", [128, 8, 512], BF16, es) for i in range(2)]
            GSt = sb("dGS", [128, 16, 512], BF16, es)
            Xt = sb("dX", [128, KD, 512], F32, es)
            MT = sb("dMT", [128, KD, 512], BF16, es)
            Ta = [sb(f"dTa{i}", [128, 512], F32, es) for i in range(2)]
            Tb = [sb(f"dTb{i}", [128, 512], F32, es) for i in range(2)]
            bWSO, bWPO, bWO, bGS, bX, bMT = B_(), B_(), B_(), B_(), B_(), B_()
            bYN, bPM, bTa, bTb = [B_(), B_()], [B_(), B_()], [B_(), B_()], [B_(), B_()]
            bPS = [B_() for _ in range(8)]
            load_weight(WSO, w_ssd_out[l].rearrange("(k p) n -> p k n", p=128), 16, D, 0, bWSO)
            NWm = sb("dNW", [128, 16], F32, es)
            bNWm = B_()
            P.dma('sp', 'ld6', lambda e: e.dma_start(out=NWm, in_=norm_w[l]), writes=[bNWm])
            for kb in range(16):
                P.op('dve', lambda e, kb=kb: e.tensor_scalar(out=WSO[:, kb, :], in0=WSO[:, kb, :], scalar1=NWm[:, kb:kb + 1], scalar2=None, op0=ALU.mult),
                     reads=[bWSO, bNWm], writes=[bWSO])
            load_weight(WPO, w_pool_out[l].rearrange("(k p) n -> p k n", p=128), 8, D, 0, bWPO)
            load_weight(WO, w_out[l].rearrange("(k p) n -> p k n", p=128), 8, D, 0, bWO)
            tiles = ([] if last else [(0, 256, 1)]) + [(256 + 512 * i, 512, 0) for i in range(4)]
            np_ = 0
            for it, (t0, w, mc) in enumerate(tiles):
                i = it % 2
                P.dma('sp', f'ld{i}', lambda e, i=i, t0=t0, w=w: e.dma_start(out=YNt[i][:, :, :w], in_=ynd_v[:, :, t0:t0 + w]), writes=[bYN[i]])
                P.dma('sp', f'ld{2 + i}', lambda e, i=i, t0=t0, w=w: e.dma_start(out=PMt[i][:, :, :w], in_=pm2_v[:, :, t0:t0 + w]), writes=[bPM[i]])
                P.dma('sp', 'ld4', lambda e, t0=t0, w=w: e.dma_start(out=GSt[:, :, :w], in_=gs_v[:, :, t0:t0 + w]), writes=[bGS])
                P.dma('sp', 'ld5', lambda e, t0=t0, w=w: e.dma_start(out=Xt[:, :, :w], in_=xtd_v[:, :, t0:t0 + w]), writes=[bX])
                for db in range(KD):
                    pa = np_ % 4
                    pb = 4 + np_ % 4
                    r = np_ % 2
                    np_ += 1

                    def mma(e, db=db, i=i, w=w, pa=pa):
                        ins = None
                        for kb in range(16):
                            ins = e.matmul(PS[pa][:, :w], lhsT=WSO[:, kb, db * 128:(db + 1) * 128], rhs=YNt[i][:, kb, :w],
                                           start=(kb == 0), stop=(kb == 15))
                        return ins
                    P.op('pe', mma, reads=[bWSO, bYN[i]], writes=[bPS[pa]])

                    def mmb(e, db=db, i=i, w=w, pb=pb):
                        ins = None
                        for kb in range(8):
                            ins = e.matmul(PS[pb][:, :w], lhsT=WPO[:, kb, db * 128:(db + 1) * 128], rhs=PMt[i][:, kb, :w],
                                           start=(kb == 0), stop=(kb == 7))
                        return ins
                    P.op('pe', mmb, reads=[bWPO, bPM[i]], writes=[bPS[pb]])
                    P.op('dve', lambda e, r=r, db=db, w=w, pa=pa: e.tensor_tensor(out=Ta[r][:, :w], in0=PS[pa][:, :w], in1=GSt[:, db, :w], op=ALU.mult),
                         reads=[bPS[pa], bGS], writes=[bTa[r]])
                    P.op('dve', lambda e, r=r, db=db, w=w, pb=pb: e.tensor_tensor(out=Tb[r][:, :w], in0=PS[pb][:, :w], in1=GSt[:, 8 + db, :w], op=ALU.mult),
                         reads=[bPS[pb], bGS], writes=[bTb[r]])
                    P.op('pool', lambda e, r=r, db=db, w=w: e.tensor_tensor(out=MT[:, db, :w], in0=Ta[r][:, :w], in1=Tb[r][:, :w], op=ALU.add),
                         reads=[bTa[r], bTb[r]], writes=[bMT])
                for db in range(KD):
                    pa = np_ % 4
                    np_ += 1

                    def mmo(e, db=db, w=w, pa=pa):
                        ins = None
                        for kb in range(8):
                            ins = e.matmul(PS[pa][:, :w], lhsT=WO[:, kb, db * 128:(db + 1) * 128], rhs=MT[:, kb, :w],
                                           start=(kb == 0), stop=(kb == 7))
                        return ins
                    P.op('pe', mmo, reads=[bWO, bMT], writes=[bPS[pa]])
                    P.op('dve', lambda e, db=db, w=w, pa=pa, mc=mc: e.scalar_tensor_tensor(
                        out=Xt[:, db, :w], in0=PS[pa][:, :w], scalar=MOD[l][:, 16 + db, mc:mc + 1], in1=Xt[:, db, :w],
                        op0=ALU.mult, op1=ALU.add), reads=[bPS[pa], bX], writes=[bX])
                P.dma('sp', 'st0', lambda e, t0=t0, w=w: e.dma_start(out=xtd_v[:, :, t0:t0 + w], in_=Xt[:, :, :w]), reads=[bX])

        def ffn_up(l, HT, es, last):
            Wa = [sb(f"fWa{i}", [128, KD, 512], BF16, es) for i in range(2)]
            Wb = [sb(f"fWb{i}", [128, KD, 512], BF16, es) for i in range(2)]
            SI = [sb(f"fSI{i}", [128, 512], F32, es) for i in range(2)]
            GST = [sb(f"fG{i}", [128, TT], BF16, es) for i in range(2)]
            bWa, bWb, bSI, bGST = [B_(), B_()], [B_(), B_()], [B_(), B_()], [B_(), B_()]
            bPS = [B_() for _ in range(8)]
            bHT = B_()
            wv = w_gate_up[l].rearrange("(k p) n -> p k n", p=128)
            tiles = ([] if last else [(0, 256)]) + [(256 + 512 * i, 512) for i in range(4)]
            np_ = 0
            ng = 0
            for jb in range(6):
                j = jb % 2
                ncol = 512 if jb < 5 else 256
                P.dma('pool', f'w{j}', lambda e, j=j, jb=jb, ncol=ncol: e.dma_start(out=Wa[j][:, :, :ncol], in_=wv[:, :, jb * 512:jb * 512 + ncol]),
                      writes=[bWa[j]])
                P.dma('pool', f'w{2 + j}', lambda e, j=j, jb=jb, ncol=ncol: e.dma_start(out=Wb[j][:, :, :ncol], in_=wv[:, :, FFH + jb * 512:FFH + jb * 512 + ncol]),
                      writes=[bWb[j]])
                for q in range(ncol // 128):
                    hb = jb * 4 + q
                    gi = ng % 2
                    ng += 1
                    for (t0, w) in tiles:
                        pa = np_ % 4
                        pb = 4 + np_ % 4
                        r = np_ % 2
                        np_ += 1

                        def mma(e, j=j, q=q, t0=t0, w=w, pa=pa):
                            ins = None
                            for k in range(KD):
                                ins = e.matmul(PS[pa][:, :w], lhsT=Wa[j][:, k, q * 128:(q + 1) * 128], rhs=HT[:, k, t0:t0 + w],
                                               start=(k == 0), stop=(k == KD - 1))
                            return ins
                        P.op('pe', mma, reads=[bWa[j], bHT], writes=[bPS[pa]])

                        def mmb(e, j=j, q=q, t0=t0, w=w, pb=pb):
                            ins = None
                            for k in range(KD):
                                ins = e.matmul(PS[pb][:, :w], lhsT=Wb[j][:, k, q * 128:(q + 1) * 128], rhs=HT[:, k, t0:t0 + w],
                                               start=(k == 0), stop=(k == KD - 1))
                            return ins
                        P.op('pe', mmb, reads=[bWb[j], bHT], writes=[bPS[pb]])
                        P.op('act', lambda e, r=r, w=w, pa=pa: e.activation(out=SI[r][:, :w], in_=PS[pa][:, :w], func=AF.Silu),
                             reads=[bPS[pa]], writes=[bSI[r]])
                        P.op('dve', lambda e, r=r, w=w, pb=pb, gi=gi, t0=t0: e.tensor_tensor(out=GST[gi][:, t0:t0 + w], in0=PS[pb][:, :w], in1=SI[r][:, :w], op=ALU.mult),
                             reads=[bPS[pb], bSI[r]], writes=[bGST[gi]])
                    clo = NCTX if last else 0
                    P.dma('sp', f'st{gi}', lambda e, gi=gi, hb=hb, clo=clo: e.dma_start(out=G_d[hb * 128:(hb + 1) * 128, clo:TT], in_=GST[gi][:, clo:TT]),
                          reads=[bGST[gi]])

        def ffn_down(l, es, last):
            WD = sb("gWD", [128, 22, D], BF16, es)
            Gt = [sb(f"gG{i}", [128, 22, 512], BF16, es) for i in range(2)]
            Xt = [sb(f"gX{i}", [128, KD, 512], F32, es) for i in range(2)]
            SQ = sb("gSQ", [128, KD, 512], F32, es)
            RS = sb("gRS", [128, 512], F32, es)
            GF = sb("gGF", [128, KD], F32, es)
            HX = sb("gHX", [128, KD, 2], F32, es)
            HX2 = sb("gHX2", [128, 2, 16], F32, es)
            bWD, bSQ, bRS, bGF, bHX, bHX2, bHCI, bHCO = B_(), B_(), B_(), B_(), B_(), B_(), B_(), B_()
            bG, bX = [B_(), B_()], [B_(), B_()]
            bPS = [B_() for _ in range(8)]
            load_weight(WD, w_down[l].rearrange("(k p) n -> p k n", p=128), 22, D, 0, bWD)
            if last:
                P.dma('sp', 'ld6', lambda e: e.dma_start(out=GF, in_=g_final), writes=[bGF])
            tiles = ([] if last else [(0, 256, 1)]) + [(256 + 512 * i, 512, 0) for i in range(4)]
            np_ = 0
            for it, (t0, w, mc) in enumerate(tiles):
                i = it % 2
                P.dma('sp', f'ld{i}', lambda e, i=i, t0=t0, w=w: e.dma_start(out=Gt[i][:, :, :w], in_=gd_v[:, :, t0:t0 + w]), writes=[bG[i]])
                P.dma('sp', f'ld{2 + i}', lambda e, i=i, t0=t0, w=w: e.dma_start(out=Xt[i][:, :, :w], in_=xtd_v[:, :, t0:t0 + w]), writes=[bX[i]])
                for db in range(KD):
                    pa = np_ % 4
                    np_ += 1

                    def mmd(e, db=db, i=i, w=w, pa=pa):
                        ins = None
                        for kb in range(22):
                            ins = e.matmul(PS[pa][:, :w], lhsT=WD[:, kb, db * 128:(db + 1) * 128], rhs=Gt[i][:, kb, :w],
                                           start=(kb == 0), stop=(kb == 21))
                        return ins
                    P.op('pe', mmd, reads=[bWD, bG[i]], writes=[bPS[pa]])
                    P.op('dve', lambda e, db=db, i=i, w=w, pa=pa, mc=mc: e.scalar_tensor_tensor(
                        out=Xt[i][:, db, :w], in0=PS[pa][:, :w], scalar=MOD[l][:, 40 + db, mc:mc + 1], in1=Xt[i][:, db, :w],
                        op0=ALU.mult, op1=ALU.add), reads=[bPS[pa], bX[i]], writes=[bX[i]])
                if not last:
                    P.dma('sp', f'st{i}', lambda e, i=i, t0=t0, w=w: e.dma_start(out=xtd_v[:, :, t0:t0 + w], in_=Xt[i][:, :, :w]), reads=[bX[i]])
                    if t0 + w == TT:
                        for hh in range(2):
                            P.op('dve', lambda e, i=i, hh=hh: e.tensor_copy(out=HX[:, :, hh], in_=Xt[i][:, :, 511 - hh]), reads=[bX[i]], writes=[bHX])
                        P.dma('sp', 'st2', lambda e: e.dma_start(out=HCI_d, in_=HX.rearrange("p k t -> p (k t)")), reads=[bHX], writes=[bHCI])
                        P.dma('pool', 'cc', lambda e: e.collective_compute("AllGather", ALU.bypass, replica_groups=pairs,
                                                                           ins=[HCI_d.opt()], outs=[HCO_d.opt()]),
                              reads=[bHCI], writes=[bHCO], inc=1)
                        P.dma('sp', 'ld4', lambda e: e.dma_start(out=HX2, in_=HCO_d.rearrange("(r p) n -> p r n", p=128)), reads=[bHCO], writes=[bHX2])
                        P.op('dve', lambda e: e.tensor_scalar(out=HX.rearrange("p k t -> p (k t)"), in0=HX2[:, 0, :], scalar1=SELM[:, 0:1], scalar2=None, op0=ALU.mult),
                             reads=[bHX2], writes=[bHX])
                        P.op('dve', lambda e: e.scalar_tensor_tensor(out=HX.rearrange("p k t -> p (k t)"), in0=HX2[:, 1, :], scalar=SELM[:, 1:2],
                                                                     in1=HX.rearrange("p k t -> p (k t)"), op0=ALU.mult, op1=ALU.add),
                             reads=[bHX2, bHX], writes=[bHX])
                        P.dma('sp', 'st3', lambda e: e.dma_start(out=xtd_v[:, :, TT:TT + 2], in_=HX), reads=[bHX])
                else:
                    P.op('act', lambda e, i=i: e.activation(out=SQ, in_=Xt[i], func=AF.Square), reads=[bX[i]], writes=[bSQ])

                    def mmq(e):
                        ins = None
                        for k in range(KD):
                            ins = e.matmul(PS[4], lhsT=ONES, rhs=SQ[:, k, :], start=(k == 0), stop=(k == KD - 1))
                        return ins
                    P.op('pe', mmq, reads=[bSQ], writes=[bPS[4]])
                    P.op('act', lambda e: e.activation(out=RS, in_=PS[4], func=AF.Ln, scale=1.0 / D, bias=EPS), reads=[bPS[4]], writes=[bRS])
                    P.op('act', lambda e: e.activation(out=RS, in_=RS, func=AF.Exp, scale=-0.5), reads=[bRS], writes=[bRS])
                    for k in range(KD):
                        P.op('dve', lambda e, k=k, i=i: e.scalar_tensor_tensor(out=SQ[:, k, :], in0=Xt[i][:, k, :], scalar=GF[:, k:k + 1], in1=RS,
                                                                               op0=ALU.mult, op1=ALU.mult), reads=[bX[i], bRS, bGF, bSQ], writes=[bSQ])
                    P.dma('sp', f'st{i}', lambda e, t0=t0: e.dma_start(out=yT_out.rearrange("(k p) t -> p k t", p=128)[:, :, t0 - NCTX:t0 - NCTX + 512], in_=SQ),
                          reads=[bSQ])

        for l in range(DEPTH):
            last = (l == DEPTH - 1)
            with ExitStack() as es:
                HT = sb("HT", [128, KD, TTH], BF16, es)
                with ExitStack() as es1:
                    norm_mod(l, 1, HT, es1)
                    P.flush()
                if debug and l == 0:
                    P.dma('sp', 'st0', lambda e: e.dma_start(out=DBG["ht"], in_=HT))
                    P.flush()
                if stop_after == 1:
                    return nc
                with ExitStack() as es2:
                    inproj(l, HT, es2)
                    P.flush()
            if stop_after == 2:
                return nc
            with ExitStack() as es:
                sweepA(l, es)
                P.flush()
            if stop_after == 3:
                return nc
            with ExitStack() as es:
                sweepB(l, es, last)
                P.flush()
            if stop_after == 4:
                return nc
            with ExitStack() as es:
                mixer_dense(l, es, last)
                P.flush()
            if stop_after == 5:
                return nc
            with ExitStack() as es:
                HT = sb("HT2", [128, KD, TTH], BF16, es)
                with ExitStack() as es1:
                    norm_mod(l, 2, HT, es1)
                    P.flush()
                with ExitStack() as es1:
                    ffn_up(l, HT, es1, last)
                    P.flush()
            with ExitStack() as es:
                ffn_down(l, es, last)
                P.flush()
            if stop_after == 6:
                return nc
    return nc


def _pool_mats(L, k, rowlen):
    Pm = np.zeros((L, L), np.float32)
    inv = np.zeros(L, np.float32)
    for t in range(L):
        r0 = (t // rowlen) * rowlen
        tt = t - r0
        lo = min(max(tt - k // 2, 0), rowlen)
        hi = min(max(tt + k // 2, 0), rowlen)
        Pm[r0 + lo:r0 + hi, t] = 1.0
        Pm[t, t] -= (hi - lo)
        inv[t] = 1.0 / (hi - lo)
    return Pm, inv


def _consts(rev):
    k = np.arange(128)[:, None]
    j = np.arange(128)[None, :]
    cm = np.stack([(k == j), (k > j), (k < j), (k <= j), (k >= j)], axis=1).astype(np.float32)
    pml = np.zeros((128, 4, 128), np.float32)
    pmc = np.zeros((128, 2, 4, 2, 128), np.float32)
    invc = np.zeros((4, 384), np.float32)
    for g, kk in enumerate((2, 4, 8, 16)):
        Pm, inv = _pool_mats(128, kk, 64)
        Pc, invcx = _pool_mats(256, kk, 256)
        if rev:
            Pm = Pm[::-1, ::-1]
            inv = inv[::-1]
            Pc = Pc[::-1, ::-1]
            invcx = invcx[::-1]
        pml[:, g, :] = Pm
        for tb_ in range(2):
            for tb in range(2):
                pmc[:, tb_, g, tb, :] = Pc[tb_ * 128:(tb_ + 1) * 128, tb * 128:(tb + 1) * 128]
        invc[g, :NCTX] = invcx
        invc[g, NCTX:] = inv
    invc = np.ascontiguousarray(np.broadcast_to(invc[None], (128, 4, 384)))
    return cm, pml, pmc, invc


def prep_inputs(inputs, n_cores=8):
    f = lambda a: np.ascontiguousarray(np.asarray(a, dtype=np.float32))
    inp = {k: np.asarray(v) for k, v in inputs.items()}
    pm = lambda v: f(v.reshape(-1, 128).T)

    def stack(fn):
        return f(np.stack([fn(l) for l in range(DEPTH)]))
    shared = dict(
        w_ada=f(inp["w_ada"]), w_ssd_out=f(inp["w_ssd_out"]), pool_w=f(inp["pool_w"]),
        w_pool_out=f(inp["w_pool_out"]), w_out=f(inp["w_out"]), w_gate_up=f(inp["w_gate_up"]),
        w_down=f(inp["w_down"]),
        b_ada=stack(lambda l: pm(inp["b_ada"][l])), g_mix=stack(lambda l: pm(inp["g_mix"][l])),
        g_ffn=stack(lambda l: pm(inp["g_ffn"][l])), conv_b=stack(lambda l: pm(inp["conv_b"][l])),
        norm_w=stack(lambda l: pm(inp["ssd_norm_w"][l])), pool_scale=stack(lambda l: pm(inp["pool_scale"][l])),
        g_final=pm(inp["g_final"]),
    )
    per_hf = []
    for hf in range(2):
        rev = hf == 1
        dA, dB = (1, 0) if rev else (0, 1)
        dtc = lambda d: np.arange(6144 + d * 32, 6144 + d * 32 + 32)
        idx = np.concatenate([np.arange(0, 2048), np.arange(2048, 6144), np.arange(7232, 9280),
                              np.arange(6208, 7232), dtc(dA), dtc(dB)])
        cm, pml, pmc, invc = _consts(rev)

        def cw(l):
            w = inp["conv_w"][l]
            if rev:
                w = w[::-1]
            return w.T.reshape(32, 128, 5).transpose(1, 0, 2)

        def bc64(v):
            r = np.concatenate([v[dA], v[dB]])
            return np.broadcast_to(r[None], (128, 64))

        def dsk(l, d):
            v = inp["d_skip"][l][d]
            return np.repeat(v.reshape(16, 2, 1), 64, axis=2).reshape(16, 128).T
        sel = np.zeros((128, 2), np.float32)
        sel[:, 1 - hf] = 1.0
        per_hf.append(dict(
            w_in=f(inp["w_in"][:, :, idx]),
            conv_w=stack(cw), dt_bias=stack(lambda l: bc64(inp["dt_bias"][l])),
            a_log=stack(lambda l: bc64(inp["a_log"][l])),
            dskA=stack(lambda l: dsk(l, dA)), dskB=stack(lambda l: dsk(l, dB)),
            cmat=f(cm), pm_lat=f(pml), pm_ctx=f(pmc), invc=f(invc), selm=f(sel),
        ))
    in_maps = []
    for core in range(n_cores):
        b, hf = core // 2, core % 2
        xb = inp["x"][b]
        cx = inp["ctx"][b]
        if hf == 0:
            lat, halo, cl = xb[:NLAT], xb[NLAT:NLAT + 2], cx
        else:
            lat, halo, cl = xb[NLAT:][::-1], xb[NLAT - 2:NLAT][::-1], cx[::-1]
        xT = f(np.concatenate([cl, lat, halo], axis=0).T)
        cc = f(np.stack([pm(inp["c"][b]), pm(inp["c_ctx"])], axis=-1))
        m = dict(xT=xT, cc=cc)
        m.update(shared)
        m.update(per_hf[hf])
        in_maps.append(m)
    return in_maps


def assemble(results, n_cores=8):
    B = n_cores // 2
    out = np.zeros((B, 2 * NLAT, D), np.float32)
    for core in range(n_cores):
        b, hf = core // 2, core % 2
        y = np.asarray(results[core]["yT"]).T
        if hf == 0:
            out[b, :NLAT] = y
        else:
            out[b, NLAT:] = y[::-1]
    return out


_NC_CACHE = {}


def kernel(**inputs):
    n = 8
    if "nc" not in _NC_CACHE:
        _NC_CACHE["nc"] = build(n)
    nc = _NC_CACHE["nc"]
    in_maps = prep_inputs(inputs, n)
    res = run_bass_kernel_spmd(nc, in_maps, core_ids=list(range(n)))
    return assemble(res.results, n)
```

```python
from contextlib import ExitStack
import numpy as np
import concourse.bass as bass
import concourse.mybir as mybir
from concourse.bass_utils import run_bass_kernel_spmd

F32 = mybir.dt.float32
BF16 = mybir.dt.bfloat16
AF = mybir.ActivationFunctionType
ALU = mybir.AluOpType

D = 1024
KD = 8
DEPTH = 2
NCTX = 256
NLAT = 2048
TT = NCTX + NLAT
TTH = TT + 2
NCH = TT // 128
DIN = 2048
NH = 32
HP = 64
NG = 8
NS = 128
XBCW = 4096
FFH = 2816
INCOLS = 9280
EPS = 1e-6
CZ, CX, CGT, CPL, CDT = 0, 2048, 6144, 8192, 9216
XW = 2308
LOFF = 260

ENGS = ['pe', 'act', 'dve', 'pool', 'sp']
SAME_ENGINE_SYNC = True


class Buf:
    __slots__ = ('name', 'w', 'r')

    def __init__(self, name=''):
        self.name = name
        self.w = None
        self.r = []


class Prog:
    def __init__(self, nc, n_dma_sems=60):
        self.nc = nc
        self.ops = {e: [] for e in ENGS}
        self.sems = {}
        self.cnt = {}
        for e in ENGS:
            self.sems[e] = nc.alloc_semaphore(name=f"c_{e}")
            self.cnt[e] = 0
        self.known = {e: {} for e in ENGS}
        self.dma_keys = {}
        self.n_dma_sems = n_dma_sems

    def _dma_sem(self, key):
        if key not in self.dma_keys:
            assert len(self.dma_keys) < self.n_dma_sems, "too many dma keys"
            s = self.nc.alloc_semaphore(name=f"d_{len(self.dma_keys)}")
            k = ('dma', key)
            self.sems[k] = s
            self.cnt[k] = 0
            self.dma_keys[key] = k
        return self.dma_keys[key]

    def _need(self, eng, ev, waits):
        if ev is None:
            return
        k, v = ev
        if self.known[eng].get(k, 0) >= v:
            return
        waits[k] = max(waits.get(k, 0), v)

    def _deps(self, eng, reads, writes):
        waits = {}
        for b in reads:
            if SAME_ENGINE_SYNC or b.w is None or b.w[0] != eng:
                self._need(eng, b.w, waits)
        for b in writes:
            if SAME_ENGINE_SYNC or b.w is None or b.w[0] != eng:
                self._need(eng, b.w, waits)
            for ev in b.r:
                if ev[0] == eng:
                    continue
                self._need(eng, ev, waits)
        for k, v in waits.items():
            self.known[eng][k] = max(self.known[eng].get(k, 0), v)
        return waits

    def _commit(self, ev, reads, writes):
        for b in reads:
            b.r.append(ev)
        for b in writes:
            b.w = ev
            b.r = []

    def op(self, eng, fn, reads=(), writes=()):
        waits = self._deps(eng, reads, writes)
        self.cnt[eng] += 1
        ev = (eng, self.cnt[eng])
        self.ops[eng].append((waits, fn, (eng, 1)))
        self._commit(ev, reads, writes)
        return ev

    def dma(self, queue, key, fn, reads=(), writes=(), inc=16):
        k = self._dma_sem(key)
        waits = self._deps(queue, reads, writes)
        prev = self.cnt[k]
        if prev > 0 and self.known[queue].get(k, 0) < prev:
            waits[k] = max(waits.get(k, 0), prev)
            self.known[queue][k] = prev
        self.cnt[k] += inc
        ev = (k, self.cnt[k])
        self.ops[queue].append((waits, fn, (k, inc)))
        self._commit(ev, reads, writes)
        return ev

    def barrier(self):
        for e in ENGS:
            waits = {}
            for k, v in self.cnt.items():
                if v > 0 and self.known[e].get(k, 0) < v:
                    waits[k] = v
                    self.known[e][k] = v
            self.ops[e].append((waits, None, None))

    def flush(self):
        self.barrier()
        nc = self.nc
        sems = self.sems
        ops = self.ops

        def run(engname):
            def f(eng):
                for waits, fn, inc in ops[engname]:
                    for k, v in waits.items():
                        eng.wait_ge(sems[k], v)
                    if fn is not None:
                        ins = fn(eng)
                        ins.then_inc(sems[inc[0]], inc[1])
            return f

        with nc.Block() as block:
            block.tensor(run('pe'))
            block.scalar(run('act'))
            block.vector(run('dve'))
            block.gpsimd(run('pool'))
            block.sync(run('sp'))
        self.ops = {e: [] for e in ENGS}


def bc(ap, dims):
    return bass.AP(ap.tensor, ap.offset, [list(ap.ap[0])] + [list(d) for d in dims])


def build(n_cores=8, debug=False, stop_after=None, ext_scratch=False):
    nc = bass.Bass("TRN2", target_bir_lowering=False)
    P = Prog(nc)
    dbg_kind = "ExternalOutput" if (debug or ext_scratch) else "Internal"

    def din(name, shape):
        return nc.dram_tensor(name, list(shape), F32, kind="ExternalInput").ap()

    xT_in = din("xT", [D, TTH])
    cc_in = din("cc", [128, KD, 2])
    w_ada = din("w_ada", [DEPTH, D, 6 * D])
    b_ada = din("b_ada", [DEPTH, 128, 48])
    g_mix = din("g_mix", [DEPTH, 128, KD])
    g_ffn = din("g_ffn", [DEPTH, 128, KD])
    w_in = din("w_in", [DEPTH, D, INCOLS])
    conv_w = din("conv_w", [DEPTH, 128, 32, 5])
    conv_b = din("conv_b", [DEPTH, 128, 32])
    dt_bias = din("dt_bias", [DEPTH, 128, 64])
    a_log = din("a_log", [DEPTH, 128, 64])
    dskA = din("dskA", [DEPTH, 128, 16])
    dskB = din("dskB", [DEPTH, 128, 16])
    norm_w = din("norm_w", [DEPTH, 128, 16])
    w_ssd_out = din("w_ssd_out", [DEPTH, DIN, D])
    pool_w = din("pool_w", [DEPTH, 4, 256, 256])
    pool_scale = din("pool_scale", [DEPTH, 128, 8])
    w_pool_out = din("w_pool_out", [DEPTH, D, D])
    w_out = din("w_out", [DEPTH, D, D])
    w_gate_up = din("w_gate_up", [DEPTH, D, 2 * FFH])
    w_down = din("w_down", [DEPTH, FFH, D])
    g_final = din("g_final", [128, KD])
    cmat = din("cmat", [128, 5, 128])
    pm_lat = din("pm_lat", [128, 4, 128])
    pm_ctx = din("pm_ctx", [128, 2, 4, 2, 128])
    invc = din("invc", [128, 4, 384])
    selm = din("selm", [128, 2])

    yT_out = nc.dram_tensor("yT", [D, NLAT], F32, kind="ExternalOutput").ap()

    def dscr(name, shape, dt):
        return nc.dram_tensor(name, list(shape), dt, kind=dbg_kind).ap()

    XT_d = dscr("XT_d", [D, TTH], F32)
    ZS_d = dscr("ZS_d", [DIN, TT], BF16)
    GS_d = dscr("GS_d", [DIN, TT], BF16)
    XBC_d = dscr("XBC_d", [XBCW, XW], BF16)
    PM2_d = dscr("PM2_d", [D, TT], BF16)
    Y_d = dscr("Y_d", [DIN, TT], F32)
    SB_d = dscr("SB_d", [NCH, 128, DIN], F32)
    YN_d = dscr("YN_d", [DIN, TT], BF16)
    G_d = dscr("G_d", [FFH, TT], BF16)
    CCI_d = nc.dram_tensor("CCI_d", [128, DIN], F32).ap()
    CCO_d = nc.dram_tensor("CCO_d", [256, DIN], F32).ap()
    HCI_d = nc.dram_tensor("HCI_d", [128, 16], F32).ap()
    HCO_d = nc.dram_tensor("HCO_d", [256, 16], F32).ap()
    DBG = {}
    if debug:
        DBG["dtt"] = nc.dram_tensor("DBG_dtt", [128, NCH, 64], F32, kind="ExternalOutput").ap()
        DBG["mod"] = nc.dram_tensor("DBG_mod", [DEPTH, 128, 48, 2], F32, kind="ExternalOutput").ap()
        DBG["ht"] = nc.dram_tensor("DBG_ht", [128, KD, TTH], BF16, kind="ExternalOutput").ap()
        DBG["sa"] = nc.dram_tensor("DBG_sa", [128, DIN], F32, kind="ExternalOutput").ap()
        DBG["yt"] = nc.dram_tensor("DBG_yt", [DIN, TT], F32, kind="ExternalOutput").ap()

    pairs = [[2 * i, 2 * i + 1] for i in range(n_cores // 2)]

    with ExitStack() as top:
        _uid = [0]

        def sb(name, shape, dt, es=top):
            _uid[0] += 1
            return es.enter_context(nc.sbuf_tensor(f"{name}_{_uid[0]}", list(shape), dt)).ap()

        PS = [nc.alloc_psum_tensor(f"ps{i}", [128, 512], F32).ap() for i in range(8)]

        CM = sb("CM", [128, 5, 128], F32)
        CMB = sb("CMB", [128, 5, 128], BF16)
        ONES = sb("ONES", [128, 128], F32)
        ONESB = sb("ONESB", [128, 128], BF16)
        MOD = [sb(f"MOD{l}", [128, 48, 2], F32) for l in range(DEPTH)]
        A1 = [sb(f"A1_{l}", [128, KD, 2], F32) for l in range(DEPTH)]
        A2 = [sb(f"A2_{l}", [128, KD, 2], F32) for l in range(DEPTH)]
        SELM = sb("SELM", [128, 2], F32)
        DTT = sb("DTT", [128, NCH, 64], F32)
        DECB = sb("DECB", [128, NCH, 32], F32)
        SA = sb("SA", [128, DIN], F32)
        SAb = sb("SAb", [128, DIN], BF16)
        ident_f = CM[:, 0, :]
        UA_f = CM[:, 1, :]
        UB_f = CM[:, 2, :]
        TLE_f = CM[:, 3, :]
        TGE_f = CM[:, 4, :]
        ident_b = CMB[:, 0, :]

        def B_(n=''):
            return Buf(n)

        with ExitStack() as es:
            b_cm, b_cmb, b_ones, b_sel = B_(), B_(), B_(), B_()
            P.dma('sp', 'ld0', lambda e: e.dma_start(out=CM, in_=cmat), writes=[b_cm])
            P.dma('pool', 'w0', lambda e: e.dma_start(out=CMB, in_=cmat), writes=[b_cmb])
            P.dma('sp', 'ld1', lambda e: e.dma_start(out=SELM, in_=selm), writes=[b_sel])
            P.op('dve', lambda e: e.memset(ONES, 1.0), writes=[b_ones])
            P.op('dve', lambda e: e.memset(ONESB, 1.0), writes=[b_ones])
            xb = [sb(f"xb{i}", [128, KD, 512], F32, es) for i in range(2)]
            bxb = [B_(), B_()]
            bxt = B_()
            xin_v = xT_in.rearrange("(k p) t -> p k t", p=128)
            xtd_v = XT_d.rearrange("(k p) t -> p k t", p=128)
            tiles0 = [(0, 256)] + [(256 + 512 * i, 512) for i in range(4)] + [(TT, 2)]
            for i, (c0, w) in enumerate(tiles0):
                j = i % 2
                P.dma('sp', f'ld{j}', lambda e, j=j, c0=c0, w=w: e.dma_start(out=xb[j][:, :, :w], in_=xin_v[:, :, c0:c0 + w]),
                      writes=[bxb[j]])
                P.dma('sp', f'st{j}', lambda e, j=j, c0=c0, w=w: e.dma_start(out=xtd_v[:, :, c0:c0 + w], in_=xb[j][:, :, :w]),
                      reads=[bxb[j]], writes=[bxt])
            ccf = sb("ccf", [128, KD, 2], F32, es)
            ccb = sb("ccb", [128, KD, 2], BF16, es)
            b_ccf, b_ccb = B_(), B_()
            P.dma('sp', 'ld2', lambda e: e.dma_start(out=ccf, in_=cc_in), writes=[b_ccf])
            P.op('act', lambda e: e.activation(out=ccb, in_=ccf, func=AF.Silu), reads=[b_ccf], writes=[b_ccb])
            wa = [sb(f"wa{i}", [128, KD, 512], BF16, es) for i in range(2)]
            bwa = [B_(), B_()]
            bada = sb("bada", [128, 48], F32, es)
            gm = sb("gm", [128, KD], F32, es)
            gf = sb("gf", [128, KD], F32, es)
            b_bada, b_gm, b_gf, b_ps = B_(), B_(), B_(), B_()
            for l in range(DEPTH):
                b_mod, b_a = B_(), B_()
                P.dma('sp', 'ld3', lambda e, l=l: e.dma_start(out=bada, in_=b_ada[l]), writes=[b_bada])
                P.dma('sp', 'ld4', lambda e, l=l: e.dma_start(out=gm, in_=g_mix[l]), writes=[b_gm])
                P.dma('sp', 'ld5', lambda e, l=l: e.dma_start(out=gf, in_=g_ffn[l]), writes=[b_gf])
                wv = w_ada[l].rearrange("(k p) n -> p k n", p=128)
                for jb in range(12):
                    j = jb % 2
                    P.dma('pool', f'w{j}', lambda e, j=j, jb=jb, wv=wv: e.dma_start(out=wa[j], in_=wv[:, :, jb * 512:(jb + 1) * 512]),
                          writes=[bwa[j]])
                    for q in range(4):
                        cb = jb * 4 + q

                        def mm(e, j=j, q=q, cb=cb):
                            ins = None
                            for k in range(KD):
                                ins = e.matmul(PS[0][:, cb * 2:cb * 2 + 2], lhsT=wa[j][:, k, q * 128:(q + 1) * 128],
                                               rhs=ccb[:, k, :], start=(k == 0), stop=(k == KD - 1))
                            return ins
                        P.op('pe', mm, reads=[bwa[j], b_ccb], writes=[b_ps])
                P.op('dve', lambda e, l=l: e.tensor_tensor(out=MOD[l], in0=PS[0][:, 0:96].rearrange("p (c t) -> p c t", t=2),
                                                           in1=bc(bada, [[1, 48], [0, 2]]), op=ALU.add),
                     reads=[b_ps, b_bada], writes=[b_mod])
                P.op('dve', lambda e, l=l: e.scalar_tensor_tensor(out=A1[l], in0=MOD[l][:, 8:16, :], scalar=1.0,
                                                                  in1=bc(gm, [[1, KD], [0, 2]]), op0=ALU.add, op1=ALU.mult),
                     reads=[b_mod, b_gm], writes=[b_a])
                P.op('dve', lambda e, l=l: e.scalar_tensor_tensor(out=A2[l], in0=MOD[l][:, 32:40, :], scalar=1.0,
                                                                  in1=bc(gf, [[1, KD], [0, 2]]), op0=ALU.add, op1=ALU.mult),
                     reads=[b_mod, b_gf], writes=[b_a])
                if debug:
                    P.dma('sp', 'st2', lambda e, l=l: e.dma_start(out=DBG["mod"][l], in_=MOD[l]), reads=[b_mod])
            P.flush()
        if stop_after == 0:
            return nc

        def norm_mod(l, which, HT, es):
            A = A1[l] if which == 1 else A2[l]
            boff = 0 if which == 1 else 24
            xt = [sb(f"nx{i}", [128, KD, 512], F32, es) for i in range(2)]
            sq = [sb(f"nq{i}", [128, KD, 512], BF16, es) for i in range(2)]
            xn = [sb(f"nn{i}", [128, KD, 512], F32, es) for i in range(2)]
            rs = [sb(f"nr{i}", [128, 512], F32, es) for i in range(2)]
            ONESb = sb("nob", [128, 128], BF16, es)
            bob = B_()
            P.op('dve', lambda e: e.memset(ONESb, 1.0), writes=[bob])
            bx, bq, bn, br, bp = [B_(), B_()], [B_(), B_()], [B_(), B_()], [B_(), B_()], [B_(), B_()]
            bht = B_()
            tiles = [(0, 256, 1)] + [(256 + 512 * i, 512, 0) for i in range(4)]
            if which == 1:
                tiles.append((TT, 2, 0))

            def front(i):
                c0, w, mc = tiles[i]
                j = i % 2
                ps = PS[j]
                P.dma('sp', f'ld{j}', lambda e, j=j, c0=c0, w=w: e.dma_start(out=xt[j][:, :, :w], in_=xtd_v[:, :, c0:c0 + w]),
                      writes=[bx[j]])
                P.op('act', lambda e, j=j, w=w: e.activation(out=sq[j][:, :, :w], in_=xt[j][:, :, :w], func=AF.Square),
                     reads=[bx[j]], writes=[bq[j]])

                def mm(e, j=j, w=w, ps=ps):
                    ins = None
                    for k in range(KD):
                        ins = e.matmul(ps[:, :w], lhsT=ONESb, rhs=sq[j][:, k, :w], start=(k == 0), stop=(k == KD - 1))
                    return ins
                P.op('pe', mm, reads=[bq[j], bob], writes=[bp[j]])

            def back(i):
                c0, w, mc = tiles[i]
                j = i % 2
                ps = PS[j]
                P.op('act', lambda e, j=j, w=w, ps=ps: e.activation(out=rs[j][:, :w], in_=ps[:, :w], func=AF.Ln, scale=1.0 / D, bias=EPS),
                     reads=[bp[j]], writes=[br[j]])
                P.op('act', lambda e, j=j, w=w: e.activation(out=rs[j][:, :w], in_=rs[j][:, :w], func=AF.Exp, scale=-0.5),
                     reads=[br[j]], writes=[br[j]])
                P.op('dve', lambda e, j=j, w=w: e.tensor_tensor(out=xn[j][:, :, :w], in0=xt[j][:, :, :w],
                                                                in1=bc(rs[j], [[0, KD], [1, w]]), op=ALU.mult),
                     reads=[bx[j], br[j]], writes=[bn[j]])
                for k in range(KD):
                    if k % 3 == 2:
                        P.op('dve', lambda e, j=j, w=w, k=k, c0=c0, mc=mc: e.tensor_scalar(
                            out=HT[:, k, c0:c0 + w], in0=xn[j][:, k, :w], scalar1=A[:, k, mc:mc + 1],
                            scalar2=MOD[l][:, boff + k, mc:mc + 1], op0=ALU.mult, op1=ALU.add),
                            reads=[bn[j]], writes=[bht])
                    else:
                        P.op('act', lambda e, j=j, w=w, k=k, c0=c0, mc=mc: e.activation(
                            out=HT[:, k, c0:c0 + w], in_=xn[j][:, k, :w], func=AF.Identity,
                            scale=A[:, k, mc:mc + 1], bias=MOD[l][:, boff + k, mc:mc + 1]),
                            reads=[bn[j]], writes=[bht])

            front(0)
            for i in range(len(tiles)):
                if i + 1 < len(tiles):
                    front(i + 1)
                back(i)

        def inproj(l, HT, es):
            W = [sb(f"ipW{i}", [128, KD, 512], BF16, es) for i in range(2)]
            bW = [B_(), B_()]
            U = [sb(f"ipU{i}", [128, XW + 4], F32, es) for i in range(2)]
            bU = [B_(), B_()]
            ACC = [sb(f"ipA{i}", [128, XW], F32, es) for i in range(2)]
            bACC = [B_(), B_()]
            STG = [sb(f"ipS{i}", [128, XW], BF16, es) for i in range(3)]
            bSTG = [B_(), B_(), B_()]
            cw = sb("ipcw", [128, 32, 5], F32, es)
            cb = sb("ipcb", [128, 32], F32, es)
            b_cw, b_cb = B_(), B_()
            bPS = [B_() for _ in range(8)]
            wv = w_in[l].rearrange("(k p) n -> p k n", p=128)
            P.dma('sp', 'ld0', lambda e: e.dma_start(out=cw, in_=conv_w[l]), writes=[b_cw])
            P.dma('sp', 'ld1', lambda e: e.dma_start(out=cb, in_=conv_b[l]), writes=[b_cb])
            for i in range(2):
                P.op('dve', lambda e, i=i: e.memset(U[i], 0.0), writes=[bU[i]])
                P.op('dve', lambda e, i=i: e.memset(ACC[i], 0.0), writes=[bACC[i]])
            bHT = B_()
            nw = [0]

            def load_w(c0, ncol):
                j = nw[0] % 2
                nw[0] += 1
                P.dma('pool', f'w{j}', lambda e, j=j, c0=c0, ncol=ncol: e.dma_start(out=W[j][:, :, :ncol], in_=wv[:, :, c0:c0 + ncol]),
                      writes=[bW[j]])
                return j

            dtb = sb("ipdtb", [128, 64], F32, es)
            dx = sb("ipdx", [128, NCH * 64], F32, es)
            d1 = sb("ipd1", [128, NCH * 64], F32, es)
            b_dtb, b_dx, b_d1, b_dtt = B_(), B_(), B_(), B_()
            P.dma('sp', 'ld2', lambda e: e.dma_start(out=dtb, in_=dt_bias[l]), writes=[b_dtb])
            j = load_w(CDT, 64)
            for c in range(NCH):
                pb = 5 + c // 8
                po = (c % 8) * 64

                def mm(e, j=j, c=c, pb=pb, po=po):
                    ins = None
                    for k in range(KD):
                        ins = e.matmul(PS[pb][:, po:po + 64], lhsT=HT[:, k, c * 128:(c + 1) * 128], rhs=W[j][:, k, 0:64],
                                       start=(k == 0), stop=(k == KD - 1))
                    return ins
                P.op('pe', mm, reads=[bW[j], bHT], writes=[bPS[pb]])
            for pb, nchk in ((5, 8), (6, 8), (7, 2)):
                c0 = (pb - 5) * 8
                P.op('dve', lambda e, pb=pb, nchk=nchk, c0=c0: e.tensor_tensor(
                    out=dx[:, c0 * 64:(c0 + nchk) * 64].rearrange("p (c h) -> p c h", h=64),
                    in0=PS[pb][:, 0:nchk * 64].rearrange("p (c h) -> p c h", h=64),
                    in1=bc(dtb, [[0, nchk], [1, 64]]), op=ALU.add), reads=[bPS[pb], b_dtb], writes=[b_dx])
            P.op('act', lambda e: e.activation(out=d1, in_=dx, func=AF.Abs), reads=[b_dx], writes=[b_d1])
            P.op('act', lambda e: e.activation(out=d1, in_=d1, func=AF.Exp, scale=-1.0), reads=[b_d1], writes=[b_d1])
            P.op('act', lambda e: e.activation(out=d1, in_=d1, func=AF.Ln, bias=1.0), reads=[b_d1], writes=[b_d1])
            P.op('dve', lambda e: e.tensor_scalar_max(out=dx, in0=dx, scalar1=0.0), reads=[b_dx], writes=[b_dx])
            P.op('dve', lambda e: e.tensor_tensor(out=DTT.rearrange("p c h -> p (c h)"), in0=dx, in1=d1, op=ALU.add),
                 reads=[b_dx, b_d1], writes=[b_dtt])
            if debug and l == 0:
                P.dma('sp', 'st3', lambda e: e.dma_start(out=DBG["dtt"], in_=DTT), reads=[b_dtt])

            PMl = sb("ipPMl", [128, 4, 128], BF16, es)
            PMc = sb("ipPMc", [128, 2, 4, 2, 128], BF16, es)
            INV = sb("ipINV", [128, 4, 384], F32, es)
            PW = sb("ipPW", [128, 4, 2, 256], BF16, es)
            PSC = sb("ipPSC", [128, 8], F32, es)
            UT = [sb(f"ipUT{i}", [128, 2, 512], BF16, es) for i in range(2)]
            bUT = [B_(), B_()]
            PMT = [sb(f"ipPMT{i}", [128, 4, 256], BF16, es) for i in range(2)]
            bPMT = [B_(), B_()]
            PM2S = sb("ipPM2S", [128, 4, TT], BF16, es)
            b_pm2s = B_()
            b_pml, b_pmc, b_inv, b_pw, b_psc = B_(), B_(), B_(), B_(), B_()
            P.dma('pool', 'w2', lambda e: e.dma_start(out=PMl, in_=pm_lat), writes=[b_pml])
            P.dma('pool', 'w3', lambda e: e.dma_start(out=PMc, in_=pm_ctx), writes=[b_pmc])
            P.dma('sp', 'ld3', lambda e: e.dma_start(out=INV, in_=invc), writes=[b_inv])
            P.dma('pool', 'w4', lambda e: e.dma_start(out=PW, in_=pool_w[l].rearrange("g (ib p) o -> p g ib o", p=128)), writes=[b_pw])
            P.dma('sp', 'ld4', lambda e: e.dma_start(out=PSC, in_=pool_scale[l]), writes=[b_psc])
            it = 0
            for jp in range(2):
                j = load_w(CPL + jp * 512, 512)
                units = [(0, 2)] + [(c, 1) for c in range(2, NCH)]
                for (c0, nck) in units:
                    u = it % 2
                    it += 1
                    wtok = nck * 128
                    for cc in range(nck):
                        def mm(e, j=j, c=c0 + cc):
                            ins = None
                            for k in range(KD):
                                ins = e.matmul(PS[4], lhsT=HT[:, k, c * 128:(c + 1) * 128], rhs=W[j][:, k, :],
                                               start=(k == 0), stop=(k == KD - 1))
                            return ins
                        P.op('pe', mm, reads=[bW[j], bHT], writes=[bPS[4]])
                        P.op('act', lambda e, u=u, cc=cc: e.activation(out=UT[u][:, cc, :], in_=PS[4], func=AF.Copy),
                             reads=[bPS[4]], writes=[bUT[u]])
                    def pbk(banks, q, wtok=wtok):
                        if wtok == 128:
                            return banks[0], q * 128
                        return banks[q // 2], (q % 2) * 256
                    for q in range(4):
                        g = 2 * jp + q // 2
                        pbi, po = pbk((5, 7), q)

                        def mmp(e, u=u, q=q, g=g, nck=nck, pbi=pbi, po=po):
                            ins = None
                            if nck == 1:
                                ins = e.matmul(PS[pbi][:, po:po + 128], lhsT=UT[u][:, 0, q * 128:(q + 1) * 128],
                                               rhs=PMl[:, g, :], start=True, stop=True)
                            else:
                                for tb in range(2):
                                    for tb_ in range(2):
                                        ins = e.matmul(PS[pbi][:, po + tb * 128:po + (tb + 1) * 128],
                                                       lhsT=UT[u][:, tb_, q * 128:(q + 1) * 128],
                                                       rhs=PMc[:, tb_, g, tb, :], start=(tb_ == 0), stop=(tb_ == 1))
                            return ins
                        P.op('pe', mmp, reads=[bUT[u], b_pml, b_pmc], writes=[bPS[pbi]])
                    for gg in range(2):
                        g = 2 * jp + gg
                        ioff = 0 if nck == 2 else 256
                        pbi, po = pbk((5, 7), 2 * gg)
                        P.op('dve', lambda e, u=u, gg=gg, g=g, wtok=wtok, ioff=ioff, pbi=pbi, po=po: e.tensor_tensor(
                            out=PMT[u][:, 2 * gg:2 * gg + 2, :wtok],
                            in0=PS[pbi][:, po:po + 2 * wtok].rearrange("p (q t) -> p q t", t=wtok),
                            in1=bc(INV[:, g, ioff:ioff + wtok], [[0, 2], [1, wtok]]), op=ALU.mult),
                            reads=[bPS[pbi], b_inv], writes=[bPMT[u]])
                    for gg in range(2):
                        g = 2 * jp + gg
                        for ob in range(2):
                            q = 2 * gg + ob
                            pbi, po = pbk((6, 3), q)

                            def mmw(e, u=u, g=g, gg=gg, ob=ob, wtok=wtok, pbi=pbi, po=po):
                                ins = None
                                for ib in range(2):
                                    ins = e.matmul(PS[pbi][:, po:po + wtok], lhsT=PW[:, g, ib, ob * 128:(ob + 1) * 128],
                                                   rhs=PMT[u][:, 2 * gg + ib, :wtok], start=(ib == 0), stop=(ib == 1))
                                return ins
                            P.op('pe', mmw, reads=[bPMT[u], b_pw], writes=[bPS[pbi]])
                    for q in range(4):
                        blk = jp * 4 + q
                        pbi, po = pbk((6, 3), q)
                        P.op('act', lambda e, q=q, blk=blk, wtok=wtok, c0=c0, pbi=pbi, po=po: e.activation(
                            out=PM2S[:, q, c0 * 128:c0 * 128 + wtok], in_=PS[pbi][:, po:po + wtok],
                            func=AF.Copy, scale=PSC[:, blk:blk + 1]), reads=[bPS[pbi], b_psc], writes=[b_pm2s])
                P.dma('sp', 'st0', lambda e, jp=jp: e.dma_start(
                    out=PM2_d[jp * 512:(jp + 1) * 512, :].rearrange("(q p) t -> p q t", p=128), in_=PM2S),
                    reads=[b_pm2s])

            tiles = [(0, 256)] + [(256 + 512 * i, 512) for i in range(4)]
            nps = [0]
            nstg = [0]
            nxb = [0]
            for jb in range(16):
                c0w = jb * 512
                fam = 'z' if jb < 4 else ('x' if jb < 12 else 'g')
                j = load_w(c0w, 512)
                for q in range(4):
                    if fam == 'x':
                        blk = (jb - 4) * 4 + q
                        ui = nxb[0] % 2
                        nxb[0] += 1
                        tl = tiles + [(TT, 2)]
                    else:
                        si = nstg[0] % 3
                        nstg[0] += 1
                        tl = tiles
                    for (t0, w) in tl:
                        pb = nps[0] % 8
                        nps[0] += 1

                        def mm(e, j=j, q=q, t0=t0, w=w, pb=pb):
                            ins = None
                            for k in range(KD):
                                ins = e.matmul(PS[pb][:, :w], lhsT=W[j][:, k, q * 128:(q + 1) * 128], rhs=HT[:, k, t0:t0 + w],
                                               start=(k == 0), stop=(k == KD - 1))
                            return ins
                        P.op('pe', mm, reads=[bW[j], bHT], writes=[bPS[pb]])
                        if fam == 'x':
                            uo = 2 if t0 == 0 else (LOFF + 2 + (t0 - NCTX))
                            P.op('act', lambda e, ui=ui, uo=uo, w=w, pb=pb: e.activation(out=U[ui][:, uo:uo + w], in_=PS[pb][:, :w], func=AF.Copy),
                                 reads=[bPS[pb]], writes=[bU[ui]])
                            if w > 2:
                                P.op('act', lambda e, ui=ui, uo=uo, w=w, pb=pb, blk=blk: e.activation(
                                    out=ACC[ui][:, uo - 2:uo - 2 + w], in_=PS[pb][:, :w], func=AF.Copy, scale=cw[:, blk, 2:3]),
                                    reads=[bPS[pb], b_cw], writes=[bACC[ui]])
                        else:
                            fn = AF.Silu if fam == 'z' else AF.Sigmoid
                            P.op('act', lambda e, si=si, t0=t0, w=w, pb=pb, fn=fn: e.activation(out=STG[si][:, t0:t0 + w], in_=PS[pb][:, :w], func=fn),
                                 reads=[bPS[pb]], writes=[bSTG[si]])
                    if fam == 'x':
                        ai = ui
                        for tp in (0, 1, 3, 4):
                            P.op('dve', lambda e, ui=ui, ai=ai, blk=blk, tp=tp: e.scalar_tensor_tensor(
                                out=ACC[ai], in0=U[ui][:, tp:tp + XW], scalar=cw[:, blk, tp:tp + 1], in1=ACC[ai],
                                op0=ALU.mult, op1=ALU.add), reads=[bU[ui], b_cw, bACC[ai]], writes=[bACC[ai]])
                        si = nstg[0] % 3
                        nstg[0] += 1
                        P.op('act', lambda e, si=si, ai=ai, blk=blk: e.activation(out=STG[si], in_=ACC[ai], func=AF.Silu, bias=cb[:, blk:blk + 1]),
                             reads=[bACC[ai], b_cb], writes=[bSTG[si]])
                        P.dma('sp', f'st{si}', lambda e, si=si, blk=blk: e.dma_start(out=XBC_d[blk * 128:(blk + 1) * 128, :], in_=STG[si]),
                              reads=[bSTG[si]])
                    else:
                        dst = ZS_d if fam == 'z' else GS_d
                        blk = (jb * 4 + q) if fam == 'z' else ((jb - 12) * 4 + q)
                        P.dma('sp', f'st{si}', lambda e, si=si, blk=blk, dst=dst: e.dma_start(out=dst[blk * 128:(blk + 1) * 128, :], in_=STG[si][:, 0:TT]),
                              reads=[bSTG[si]])


        xbc_v = XBC_d.rearrange("(blk p) t -> p blk t", p=128)
        yd_v = Y_d.rearrange("(blk p) t -> p blk t", p=128)
        zs_v = ZS_d.rearrange("(blk p) t -> p blk t", p=128)
        ynd_v = YN_d.rearrange("(blk p) t -> p blk t", p=128)

        def chunk_col(c):
            return c * 128 if c < 2 else LOFF + (c - 2) * 128

        def pad_ap(X, h0, nh):
            return bass.AP(X.tensor, X.offset + h0 * 128, [list(X.ap[0]), [256, nh // 2], [192, 2], [1, 64]])

        def sweepA(l, es):
            XB = [sb(f"aXB{i}", [128, 32, 128], BF16, es) for i in range(2)]
            bXB = [B_(), B_()]
            XDA = sb("aXDA", [128, 32, 128], BF16, es)
            XDB = sb("aXDB", [128, 32, 128], BF16, es)
            XWA = sb("aXWA", [128, DIN], BF16, es)
            XWB = sb("aXWB", [128, DIN], BF16, es)
            BT = sb("aBT", [128, 1024], BF16, es)
            AN = sb("aAN", [128, 64], F32, es)
            DTA = sb("aDTA", [128, 64], F32, es)
            EXPO = sb("aEXPO", [128, 128], F32, es)
            FAC = sb("aFAC", [128, 64], F32, es)
            E = [sb(f"aE{i}", [128, 4, 128], F32, es) for i in range(2)]
            MP = [[sb(f"aMP{i}{d}", [128, 4, 128], BF16, es) for d in range(2)] for i in range(2)]
            CBTm = sb("aCBT", [128, 2, 8, 128], F32, es)
            DBH = sb("aDBH", [128, 2, DIN], BF16, es)
            DTH = sb("aDTH", [128, 2, 32], BF16, es)
            bDTH = B_()
            EA = sb("aEA", [128, DIN], F32, es)
            T1 = sb("aT1", [128, DIN], F32, es)
            YP = [sb(f"aYP{i}", [128, DIN], F32, es) for i in range(2)]
            SBL = [sb(f"aSBL{i}", [128, DIN], F32, es) for i in range(2)]
            DSK = sb("aDSK", [128, 16, 128], BF16, es)
            dsa = sb("adsa", [128, 16], F32, es)
            dsb = sb("adsb", [128, 16], F32, es)
            bXD, bXW, bBT, bAN, bDTA, bEXPO, bFAC = B_(), B_(), B_(), B_(), B_(), B_(), B_()
            bE = [B_(), B_()]
            bMP = [[B_(), B_()], [B_(), B_()]]
            bCBT, bDBC, bEA, bT1, bDSK, bds, bSA, bSAb, bDECB = B_(), B_(), B_(), B_(), B_(), B_(), B_(), B_(), B_()
            bYP, bSBL = [B_(), B_()], [B_(), B_()]
            bPS = [B_() for _ in range(8)]
            PSb = [PS[i].bitcast(BF16) for i in range(3)]
            P.dma('sp', 'ld0', lambda e: e.dma_start(out=AN, in_=a_log[l]), writes=[bAN])
            P.op('act', lambda e: e.activation(out=AN, in_=AN, func=AF.Exp), reads=[bAN], writes=[bAN])
            P.op('dve', lambda e: e.tensor_scalar_mul(out=AN, in0=AN, scalar1=-1.0), reads=[bAN], writes=[bAN])
            P.dma('sp', 'ld1', lambda e: e.dma_start(out=dsa, in_=dskA[l]), writes=[bds])
            P.dma('sp', 'ld2', lambda e: e.dma_start(out=dsb, in_=dskB[l]), writes=[bds])
            P.op('dve', lambda e: e.tensor_tensor(out=dsa, in0=dsa, in1=dsb, op=ALU.add), reads=[bds], writes=[bds])
            for pp in range(16):
                P.op('dve', lambda e, pp=pp: e.tensor_scalar(out=DSK[:, pp, :], in0=ident_f, scalar1=dsa[:, pp:pp + 1], scalar2=None, op0=ALU.mult),
                     reads=[bds], writes=[bDSK])
            P.op('dve', lambda e: e.memset(XDA, 0.0), writes=[bXD])
            P.op('dve', lambda e: e.memset(XDB, 0.0), writes=[bXD])
            P.op('dve', lambda e: e.memset(SA, 0.0), writes=[bSA])
            P.op('dve', lambda e: e.memset(SAb, 0.0), writes=[bSAb])

            R2 = [sb(f"aR2x{i}", [128, 4, 128], F32, es) for i in range(3)]
            bR2 = [B_(), B_(), B_()]
            P.dma('sp', 'ld3', lambda e: e.dma_start(out=XB[0], in_=xbc_v[:, :, 0:128]), writes=[bXB[0]])
            for c in range(NCH):
                i = c % 2
                co = chunk_col(c)
                yi = c % 2
                if c + 1 < NCH:
                    con = chunk_col(c + 1)
                    P.dma('sp', f'ld{3 + (1 - i)}', lambda e, i=i, con=con: e.dma_start(out=XB[1 - i], in_=xbc_v[:, :, con:con + 128]), writes=[bXB[1 - i]])
                P.op('dve', lambda e, c=c: e.tensor_tensor(out=DTA, in0=DTT[:, c, :], in1=AN, op=ALU.mult), reads=[bAN], writes=[bDTA])
                P.op('dve', lambda e: e.tensor_copy(out=DTH[:, 0, :], in_=DTA[:, 0:32]), reads=[bDTA], writes=[bDTH])
                P.op('dve', lambda e: e.tensor_tensor(out=DTH[:, 1, :], in0=DTA[:, 0:32], in1=DTH[:, 0, :], op=ALU.subtract), reads=[bDTA, bDTH], writes=[bDTH])
                for hl in range(2):
                    P.op('dve', lambda e, hl=hl: e.tensor_copy(out=DBH[:, hl, :].rearrange("p (h q) -> p h q", q=64), in_=bc(DTH[:, hl, :], [[1, 32], [0, 64]])),
                         reads=[bDTH], writes=[bDBC])

                def emit_R2(n):
                    g, d = n // 2, n % 2
                    Mk = TLE_f if d == 0 else TGE_f
                    rr = n % 3
                    P.op('pool' if n % 2 == 1 else 'dve', lambda e, rr=rr, d=d, g=g, Mk=Mk: e.tensor_tensor(
                        out=R2[rr], in0=bc(DTA[:, d * 32 + 4 * g:d * 32 + 4 * g + 4], [[1, 4], [0, 128]]),
                        in1=bc(Mk, [[0, 4], [1, 128]]), op=ALU.mult), reads=[bDTA], writes=[bR2[rr]])

                def emit_D(n):
                    d = n % 2
                    Uf = UA_f if d == 0 else UB_f
                    rr = n % 3
                    pb = 4 + n % 2
                    P.op('pe', lambda e, rr=rr, Uf=Uf, pb=pb: e.matmul(PS[pb], lhsT=Uf, rhs=R2[rr].rearrange("p h t -> p (h t)"), start=True, stop=True),
                         reads=[bR2[rr]], writes=[bPS[pb]])
                    P.op('act', lambda e, pb=pb, n=n: e.activation(out=E[n % 2].rearrange("p h t -> p (h t)"), in_=PS[pb], func=AF.Exp),
                         reads=[bPS[pb]], writes=[bE[n % 2]])

                def emit_M(n):
                    g, d = n // 2, n % 2
                    r = g % 2
                    P.op('dve', lambda e, n=n, r=r, d=d, g=g: e.tensor_tensor(
                        out=MP[r][d], in0=E[n % 2], in1=bc(CBTm[:, d, g, :], [[0, 4], [1, 128]]), op=ALU.mult),
                        reads=[bE[n % 2], bCBT], writes=[bMP[r][d]])

                def emit_y(g):
                    r = g % 2
                    for pe_ in range(2):
                        pp = 2 * g + pe_
                        bank = pp // 4
                        col = (pp % 4) * 128

                        def mmy(e, pp=pp, bank=bank, col=col, r=r, g=g, i=i):
                            o = PS[bank][:, col:col + 128]
                            e.matmul(o, lhsT=DSK[:, pp, :], rhs=XB[i][:, pp, :], start=True, stop=False)
                            ins = None
                            for ee in range(2):
                                h = 2 * pp + ee
                                hh = h - 4 * g
                                e.matmul(o, lhsT=XDA[:, h, :], rhs=MP[r][0][:, hh, :], start=False, stop=False)
                                ins = e.matmul(o, lhsT=XDB[:, h, :], rhs=MP[r][1][:, hh, :], start=False, stop=(ee == 1))
                            return ins
                        P.op('pe', mmy, reads=[bDSK, bXB[i], bXD, bMP[r][0], bMP[r][1]], writes=[bPS[bank]])

                emit_R2(0)
                emit_R2(1)

                def mm_small(e):
                    e.matmul(PS[3][:, 0:32], lhsT=UA_f, rhs=DTA[:, 0:32], start=True, stop=True)
                    e.matmul(PS[3][:, 32:64], lhsT=UB_f, rhs=DTA[:, 32:64], start=True, stop=True)
                    return e.matmul(PS[3][:, 64:128], lhsT=ONES, rhs=DTA[:, 0:64], start=True, stop=True)
                P.op('pe', mm_small, reads=[bDTA], writes=[bPS[3]])
                P.op('act', lambda e: e.activation(out=EXPO, in_=PS[3][:, 0:128], func=AF.Exp), reads=[bPS[3]], writes=[bEXPO])
                for b2 in range(2):
                    def tr(e, b2=b2, i=i):
                        ins = None
                        for q in range(8):
                            ins = e.transpose(PSb[b2][:, q * 128:(q + 1) * 128], XB[i][:, b2 * 8 + q, :], ident_b)
                        return ins
                    P.op('pe', tr, reads=[bXB[i]], writes=[bPS[b2]])

                def trb(e, i=i):
                    ins = None
                    for q in range(8):
                        ins = e.transpose(PSb[2][:, q * 128:(q + 1) * 128], XB[i][:, 16 + q, :], ident_b)
                    return ins
                P.op('pe', trb, reads=[bXB[i]], writes=[bPS[2]])
                for b2 in range(2):
                    def mmc(e, b2=b2, i=i):
                        ins = None
                        for gq in range(4):
                            g = b2 * 4 + gq
                            ins = e.matmul(PS[4 + b2][:, gq * 128:(gq + 1) * 128], lhsT=XB[i][:, 16 + g, :], rhs=XB[i][:, 24 + g, :],
                                           start=True, stop=True)
                        return ins
                    P.op('pe', mmc, reads=[bXB[i]], writes=[bPS[4 + b2]])
                for rd in range(2):
                    for b2 in range(2):
                        b4 = rd * 2 + b2

                        def mme(e, b4=b4, b2=b2):
                            ins = None
                            for q in range(4):
                                pp = b4 * 4 + q
                                e.matmul(PS[6 + b2][:, q * 128:(q + 1) * 128], lhsT=DBH[:, 0, pp * 128:(pp + 1) * 128], rhs=CMB[:, 3, :],
                                         start=True, stop=False)
                                ins = e.matmul(PS[6 + b2][:, q * 128:(q + 1) * 128], lhsT=DBH[:, 1, pp * 128:(pp + 1) * 128], rhs=CMB[:, 3, :],
                                               start=False, stop=True)
                            return ins
                        P.op('pe', mme, reads=[bDBC], writes=[bPS[6 + b2]])
                        P.op('act', lambda e, b4=b4, b2=b2: e.activation(out=EA[:, b4 * 512:(b4 + 1) * 512], in_=PS[6 + b2], func=AF.Exp),
                             reads=[bPS[6 + b2]], writes=[bEA])
                    if rd == 0:
                        for b2 in range(2):
                            for d, Mk in ((0, TLE_f), (1, TGE_f)):
                                P.op('dve', lambda e, b2=b2, d=d, Mk=Mk: e.tensor_tensor(
                                    out=CBTm[:, d, b2 * 4:(b2 + 1) * 4, :], in0=PS[4 + b2].rearrange("p (g t) -> p g t", t=128),
                                    in1=bc(Mk, [[0, 4], [1, 128]]), op=ALU.mult), reads=[bPS[4 + b2]], writes=[bCBT])
                        emit_D(0)
                        emit_D(1)
                P.op('dve', lambda e, c=c: e.tensor_tensor(out=FAC, in0=DTT[:, c, :], in1=EXPO[:, 0:64], op=ALU.mult), reads=[bEXPO], writes=[bFAC])
                P.op('dve', lambda e, c=c: e.tensor_copy(out=DECB[:, c, :], in_=EXPO[:, 96:128]), reads=[bEXPO], writes=[bDECB])
                for b2 in range(2):
                    src3 = bass.AP(PSb[b2].tensor, PSb[b2].offset, [list(PSb[b2].ap[0]), [128, 8], [64, 2], [1, 64]])
                    for d, X in ((0, XDA), (1, XDB)):
                        fac = bass.AP(DTT.tensor, DTT[:, c, :].offset + d * 32 + 16 * b2, [list(DTT.ap[0]), [2, 8], [1, 2], [0, 64]])
                        P.op('dve', lambda e, X=X, b2=b2, src3=src3, fac=fac: e.tensor_tensor(out=pad_ap(X, 16 * b2, 16), in0=src3, in1=fac, op=ALU.mult),
                             reads=[bPS[b2]], writes=[bXD])
                    for d, X in ((0, XWA), (1, XWB)):
                        P.op('dve', lambda e, X=X, b2=b2, d=d: e.tensor_tensor(
                            out=X[:, b2 * 1024:(b2 + 1) * 1024].rearrange("p (h q) -> p h q", q=64),
                            in0=PSb[b2].rearrange("p (h q) -> p h q", q=64),
                            in1=bc(FAC[:, d * 32 + 16 * b2:d * 32 + 16 * b2 + 16], [[1, 16], [0, 64]]), op=ALU.mult),
                            reads=[bPS[b2], bFAC], writes=[bXW])
                P.op('act', lambda e: e.activation(out=BT, in_=PSb[2], func=AF.Copy), reads=[bPS[2]], writes=[bBT])
                for n in range(2, 18):
                    emit_M(n - 2)
                    if n < 16:
                        emit_R2(n)
                        emit_D(n)
                    if (n - 2) % 2 == 1:
                        emit_y((n - 2) // 2)
                for hb in range(2):
                    for b2 in range(2):
                        def mmi(e, hb=hb, b2=b2, i=i):
                            ins = None
                            for q in range(4):
                                pp = hb * 8 + b2 * 4 + q
                                ins = e.matmul(PS[6 + b2][:, q * 128:(q + 1) * 128], lhsT=SAb[:, pp * 128:(pp + 1) * 128],
                                               rhs=XB[i][:, 24 + pp // 2, :], start=True, stop=True)
                            return ins
                        P.op('pe', mmi, reads=[bSAb, bXB[i]], writes=[bPS[6 + b2]])
                        cs = (hb * 2 + b2) * 512
                        P.op('dve', lambda e, b2=b2, cs=cs: e.tensor_tensor(out=T1[:, cs:cs + 512], in0=PS[6 + b2], in1=EA[:, cs:cs + 512], op=ALU.mult),
                             reads=[bPS[6 + b2], bEA], writes=[bT1])
                        yb = hb * 2 + b2
                        P.op('dve', lambda e, yb=yb, cs=cs, yi=yi: e.tensor_tensor(out=YP[yi][:, cs:cs + 512], in0=PS[yb], in1=T1[:, cs:cs + 512], op=ALU.add),
                             reads=[bPS[yb], bT1], writes=[bYP[yi]])
                P.dma('sp', f'st{yi}', lambda e, yi=yi, c=c: e.dma_start(out=yd_v[:, :, c * 128:(c + 1) * 128], in_=YP[yi].rearrange("p (b t) -> p b t", t=128)),
                      reads=[bYP[yi]])
                for b4 in range(4):
                    def mms(e, b4=b4):
                        ins = None
                        for q in range(2):
                            g = b4 * 2 + q
                            ins = e.matmul(PS[b4][:, q * 256:(q + 1) * 256], lhsT=BT[:, g * 128:(g + 1) * 128], rhs=XWA[:, g * 256:(g + 1) * 256],
                                           start=True, stop=True)
                        return ins
                    P.op('pe', mms, reads=[bBT, bXW], writes=[bPS[b4]])
                P.op('dve', lambda e: e.tensor_tensor(out=SA.rearrange("p (h q) -> p h q", q=64), in0=SA.rearrange("p (h q) -> p h q", q=64),
                                                      in1=bc(EXPO[:, 64:96], [[1, 32], [0, 64]]), op=ALU.mult), reads=[bEXPO, bSA], writes=[bSA])
                for b4 in range(4):
                    def mmsb(e, b4=b4):
                        ins = None
                        for q in range(2):
                            g = b4 * 2 + q
                            ins = e.matmul(PS[4 + b4][:, q * 256:(q + 1) * 256], lhsT=BT[:, g * 128:(g + 1) * 128], rhs=XWB[:, g * 256:(g + 1) * 256],
                                           start=True, stop=True)
                        return ins
                    P.op('pe', mmsb, reads=[bBT, bXW], writes=[bPS[4 + b4]])
                for b4 in range(4):
                    P.op('dve', lambda e, b4=b4: e.tensor_tensor(out=SA[:, b4 * 512:(b4 + 1) * 512], in0=SA[:, b4 * 512:(b4 + 1) * 512], in1=PS[b4], op=ALU.add),
                         reads=[bPS[b4], bSA], writes=[bSA])
                    P.op('act', lambda e, b4=b4, yi=yi: e.activation(out=SBL[yi][:, b4 * 512:(b4 + 1) * 512], in_=PS[4 + b4], func=AF.Copy),
                         reads=[bPS[4 + b4]], writes=[bSBL[yi]])
                P.op('act', lambda e: e.activation(out=SAb, in_=SA, func=AF.Copy), reads=[bSA], writes=[bSAb])
                P.dma('sp', f'st{2 + yi}', lambda e, yi=yi, c=c: e.dma_start(out=SB_d[c], in_=SBL[yi]), reads=[bSBL[yi]])
                if debug and l == 0 and c == 1:
                    P.dma('sp', 'st4', lambda e: e.dma_start(out=DBG["sa"], in_=SA), reads=[bSA])


        def sweepB(l, es, last):
            CT = [sb(f"bCT{i}", [128, 8, 128], BF16, es) for i in range(2)]
            YPl = [sb(f"bYP{i}", [128, DIN], F32, es) for i in range(2)]
            SBLl = [sb(f"bSBL{i}", [128, DIN], F32, es) for i in range(2)]
            ZSl = [sb(f"bZS{i}", [128, 16, 128], BF16, es) for i in range(2)]
            YNs = [sb(f"bYN{i}", [128, 16, 128], BF16, es) for i in range(2)]
            SBr = sb("bSBr", [128, DIN], F32, es)
            SBb = sb("bSBb", [128, DIN], BF16, es)
            CC2 = sb("bCC2", [128, 2, DIN], F32, es)
            AN = sb("bAN", [128, 32], F32, es)
            DTA = sb("bDTA", [128, 32], F32, es)
            DBH = sb("bDBH", [128, 2, DIN], BF16, es)
            DTH = sb("bDTH", [128, 2, 32], BF16, es)
            bDTH = B_()
            EA = sb("bEA", [128, DIN], F32, es)
            T1 = sb("bT1", [128, DIN], F32, es)
            YZ2 = [sb(f"bYZ{i}", [128, DIN], F32, es) for i in range(2)]
            SQ = sb("bSQ", [128, DIN], BF16, es)
            RST2 = [sb(f"bRST{i}", [128, 8, 128], F32, es) for i in range(2)]
            bYZ2, bRST2 = [B_(), B_()], [B_(), B_()]
            NW = sb("bNW", [128, 16], F32, es)
            bCT, bYPl, bSBLl, bZSl, bYNs = [B_(), B_()], [B_(), B_()], [B_(), B_()], [B_(), B_()], [B_(), B_()]
            bSBr, bSBb, bCC2, bAN, bDTA, bDBC, bEA, bT1, bYZ_, bSQ, bRST_, bNW = [B_() for _ in range(12)]
            bCCI, bCCO = B_(), B_()
            bPS = [B_() for _ in range(8)]
            P.dma('sp', 'st0', lambda e: e.dma_start(out=CCI_d, in_=SA), writes=[bCCI])
            P.dma('pool', 'cc', lambda e: e.collective_compute("AllGather", ALU.bypass, replica_groups=pairs,
                                                               ins=[CCI_d.opt()], outs=[CCO_d.opt()]),
                  reads=[bCCI], writes=[bCCO], inc=1)
            P.dma('sp', 'ld0', lambda e: e.dma_start(out=AN, in_=a_log[l][:, 32:64]), writes=[bAN])
            P.op('act', lambda e: e.activation(out=AN, in_=AN, func=AF.Exp), reads=[bAN], writes=[bAN])
            P.op('dve', lambda e: e.tensor_scalar_mul(out=AN, in0=AN, scalar1=-1.0), reads=[bAN], writes=[bAN])
            P.dma('sp', 'ld1', lambda e: e.dma_start(out=NW, in_=norm_w[l]), writes=[bNW])
            P.op('dve', lambda e: e.memset(SBr, 0.0), writes=[bSBr])
            P.op('dve', lambda e: e.memset(SBb, 0.0), writes=[bSBb])
            order = ([] if last else [1, 0]) + list(range(NCH - 1, 1, -1))
            EA2 = [EA, sb("bEA2", [128, DIN], F32, es)]
            bEA2 = [B_(), B_()]

            def emit_loads(it, c):
                i = it % 2
                co = chunk_col(c)
                P.dma('sp', f'ld{3 + i}', lambda e, i=i, co=co: e.dma_start(out=CT[i], in_=xbc_v[:, 24:32, co:co + 128]), writes=[bCT[i]])
                P.dma('sp', f'ld{5 + i}', lambda e, i=i, c=c: e.dma_start(out=YPl[i].rearrange("p (b t) -> p b t", t=128), in_=yd_v[:, :, c * 128:(c + 1) * 128]),
                      writes=[bYPl[i]])
                P.dma('sp', f'ld{7 + i}', lambda e, i=i, c=c: e.dma_start(out=SBLl[i], in_=SB_d[c]), writes=[bSBLl[i]])
                P.dma('sp', f'ld{9 + i}', lambda e, i=i, c=c: e.dma_start(out=ZSl[i], in_=zs_v[:, :, c * 128:(c + 1) * 128]), writes=[bZSl[i]])

            def emit_EA(it, c):
                sl = it % 2
                P.op('dve', lambda e, c=c: e.tensor_tensor(out=DTA, in0=DTT[:, c, 32:64], in1=AN, op=ALU.mult), reads=[bAN], writes=[bDTA])
                P.op('dve', lambda e: e.tensor_copy(out=DTH[:, 0, :], in_=DTA), reads=[bDTA], writes=[bDTH])
                P.op('dve', lambda e: e.tensor_tensor(out=DTH[:, 1, :], in0=DTA, in1=DTH[:, 0, :], op=ALU.subtract), reads=[bDTA, bDTH], writes=[bDTH])
                for hl in range(2):
                    P.op('dve', lambda e, hl=hl: e.tensor_copy(out=DBH[:, hl, :].rearrange("p (h q) -> p h q", q=64), in_=bc(DTH[:, hl, :], [[1, 32], [0, 64]])),
                         reads=[bDTH], writes=[bDBC])
                for b4 in range(4):
                    def mme(e, b4=b4):
                        ins = None
                        for q in range(4):
                            pp = b4 * 4 + q
                            e.matmul(PS[b4][:, q * 128:(q + 1) * 128], lhsT=DBH[:, 0, pp * 128:(pp + 1) * 128], rhs=CMB[:, 4, :],
                                     start=True, stop=False)
                            ins = e.matmul(PS[b4][:, q * 128:(q + 1) * 128], lhsT=DBH[:, 1, pp * 128:(pp + 1) * 128], rhs=CMB[:, 4, :],
                                           start=False, stop=True)
                        return ins
                    P.op('pe', mme, reads=[bDBC], writes=[bPS[b4]])
                    P.op('act', lambda e, b4=b4, sl=sl: e.activation(out=EA2[sl][:, b4 * 512:(b4 + 1) * 512], in_=PS[b4], func=AF.Exp),
                         reads=[bPS[b4]], writes=[bEA2[sl]])

            emit_loads(0, order[0])
            emit_EA(0, order[0])
            for it, c in enumerate(order):
                i = it % 2
                sl = it % 2
                if c == NCH - 1:
                    P.dma('sp', 'ld2', lambda e: e.dma_start(out=CC2, in_=CCO_d.rearrange("(r p) n -> p r n", p=128)),
                          reads=[bCCO], writes=[bCC2])
                    P.op('dve', lambda e: e.tensor_scalar(out=SBr, in0=CC2[:, 0, :], scalar1=SELM[:, 0:1], scalar2=None, op0=ALU.mult),
                         reads=[bCC2], writes=[bSBr])
                    P.op('dve', lambda e: e.scalar_tensor_tensor(out=SBr, in0=CC2[:, 1, :], scalar=SELM[:, 1:2], in1=SBr,
                                                                 op0=ALU.mult, op1=ALU.add), reads=[bCC2, bSBr], writes=[bSBr])
                    P.op('act', lambda e: e.activation(out=SBb, in_=SBr, func=AF.Copy), reads=[bSBr], writes=[bSBb])
                if it + 1 < len(order):
                    emit_loads(it + 1, order[it + 1])
                for b4 in range(4):
                    def mmi(e, b4=b4, i=i):
                        ins = None
                        for q in range(4):
                            pp = b4 * 4 + q
                            ins = e.matmul(PS[4 + b4][:, q * 128:(q + 1) * 128], lhsT=SBb[:, pp * 128:(pp + 1) * 128],
                                           rhs=CT[i][:, pp // 2, :], start=True, stop=True)
                        return ins
                    P.op('pe', mmi, reads=[bSBb, bCT[i]], writes=[bPS[4 + b4]])
                    P.op('dve', lambda e, b4=b4, sl=sl: e.tensor_tensor(out=T1[:, b4 * 512:(b4 + 1) * 512], in0=PS[4 + b4], in1=EA2[sl][:, b4 * 512:(b4 + 1) * 512], op=ALU.mult),
                         reads=[bPS[4 + b4], bEA2[sl]], writes=[bT1])
                P.op('dve', lambda e, i=i: e.tensor_tensor(out=T1, in0=T1, in1=YPl[i], op=ALU.add), reads=[bT1, bYPl[i]], writes=[bT1])
                if debug and l == 0:
                    P.dma('sp', 'st5', lambda e, c=c: e.dma_start(out=DBG["yt"].rearrange("(blk p) t -> p blk t", p=128)[:, :, c * 128:(c + 1) * 128],
                                                                 in_=T1.rearrange("p (b t) -> p b t", t=128)), reads=[bT1])
                P.op('dve', lambda e, c=c: e.tensor_tensor(out=SBr.rearrange("p (h q) -> p h q", q=64), in0=SBr.rearrange("p (h q) -> p h q", q=64),
                                                           in1=bc(DECB[:, c, :], [[1, 32], [0, 64]]), op=ALU.mult), reads=[bSBr], writes=[bSBr])
                P.op('dve', lambda e, i=i: e.tensor_tensor(out=SBr, in0=SBr, in1=SBLl[i], op=ALU.add), reads=[bSBr, bSBLl[i]], writes=[bSBr])
                P.op('act', lambda e: e.activation(out=SBb, in_=SBr, func=AF.Copy), reads=[bSBr], writes=[bSBb])
                if it + 1 < len(order) and order[it + 1] != NCH - 1:
                    emit_EA(it + 1, order[it + 1])
                YZ, RST, bYZ, bRST = YZ2[i], RST2[i], bYZ2[i], bRST2[i]
                P.op('dve', lambda e, i=i, YZ=YZ: e.tensor_tensor(out=YZ, in0=T1, in1=ZSl[i].rearrange("p b t -> p (b t)"), op=ALU.mult),
                     reads=[bT1, bZSl[i]], writes=[bYZ])
                P.op('act', lambda e, YZ=YZ: e.activation(out=bass.AP(SQ.tensor, SQ.offset, [list(SQ.ap[0]), [128, 8], [1024, 2], [1, 128]]),
                                                          in_=YZ.rearrange("p (g r t) -> p g r t", r=2, t=128), func=AF.Square), reads=[bYZ], writes=[bSQ])
                for b2 in range(2):
                    def mmn(e, b2=b2):
                        e.matmul(PS[4 + b2], lhsT=ONESB, rhs=SQ[:, b2 * 512:(b2 + 1) * 512], start=True, stop=False)
                        return e.matmul(PS[4 + b2], lhsT=ONESB, rhs=SQ[:, 1024 + b2 * 512:1024 + (b2 + 1) * 512], start=False, stop=True)
                    P.op('pe', mmn, reads=[bSQ], writes=[bPS[4 + b2]])
                    P.op('act', lambda e, b2=b2, RST=RST: e.activation(out=RST[:, b2 * 4:(b2 + 1) * 4, :].rearrange("p g t -> p (g t)"), in_=PS[4 + b2],
                                                                       func=AF.Ln, scale=1.0 / 256, bias=EPS), reads=[bPS[4 + b2]], writes=[bRST])
                P.op('act', lambda e, RST=RST: e.activation(out=RST, in_=RST, func=AF.Exp, scale=-0.5), reads=[bRST], writes=[bRST])
                P.op('pool', lambda e, i=i, YZ=YZ, RST=RST: e.tensor_tensor(out=YNs[i].rearrange("p (g r) t -> p g r t", r=2),
                                                                            in0=YZ.rearrange("p (g r t) -> p g r t", r=2, t=128),
                                                                            in1=bc(RST, [[128, 8], [0, 2], [1, 128]]), op=ALU.mult),
                     reads=[bYZ, bRST], writes=[bYNs[i]])
                if it + 1 < len(order) and order[it + 1] == NCH - 1:
                    emit_EA(it + 1, order[it + 1])
                P.dma('sp', f'st{1 + i}', lambda e, i=i, c=c: e.dma_start(out=ynd_v[:, :, c * 128:(c + 1) * 128], in_=YNs[i]), reads=[bYNs[i]])


        gs_v = GS_d.rearrange("(blk p) t -> p blk t", p=128)
        pm2_v = PM2_d.rearrange("(blk p) t -> p blk t", p=128)
        gd_v = G_d.rearrange("(blk p) t -> p blk t", p=128)

        def load_weight(dst, src_v, nk, ncols, keybase, bdst):
            n = 0
            for k0 in range(0, nk, 8):
                k1 = min(nk, k0 + 8)
                for c0 in range(0, ncols, 512):
                    c1 = min(ncols, c0 + 512)
                    P.dma('pool', f'w{keybase + n % 4}', lambda e, k0=k0, k1=k1, c0=c0, c1=c1: e.dma_start(
                        out=dst[:, k0:k1, c0:c1], in_=src_v[:, k0:k1, c0:c1]), writes=[bdst])
                    n += 1

        def mixer_dense(l, es, last):
            WSO = sb("dWSO", [128, 16, D], BF16, es)
            WPO = sb("dWPO", [128, 8, D], BF16, es)
            WO = sb("dWO", [128, 8, D], BF16, es)
            YNt = [sb(f"dYN{i}", [128, 16, 512], BF16, es) for i in range(2)]
            PMt = [sb(f"dPM{i}", [128, 8, 512], BF16, es) for i in range(2)]
            GSt = sb("dGS", [128, 16, 512], BF16, es)
            Xt = sb("dX", [128, KD, 512], F32, es)
            MT = sb("dMT", [128, KD, 512], BF16, es)
            Ta = [sb(f"dTa{i}", [128, 512], F32, es) for i in range(2)]
            Tb = [sb(f"dTb{i}", [128, 512], F32, es) for i in range(2)]
            bWSO, bWPO, bWO, bGS, bX, bMT = B_(), B_(), B_(), B_(), B_(), B_()
            bYN, bPM, bTa, bTb = [B_(), B_()], [B_(), B_()], [B_(), B_()], [B_(), B_()]
            bPS = [B_() for _ in range(8)]
            load_weight(WSO, w_ssd_out[l].rearrange("(k p) n -> p k n", p=128), 16, D, 0, bWSO)
            NWm = sb("dNW", [128, 16], F32, es)
            bNWm = B_()
            P.dma('sp', 'ld6', lambda e: e.dma_start(out=NWm, in_=norm_w[l]), writes=[bNWm])
            for kb in range(16):
                P.op('dve', lambda e, kb=kb: e.tensor_scalar(out=WSO[:, kb, :], in0=WSO[:, kb, :], scalar1=NWm[:, kb:kb + 1], scalar2=None, op0=ALU.mult),
                     reads=[bWSO, bNWm], writes=[bWSO])
            load_weight(WPO, w_pool_out[l].rearrange("(k p) n -> p k n", p=128), 8, D, 0, bWPO)
            load_weight(WO, w_out[l].rearrange("(k p) n -> p k n", p=128), 8, D, 0, bWO)
            tiles = ([] if last else [(0, 256, 1)]) + [(256 + 512 * i, 512, 0) for i in range(4)]
            np_ = 0
            for it, (t0, w, mc) in enumerate(tiles):
                i = it % 2
                P.dma('sp', f'ld{i}', lambda e, i=i, t0=t0, w=w: e.dma_start(out=YNt[i][:, :, :w], in_=ynd_v[:, :, t0:t0 + w]), writes=[bYN[i]])
                P.dma('sp', f'ld{2 + i}', lambda e, i=i, t0=t0, w=w: e.dma_start(out=PMt[i][:, :, :w], in_=pm2_v[:, :, t0:t0 + w]), writes=[bPM[i]])
                P.dma('sp', 'ld4', lambda e, t0=t0, w=w: e.dma_start(out=GSt[:, :, :w], in_=gs_v[:, :, t0:t0 + w]), writes=[bGS])
                P.dma('sp', 'ld5', lambda e, t0=t0, w=w: e.dma_start(out=Xt[:, :, :w], in_=xtd_v[:, :, t0:t0 + w]), writes=[bX])
                for db in range(KD):
                    pa = np_ % 4
                    pb = 4 + np_ % 4
                    r = np_ % 2
                    np_ += 1

                    def mma(e, db=db, i=i, w=w, pa=pa):
                        ins = None
                        for kb in range(16):
                            ins = e.matmul(PS[pa][:, :w], lhsT=WSO[:, kb, db * 128:(db + 1) * 128], rhs=YNt[i][:, kb, :w],
                                           start=(kb == 0), stop=(kb == 15))
                        return ins
                    P.op('pe', mma, reads=[bWSO, bYN[i]], writes=[bPS[pa]])

                    def mmb(e, db=db, i=i, w=w, pb=pb):
                        ins = None
                        for kb in range(8):
                            ins = e.matmul(PS[pb][:, :w], lhsT=WPO[:, kb, db * 128:(db + 1) * 128], rhs=PMt[i][:, kb, :w],
                                           start=(kb == 0), stop=(kb == 7))
                        return ins
                    P.op('pe', mmb, reads=[bWPO, bPM[i]], writes=[bPS[pb]])
                    P.op('dve', lambda e, r=r, db=db, w=w, pa=pa: e.tensor_tensor(out=Ta[r][:, :w], in0=PS[pa][:, :w], in1=GSt[:, db, :w], op=ALU.mult),
                         reads=[bPS[pa], bGS], writes=[bTa[r]])
                    P.op('dve', lambda e, r=r, db=db, w=w, pb=pb: e.tensor_tensor(out=Tb[r][:, :w], in0=PS[pb][:, :w], in1=GSt[:, 8 + db, :w], op=ALU.mult),
                         reads=[bPS[pb], bGS], writes=[bTb[r]])
                    P.op('pool', lambda e, r=r, db=db, w=w: e.tensor_tensor(out=MT[:, db, :w], in0=Ta[r][:, :w], in1=Tb[r][:, :w], op=ALU.add),
                         reads=[bTa[r], bTb[r]], writes=[bMT])
                for db in range(KD):
                    pa = np_ % 4
                    np_ += 1

                    def mmo(e, db=db, w=w, pa=pa):
                        ins = None
                        for kb in range(8):
                            ins = e.matmul(PS[pa][:, :w], lhsT=WO[:, kb, db * 128:(db + 1) * 128], rhs=MT[:, kb, :w],
                                           start=(kb == 0), stop=(kb == 7))
                        return ins
                    P.op('pe', mmo, reads=[bWO, bMT], writes=[bPS[pa]])
                    P.op('dve', lambda e, db=db, w=w, pa=pa, mc=mc: e.scalar_tensor_tensor(
                        out=Xt[:, db, :w], in0=PS[pa][:, :w], scalar=MOD[l][:, 16 + db, mc:mc + 1], in1=Xt[:, db, :w],
                        op0=ALU.mult, op1=ALU.add), reads=[bPS[pa], bX], writes=[bX])
                P.dma('sp', 'st0', lambda e, t0=t0, w=w: e.dma_start(out=xtd_v[:, :, t0:t0 + w], in_=Xt[:, :, :w]), reads=[bX])

        def ffn_up(l, HT, es, last):
            Wa = [sb(f"fWa{i}", [128, KD, 512], BF16, es) for i in range(2)]
            Wb = [sb(f"fWb{i}", [128, KD, 512], BF16, es) for i in range(2)]
            SI = [sb(f"fSI{i}", [128, 512], F32, es) for i in range(2)]
            GST = [sb(f"fG{i}", [128, TT], BF16, es) for i in range(2)]
            bWa, bWb, bSI, bGST = [B_(), B_()], [B_(), B_()], [B_(), B_()], [B_(), B_()]
            bPS = [B_() for _ in range(8)]
            bHT = B_()
            wv = w_gate_up[l].rearrange("(k p) n -> p k n", p=128)
            tiles = ([] if last else [(0, 256)]) + [(256 + 512 * i, 512) for i in range(4)]
            np_ = 0
            ng = 0
            for jb in range(6):
                j = jb % 2
                ncol = 512 if jb < 5 else 256
                P.dma('pool', f'w{j}', lambda e, j=j, jb=jb, ncol=ncol: e.dma_start(out=Wa[j][:, :, :ncol], in_=wv[:, :, jb * 512:jb * 512 + ncol]),
                      writes=[bWa[j]])
                P.dma('pool', f'w{2 + j}', lambda e, j=j, jb=jb, ncol=ncol: e.dma_start(out=Wb[j][:, :, :ncol], in_=wv[:, :, FFH + jb * 512:FFH + jb * 512 + ncol]),
                      writes=[bWb[j]])
                for q in range(ncol // 128):
                    hb = jb * 4 + q
                    gi = ng % 2
                    ng += 1
                    for (t0, w) in tiles:
                        pa = np_ % 4
                        pb = 4 + np_ % 4
                        r = np_ % 2
                        np_ += 1

                        def mma(e, j=j, q=q, t0=t0, w=w, pa=pa):
                            ins = None
                            for k in range(KD):
                                ins = e.matmul(PS[pa][:, :w], lhsT=Wa[j][:, k, q * 128:(q + 1) * 128], rhs=HT[:, k, t0:t0 + w],
                                               start=(k == 0), stop=(k == KD - 1))
                            return ins
                        P.op('pe', mma, reads=[bWa[j], bHT], writes=[bPS[pa]])

                        def mmb(e, j=j, q=q, t0=t0, w=w, pb=pb):
                            ins = None
                            for k in range(KD):
                                ins = e.matmul(PS[pb][:, :w], lhsT=Wb[j][:, k, q * 128:(q + 1) * 128], rhs=HT[:, k, t0:t0 + w],
                                               start=(k == 0), stop=(k == KD - 1))
                            return ins
                        P.op('pe', mmb, reads=[bWb[j], bHT], writes=[bPS[pb]])
                        P.op('act', lambda e, r=r, w=w, pa=pa: e.activation(out=SI[r][:, :w], in_=PS[pa][:, :w], func=AF.Silu),
                             reads=[bPS[pa]], writes=[bSI[r]])
                        P.op('dve', lambda e, r=r, w=w, pb=pb, gi=gi, t0=t0: e.tensor_tensor(out=GST[gi][:, t0:t0 + w], in0=PS[pb][:, :w], in1=SI[r][:, :w], op=ALU.mult),
                             reads=[bPS[pb], bSI[r]], writes=[bGST[gi]])
                    clo = NCTX if last else 0
                    P.dma('sp', f'st{gi}', lambda e, gi=gi, hb=hb, clo=clo: e.dma_start(out=G_d[hb * 128:(hb + 1) * 128, clo:TT], in_=GST[gi][:, clo:TT]),
                          reads=[bGST[gi]])

        def ffn_down(l, es, last):
            WD = sb("gWD", [128, 22, D], BF16, es)
            Gt = [sb(f"gG{i}", [128, 22, 512], BF16, es) for i in range(2)]
            Xt = [sb(f"gX{i}", [128, KD, 512], F32, es) for i in range(2)]
            SQ = sb("gSQ", [128, KD, 512], F32, es)
            RS = sb("gRS", [128, 512], F32, es)
            GF = sb("gGF", [128, KD], F32, es)
            HX = sb("gHX", [128, KD, 2], F32, es)
            HX2 = sb("gHX2", [128, 2, 16], F32, es)
            bWD, bSQ, bRS, bGF, bHX, bHX2, bHCI, bHCO = B_(), B_(), B_(), B_(), B_(), B_(), B_(), B_()
            bG, bX = [B_(), B_()], [B_(), B_()]
            bPS = [B_() for _ in range(8)]
            load_weight(WD, w_down[l].rearrange("(k p) n -> p k n", p=128), 22, D, 0, bWD)
            if last:
                P.dma('sp', 'ld6', lambda e: e.dma_start(out=GF, in_=g_final), writes=[bGF])
            tiles = ([] if last else [(0, 256, 1)]) + [(256 + 512 * i, 512, 0) for i in range(4)]
            np_ = 0
            for it, (t0, w, mc) in enumerate(tiles):
                i = it % 2
                P.dma('sp', f'ld{i}', lambda e, i=i, t0=t0, w=w: e.dma_start(out=Gt[i][:, :, :w], in_=gd_v[:, :, t0:t0 + w]), writes=[bG[i]])
                P.dma('sp', f'ld{2 + i}', lambda e, i=i, t0=t0, w=w: e.dma_start(out=Xt[i][:, :, :w], in_=xtd_v[:, :, t0:t0 + w]), writes=[bX[i]])
                for db in range(KD):
                    pa = np_ % 8 if not last else np_ % 4
                    np_ += 1

                    def mmd(e, db=db, i=i, w=w, pa=pa):
                        ins = None
                        for kb in range(22):
                            ins = e.matmul(PS[pa][:, :w], lhsT=WD[:, kb, db * 128:(db + 1) * 128], rhs=Gt[i][:, kb, :w],
                                           start=(kb == 0), stop=(kb == 21))
                        return ins
                    P.op('pe', mmd, reads=[bWD, bG[i]], writes=[bPS[pa]])
                    P.op('dve', lambda e, db=db, i=i, w=w, pa=pa, mc=mc: e.scalar_tensor_tensor(
                        out=Xt[i][:, db, :w], in0=PS[pa][:, :w], scalar=MOD[l][:, 40 + db, mc:mc + 1], in1=Xt[i][:, db, :w],
                        op0=ALU.mult, op1=ALU.add), reads=[bPS[pa], bX[i]], writes=[bX[i]])
                if not last:
                    P.dma('sp', f'st{i}', lambda e, i=i, t0=t0, w=w: e.dma_start(out=xtd_v[:, :, t0:t0 + w], in_=Xt[i][:, :, :w]), reads=[bX[i]])
                    if t0 + w == TT:
                        for hh in range(2):
                            P.op('dve', lambda e, i=i, hh=hh: e.tensor_copy(out=HX[:, :, hh], in_=Xt[i][:, :, 511 - hh]), reads=[bX[i]], writes=[bHX])
                        P.dma('sp', 'st2', lambda e: e.dma_start(out=HCI_d, in_=HX.rearrange("p k t -> p (k t)")), reads=[bHX], writes=[bHCI])
                        P.dma('pool', 'cc', lambda e: e.collective_compute("AllGather", ALU.bypass, replica_groups=pairs,
                                                                           ins=[HCI_d.opt()], outs=[HCO_d.opt()]),
                              reads=[bHCI], writes=[bHCO], inc=1)
                        P.dma('sp', 'ld4', lambda e: e.dma_start(out=HX2, in_=HCO_d.rearrange("(r p) n -> p r n", p=128)), reads=[bHCO], writes=[bHX2])
                        P.op('dve', lambda e: e.tensor_scalar(out=HX.rearrange("p k t -> p (k t)"), in0=HX2[:, 0, :], scalar1=SELM[:, 0:1], scalar2=None, op0=ALU.mult),
                             reads=[bHX2], writes=[bHX])
                        P.op('dve', lambda e: e.scalar_tensor_tensor(out=HX.rearrange("p k t -> p (k t)"), in0=HX2[:, 1, :], scalar=SELM[:, 1:2],
                                                                     in1=HX.rearrange("p k t -> p (k t)"), op0=ALU.mult, op1=ALU.add),
                             reads=[bHX2, bHX], writes=[bHX])
                        P.dma('sp', 'st3', lambda e: e.dma_start(out=xtd_v[:, :, TT:TT + 2], in_=HX), reads=[bHX])
                else:
                    P.op('act', lambda e, i=i: e.activation(out=SQ, in_=Xt[i], func=AF.Square), reads=[bX[i]], writes=[bSQ])

                    def mmq(e):
                        ins = None
                        for k in range(KD):
                            ins = e.matmul(PS[4], lhsT=ONES, rhs=SQ[:, k, :], start=(k == 0), stop=(k == KD - 1))
                        return ins
                    P.op('pe', mmq, reads=[bSQ], writes=[bPS[4]])
                    P.op('act', lambda e: e.activation(out=RS, in_=PS[4], func=AF.Ln, scale=1.0 / D, bias=EPS), reads=[bPS[4]], writes=[bRS])
                    P.op('act', lambda e: e.activation(out=RS, in_=RS, func=AF.Exp, scale=-0.5), reads=[bRS], writes=[bRS])
                    for k in range(KD):
                        P.op('dve', lambda e, k=k, i=i: e.scalar_tensor_tensor(out=SQ[:, k, :], in0=Xt[i][:, k, :], scalar=GF[:, k:k + 1], in1=RS,
                                                                               op0=ALU.mult, op1=ALU.mult), reads=[bX[i], bRS, bGF, bSQ], writes=[bSQ])
                    P.dma('sp', f'st{i}', lambda e, t0=t0: e.dma_start(out=yT_out.rearrange("(k p) t -> p k t", p=128)[:, :, t0 - NCTX:t0 - NCTX + 512], in_=SQ),
                          reads=[bSQ])

        for l in range(DEPTH):
            last = (l == DEPTH - 1)
            with ExitStack() as es:
                HT = sb("HT", [128, KD, TTH], BF16, es)
                with ExitStack() as es1:
                    norm_mod(l, 1, HT, es1)
                    P.flush()
                if debug and l == 0:
                    P.dma('sp', 'st0', lambda e: e.dma_start(out=DBG["ht"], in_=HT))
                    P.flush()
                if stop_after == 1:
                    return nc
                with ExitStack() as es2:
                    inproj(l, HT, es2)
                    P.flush()
            if stop_after == 2:
                return nc
            with ExitStack() as es:
                sweepA(l, es)
                P.flush()
            if stop_after == 3:
                return nc
            with ExitStack() as es:
                sweepB(l, es, last)
                P.flush()
            if stop_after == 4:
                return nc
            with ExitStack() as es:
                mixer_dense(l, es, last)
                P.flush()
            if stop_after == 5:
                return nc
            with ExitStack() as es:
                HT = sb("HT2", [128, KD, TTH], BF16, es)
                with ExitStack() as es1:
                    norm_mod(l, 2, HT, es1)
                    P.flush()
                with ExitStack() as es1:
                    ffn_up(l, HT, es1, last)
                    P.flush()
            with ExitStack() as es:
                ffn_down(l, es, last)
                P.flush()
            if stop_after == 6:
                return nc
    return nc


def _pool_mats(L, k, rowlen):
    Pm = np.zeros((L, L), np.float32)
    inv = np.zeros(L, np.float32)
    for t in range(L):
        r0 = (t // rowlen) * rowlen
        tt = t - r0
        lo = min(max(tt - k // 2, 0), rowlen)
        hi = min(max(tt + k // 2, 0), rowlen)
        Pm[r0 + lo:r0 + hi, t] = 1.0
        Pm[t, t] -= (hi - lo)
        inv[t] = 1.0 / (hi - lo)
    return Pm, inv


def _consts(rev):
    k = np.arange(128)[:, None]
    j = np.arange(128)[None, :]
    cm = np.stack([(k == j), (k > j), (k < j), (k <= j), (k >= j)], axis=1).astype(np.float32)
    pml = np.zeros((128, 4, 128), np.float32)
    pmc = np.zeros((128, 2, 4, 2, 128), np.float32)
    invc = np.zeros((4, 384), np.float32)
    for g, kk in enumerate((2, 4, 8, 16)):
        Pm, inv = _pool_mats(128, kk, 64)
        Pc, invcx = _pool_mats(256, kk, 256)
        if rev:
            Pm = Pm[::-1, ::-1]
            inv = inv[::-1]
            Pc = Pc[::-1, ::-1]
            invcx = invcx[::-1]
        pml[:, g, :] = Pm
        for tb_ in range(2):
            for tb in range(2):
                pmc[:, tb_, g, tb, :] = Pc[tb_ * 128:(tb_ + 1) * 128, tb * 128:(tb + 1) * 128]
        invc[g, :NCTX] = invcx
        invc[g, NCTX:] = inv
    invc = np.ascontiguousarray(np.broadcast_to(invc[None], (128, 4, 384)))
    return cm, pml, pmc, invc


def prep_inputs(inputs, n_cores=8):
    f = lambda a: np.ascontiguousarray(np.asarray(a, dtype=np.float32))
    inp = {k: np.asarray(v) for k, v in inputs.items()}
    pm = lambda v: f(v.reshape(-1, 128).T)

    def stack(fn):
        return f(np.stack([fn(l) for l in range(DEPTH)]))
    shared = dict(
        w_ada=f(inp["w_ada"]), w_ssd_out=f(inp["w_ssd_out"]), pool_w=f(inp["pool_w"]),
        w_pool_out=f(inp["w_pool_out"]), w_out=f(inp["w_out"]), w_gate_up=f(inp["w_gate_up"]),
        w_down=f(inp["w_down"]),
        b_ada=stack(lambda l: pm(inp["b_ada"][l])), g_mix=stack(lambda l: pm(inp["g_mix"][l])),
        g_ffn=stack(lambda l: pm(inp["g_ffn"][l])), conv_b=stack(lambda l: pm(inp["conv_b"][l])),
        norm_w=stack(lambda l: pm(inp["ssd_norm_w"][l])), pool_scale=stack(lambda l: pm(inp["pool_scale"][l])),
        g_final=pm(inp["g_final"]),
    )
    per_hf = []
    for hf in range(2):
        rev = hf == 1
        dA, dB = (1, 0) if rev else (0, 1)
        dtc = lambda d: np.arange(6144 + d * 32, 6144 + d * 32 + 32)
        idx = np.concatenate([np.arange(0, 2048), np.arange(2048, 6144), np.arange(7232, 9280),
                              np.arange(6208, 7232), dtc(dA), dtc(dB)])
        cm, pml, pmc, invc = _consts(rev)

        def cw(l):
            w = inp["conv_w"][l]
            if rev:
                w = w[::-1]
            return w.T.reshape(32, 128, 5).transpose(1, 0, 2)

        def bc64(v):
            r = np.concatenate([v[dA], v[dB]])
            return np.broadcast_to(r[None], (128, 64))

        def dsk(l, d):
            v = inp["d_skip"][l][d]
            return np.repeat(v.reshape(16, 2, 1), 64, axis=2).reshape(16, 128).T
        sel = np.zeros((128, 2), np.float32)
        sel[:, 1 - hf] = 1.0
        per_hf.append(dict(
            w_in=f(inp["w_in"][:, :, idx]),
            conv_w=stack(cw), dt_bias=stack(lambda l: bc64(inp["dt_bias"][l])),
            a_log=stack(lambda l: bc64(inp["a_log"][l])),
            dskA=stack(lambda l: dsk(l, dA)), dskB=stack(lambda l: dsk(l, dB)),
            cmat=f(cm), pm_lat=f(pml), pm_ctx=f(pmc), invc=f(invc), selm=f(sel),
        ))
    in_maps = []
    for core in range(n_cores):
        b, hf = core // 2, core % 2
        xb = inp["x"][b]
        cx = inp["ctx"][b]
        if hf == 0:
            lat, halo, cl = xb[:NLAT], xb[NLAT:NLAT + 2], cx
        else:
            lat, halo, cl = xb[NLAT:][::-1], xb[NLAT - 2:NLAT][::-1], cx[::-1]
        xT = f(np.concatenate([cl, lat, halo], axis=0).T)
        cc = f(np.stack([pm(inp["c"][b]), pm(inp["c_ctx"])], axis=-1))
        m = dict(xT=xT, cc=cc)
        m.update(shared)
        m.update(per_hf[hf])
        in_maps.append(m)
    return in_maps


def assemble(results, n_cores=8):
    B = n_cores // 2
    out = np.zeros((B, 2 * NLAT, D), np.float32)
    for core in range(n_cores):
        b, hf = core // 2, core % 2
        y = np.asarray(results[core]["yT"]).T
        if hf == 0:
            out[b, :NLAT] = y
        else:
            out[b, NLAT:] = y[::-1]
    return out


_NC_CACHE = {}


def kernel(**inputs):
    n = 8
    if "nc" not in _NC_CACHE:
        _NC_CACHE["nc"] = build(n)
    nc = _NC_CACHE["nc"]
    in_maps = prep_inputs(inputs, n)
    res = run_bass_kernel_spmd(nc, in_maps, core_ids=list(range(n)))
    return assemble(res.results, n)
```

```python
from contextlib import ExitStack
import numpy as np
import concourse.bass as bass
import concourse.mybir as mybir
from concourse.bass_utils import run_bass_kernel_spmd

F32 = mybir.dt.float32
BF16 = mybir.dt.bfloat16
AF = mybir.ActivationFunctionType
ALU = mybir.AluOpType

D = 1024
KD = 8
DEPTH = 2
NCTX = 256
NLAT = 2048
TT = NCTX + NLAT
TTH = TT + 2
NCH = TT // 128
DIN = 2048
NH = 32
HP = 64
NG = 8
NS = 128
XBCW = 4096
FFH = 2816
INCOLS = 9280
EPS = 1e-6
CZ, CX, CGT, CPL, CDT = 0, 2048, 6144, 8192, 9216
XW = 2308
LOFF = 260

ENGS = ['pe', 'act', 'dve', 'pool', 'sp']
SAME_ENGINE_SYNC = True


class Buf:
    __slots__ = ('name', 'w', 'r')

    def __init__(self, name=''):
        self.name = name
        self.w = None
        self.r = []


class Prog:
    def __init__(self, nc, n_dma_sems=60):
        self.nc = nc
        self.ops = {e: [] for e in ENGS}
        self.sems = {}
        self.cnt = {}
        for e in ENGS:
            self.sems[e] = nc.alloc_semaphore(name=f"c_{e}")
            self.cnt[e] = 0
        self.known = {e: {} for e in ENGS}
        self.dma_keys = {}
        self.n_dma_sems = n_dma_sems

    def _dma_sem(self, key):
        if key not in self.dma_keys:
            assert len(self.dma_keys) < self.n_dma_sems, "too many dma keys"
            s = self.nc.alloc_semaphore(name=f"d_{len(self.dma_keys)}")
            k = ('dma', key)
            self.sems[k] = s
            self.cnt[k] = 0
            self.dma_keys[key] = k
        return self.dma_keys[key]

    def _need(self, eng, ev, waits):
        if ev is None:
            return
        k, v = ev
        if self.known[eng].get(k, 0) >= v:
            return
        waits[k] = max(waits.get(k, 0), v)

    def _deps(self, eng, reads, writes):
        waits = {}
        for b in reads:
            if SAME_ENGINE_SYNC or b.w is None or b.w[0] != eng:
                self._need(eng, b.w, waits)
        for b in writes:
            if SAME_ENGINE_SYNC or b.w is None or b.w[0] != eng:
                self._need(eng, b.w, waits)
            for ev in b.r:
                if ev[0] == eng:
                    continue
                self._need(eng, ev, waits)
        for k, v in waits.items():
            self.known[eng][k] = max(self.known[eng].get(k, 0), v)
        return waits

    def _commit(self, ev, reads, writes):
        for b in reads:
            b.r.append(ev)
        for b in writes:
            b.w = ev
            b.r = []

    def op(self, eng, fn, reads=(), writes=()):
        waits = self._deps(eng, reads, writes)
        self.cnt[eng] += 1
        ev = (eng, self.cnt[eng])
        self.ops[eng].append((waits, fn, (eng, 1)))
        self._commit(ev, reads, writes)
        return ev

    def dma(self, queue, key, fn, reads=(), writes=(), inc=16):
        k = self._dma_sem(key)
        waits = self._deps(queue, reads, writes)
        prev = self.cnt[k]
        if prev > 0 and self.known[queue].get(k, 0) < prev:
            waits[k] = max(waits.get(k, 0), prev)
            self.known[queue][k] = prev
        self.cnt[k] += inc
        ev = (k, self.cnt[k])
        self.ops[queue].append((waits, fn, (k, inc)))
        self._commit(ev, reads, writes)
        return ev

    def barrier(self):
        for e in ENGS:
            waits = {}
            for k, v in self.cnt.items():
                if v > 0 and self.known[e].get(k, 0) < v:
                    waits[k] = v
                    self.known[e][k] = v
            self.ops[e].append((waits, None, None))

    def flush(self):
        self.barrier()
        nc = self.nc
        sems = self.sems
        ops = self.ops

        def run(engname):
            def f(eng):
                for waits, fn, inc in ops[engname]:
                    for k, v in waits.items():
                        eng.wait_ge(sems[k], v)
                    if fn is not None:
                        ins = fn(eng)
                        ins.then_inc(sems[inc[0]], inc[1])
            return f

        with nc.Block() as block:
            block.tensor(run('pe'))
            block.scalar(run('act'))
            block.vector(run('dve'))
            block.gpsimd(run('pool'))
            block.sync(run('sp'))
        self.ops = {e: [] for e in ENGS}


def bc(ap, dims):
    return bass.AP(ap.tensor, ap.offset, [list(ap.ap[0])] + [list(d) for d in dims])


def build(n_cores=8, debug=False, stop_after=None, ext_scratch=False):
    nc = bass.Bass("TRN2", target_bir_lowering=False)
    P = Prog(nc)
    dbg_kind = "ExternalOutput" if (debug or ext_scratch) else "Internal"

    def din(name, shape):
        return nc.dram_tensor(name, list(shape), F32, kind="ExternalInput").ap()

    xT_in = din("xT", [D, TTH])
    cc_in = din("cc", [128, KD, 2])
    w_ada = din("w_ada", [DEPTH, D, 6 * D])
    b_ada = din("b_ada", [DEPTH, 128, 48])
    g_mix = din("g_mix", [DEPTH, 128, KD])
    g_ffn = din("g_ffn", [DEPTH, 128, KD])
    w_in = din("w_in", [DEPTH, D, INCOLS])
    conv_w = din("conv_w", [DEPTH, 128, 32, 5])
    conv_b = din("conv_b", [DEPTH, 128, 32])
    dt_bias = din("dt_bias", [DEPTH, 128, 64])
    a_log = din("a_log", [DEPTH, 128, 64])
    dskA = din("dskA", [DEPTH, 128, 16])
    dskB = din("dskB", [DEPTH, 128, 16])
    norm_w = din("norm_w", [DEPTH, 128, 16])
    w_ssd_out = din("w_ssd_out", [DEPTH, DIN, D])
    pool_w = din("pool_w", [DEPTH, 4, 256, 256])
    pool_scale = din("pool_scale", [DEPTH, 128, 8])
    w_pool_out = din("w_pool_out", [DEPTH, D, D])
    w_out = din("w_out", [DEPTH, D, D])
    w_gate_up = din("w_gate_up", [DEPTH, D, 2 * FFH])
    w_down = din("w_down", [DEPTH, FFH, D])
    g_final = din("g_final", [128, KD])
    cmat = din("cmat", [128, 5, 128])
    pm_lat = din("pm_lat", [128, 4, 128])
    pm_ctx = din("pm_ctx", [128, 2, 4, 2, 128])
    invc = din("invc", [128, 4, 384])
    selm = din("selm", [128, 2])

    yT_out = nc.dram_tensor("yT", [D, NLAT], F32, kind="ExternalOutput").ap()

    def dscr(name, shape, dt):
        return nc.dram_tensor(name, list(shape), dt, kind=dbg_kind).ap()

    XT_d = dscr("XT_d", [D, TTH], F32)
    ZS_d = dscr("ZS_d", [DIN, TT], BF16)
    GS_d = dscr("GS_d", [DIN, TT], BF16)
    XBC_d = dscr("XBC_d", [XBCW, XW], BF16)
    PM2_d = dscr("PM2_d", [D, TT], BF16)
    Y_d = dscr("Y_d", [DIN, TT], F32)
    SB_d = dscr("SB_d", [NCH, 128, DIN], F32)
    YN_d = dscr("YN_d", [DIN, TT], BF16)
    G_d = dscr("G_d", [FFH, TT], BF16)
    CCI_d = nc.dram_tensor("CCI_d", [128, DIN], F32).ap()
    CCO_d = nc.dram_tensor("CCO_d", [256, DIN], F32).ap()
    HCI_d = nc.dram_tensor("HCI_d", [128, 16], F32).ap()
    HCO_d = nc.dram_tensor("HCO_d", [256, 16], F32).ap()
    DBG = {}
    if debug:
        DBG["dtt"] = nc.dram_tensor("DBG_dtt", [128, NCH, 64], F32, kind="ExternalOutput").ap()
        DBG["mod"] = nc.dram_tensor("DBG_mod", [DEPTH, 128, 48, 2], F32, kind="ExternalOutput").ap()
        DBG["ht"] = nc.dram_tensor("DBG_ht", [128, KD, TTH], BF16, kind="ExternalOutput").ap()
        DBG["sa"] = nc.dram_tensor("DBG_sa", [128, DIN], F32, kind="ExternalOutput").ap()
        DBG["yt"] = nc.dram_tensor("DBG_yt", [DIN, TT], F32, kind="ExternalOutput").ap()

    pairs = [[2 * i, 2 * i + 1] for i in range(n_cores // 2)]

    with ExitStack() as top:
        _uid = [0]

        def sb(name, shape, dt, es=top):
            _uid[0] += 1
            return es.enter_context(nc.sbuf_tensor(f"{name}_{_uid[0]}", list(shape), dt)).ap()

        PS = [nc.alloc_psum_tensor(f"ps{i}", [128, 512], F32).ap() for i in range(8)]

        CM = sb("CM", [128, 5, 128], F32)
        CMB = sb("CMB", [128, 5, 128], BF16)
        ONES = sb("ONES", [128, 128], F32)
        ONESB = sb("ONESB", [128, 128], BF16)
        MOD = [sb(f"MOD{l}", [128, 48, 2], F32) for l in range(DEPTH)]
        A1 = [sb(f"A1_{l}", [128, KD, 2], F32) for l in range(DEPTH)]
        A2 = [sb(f"A2_{l}", [128, KD, 2], F32) for l in range(DEPTH)]
        SELM = sb("SELM", [128, 2], F32)
        CCB = sb("CCB", [128, KD, 2], BF16)
        DTT = sb("DTT", [128, NCH, 64], F32)
        DECB = sb("DECB", [128, NCH, 32], F32)
        SA = sb("SA", [128, DIN], F32)
        SAb = sb("SAb", [128, DIN], BF16)
        ident_f = CM[:, 0, :]
        UA_f = CM[:, 1, :]
        UB_f = CM[:, 2, :]
        TLE_f = CM[:, 3, :]
        TGE_f = CM[:, 4, :]
        ident_b = CMB[:, 0, :]

        def B_(n=''):
            return Buf(n)

        with ExitStack() as es:
            b_cm, b_cmb, b_ones, b_sel = B_(), B_(), B_(), B_()
            P.dma('sp', 'ld0', lambda e: e.dma_start(out=CM, in_=cmat), writes=[b_cm])
            P.dma('pool', 'w0', lambda e: e.dma_start(out=CMB, in_=cmat), writes=[b_cmb])
            P.dma('sp', 'ld1', lambda e: e.dma_start(out=SELM, in_=selm), writes=[b_sel])
            P.op('dve', lambda e: e.memset(ONES, 1.0), writes=[b_ones])
            P.op('dve', lambda e: e.memset(ONESB, 1.0), writes=[b_ones])
            xb = [sb(f"xb{i}", [128, KD, 512], F32, es) for i in range(2)]
            bxb = [B_(), B_()]
            bxt = B_()
            xin_v = xT_in.rearrange("(k p) t -> p k t", p=128)
            xtd_v = XT_d.rearrange("(k p) t -> p k t", p=128)
            tiles0 = [(0, 256)] + [(256 + 512 * i, 512) for i in range(4)] + [(TT, 2)]
            for i, (c0, w) in enumerate(tiles0):
                j = i % 2
                P.dma('sp', f'ld{j}', lambda e, j=j, c0=c0, w=w: e.dma_start(out=xb[j][:, :, :w], in_=xin_v[:, :, c0:c0 + w]),
                      writes=[bxb[j]])
                P.dma('sp', f'st{j}', lambda e, j=j, c0=c0, w=w: e.dma_start(out=xtd_v[:, :, c0:c0 + w], in_=xb[j][:, :, :w]),
                      reads=[bxb[j]], writes=[bxt])
            ccf = sb("ccf", [128, KD, 2], F32, es)
            ccb = CCB
            b_ccf, b_ccb = B_(), B_()
            P.dma('sp', 'ld2', lambda e: e.dma_start(out=ccf, in_=cc_in), writes=[b_ccf])
            P.op('act', lambda e: e.activation(out=ccb, in_=ccf, func=AF.Silu), reads=[b_ccf], writes=[b_ccb])
            wa = [sb(f"wa{i}", [128, KD, 512], BF16, es) for i in range(2)]
            bwa = [B_(), B_()]
            bada = sb("bada", [128, 48], F32, es)
            gm = sb("gm", [128, KD], F32, es)
            gf = sb("gf", [128, KD], F32, es)
            b_bada, b_gm, b_gf, b_ps = B_(), B_(), B_(), B_()
            for l in range(1):
                b_mod, b_a = B_(), B_()
                P.dma('sp', 'ld3', lambda e, l=l: e.dma_start(out=bada, in_=b_ada[l]), writes=[b_bada])
                P.dma('sp', 'ld4', lambda e, l=l: e.dma_start(out=gm, in_=g_mix[l]), writes=[b_gm])
                P.dma('sp', 'ld5', lambda e, l=l: e.dma_start(out=gf, in_=g_ffn[l]), writes=[b_gf])
                wv = w_ada[l].rearrange("(k p) n -> p k n", p=128)
                for jb in range(12):
                    j = jb % 2
                    P.dma('pool', f'w{j}', lambda e, j=j, jb=jb, wv=wv: e.dma_start(out=wa[j], in_=wv[:, :, jb * 512:(jb + 1) * 512]),
                          writes=[bwa[j]])
                    for q in range(4):
                        cb = jb * 4 + q

                        def mm(e, j=j, q=q, cb=cb):
                            ins = None
                            for k in range(KD):
                                ins = e.matmul(PS[0][:, cb * 2:cb * 2 + 2], lhsT=wa[j][:, k, q * 128:(q + 1) * 128],
                                               rhs=ccb[:, k, :], start=(k == 0), stop=(k == KD - 1))
                            return ins
                        P.op('pe', mm, reads=[bwa[j], b_ccb], writes=[b_ps])
                P.op('dve', lambda e, l=l: e.tensor_tensor(out=MOD[l], in0=PS[0][:, 0:96].rearrange("p (c t) -> p c t", t=2),
                                                           in1=bc(bada, [[1, 48], [0, 2]]), op=ALU.add),
                     reads=[b_ps, b_bada], writes=[b_mod])
                P.op('dve', lambda e, l=l: e.scalar_tensor_tensor(out=A1[l], in0=MOD[l][:, 8:16, :], scalar=1.0,
                                                                  in1=bc(gm, [[1, KD], [0, 2]]), op0=ALU.add, op1=ALU.mult),
                     reads=[b_mod, b_gm], writes=[b_a])
                P.op('dve', lambda e, l=l: e.scalar_tensor_tensor(out=A2[l], in0=MOD[l][:, 32:40, :], scalar=1.0,
                                                                  in1=bc(gf, [[1, KD], [0, 2]]), op0=ALU.add, op1=ALU.mult),
                     reads=[b_mod, b_gf], writes=[b_a])
                if debug:
                    P.dma('sp', 'st2', lambda e, l=l: e.dma_start(out=DBG["mod"][l], in_=MOD[l]), reads=[b_mod])
            P.flush()
        if stop_after == 0:
            return nc

        def norm_mod(l, which, HT, es):
            A = A1[l] if which == 1 else A2[l]
            boff = 0 if which == 1 else 24
            xt = [sb(f"nx{i}", [128, KD, 512], F32, es) for i in range(2)]
            sq = [sb(f"nq{i}", [128, KD, 512], BF16, es) for i in range(2)]
            xn = [sb(f"nn{i}", [128, KD, 512], F32, es) for i in range(2)]
            rs = [sb(f"nr{i}", [128, 512], F32, es) for i in range(2)]
            ONESb = sb("nob", [128, 128], BF16, es)
            bob = B_()
            P.op('dve', lambda e: e.memset(ONESb, 1.0), writes=[bob])
            bx, bq, bn, br, bp = [B_(), B_()], [B_(), B_()], [B_(), B_()], [B_(), B_()], [B_(), B_()]
            bht = B_()
            tiles = [(0, 256, 1)] + [(256 + 512 * i, 512, 0) for i in range(4)]
            if which == 1:
                tiles.append((TT, 2, 0))

            def front(i):
                c0, w, mc = tiles[i]
                j = i % 2
                ps = PS[j]
                P.dma('sp', f'ld{j}', lambda e, j=j, c0=c0, w=w: e.dma_start(out=xt[j][:, :, :w], in_=xtd_v[:, :, c0:c0 + w]),
                      writes=[bx[j]])
                P.op('act', lambda e, j=j, w=w: e.activation(out=sq[j][:, :, :w], in_=xt[j][:, :, :w], func=AF.Square),
                     reads=[bx[j]], writes=[bq[j]])

                def mm(e, j=j, w=w, ps=ps):
                    ins = None
                    for k in range(KD):
                        ins = e.matmul(ps[:, :w], lhsT=ONESb, rhs=sq[j][:, k, :w], start=(k == 0), stop=(k == KD - 1))
                    return ins
                P.op('pe', mm, reads=[bq[j], bob], writes=[bp[j]])

            def back(i):
                c0, w, mc = tiles[i]
                j = i % 2
                ps = PS[j]
                P.op('act', lambda e, j=j, w=w, ps=ps: e.activation(out=rs[j][:, :w], in_=ps[:, :w], func=AF.Ln, scale=1.0 / D, bias=EPS),
                     reads=[bp[j]], writes=[br[j]])
                P.op('act', lambda e, j=j, w=w: e.activation(out=rs[j][:, :w], in_=rs[j][:, :w], func=AF.Exp, scale=-0.5),
                     reads=[br[j]], writes=[br[j]])
                P.op('dve', lambda e, j=j, w=w: e.tensor_tensor(out=xn[j][:, :, :w], in0=xt[j][:, :, :w],
                                                                in1=bc(rs[j], [[0, KD], [1, w]]), op=ALU.mult),
                     reads=[bx[j], br[j]], writes=[bn[j]])
                for k in range(KD):
                    if k % 3 == 2:
                        P.op('dve', lambda e, j=j, w=w, k=k, c0=c0, mc=mc: e.tensor_scalar(
                            out=HT[:, k, c0:c0 + w], in0=xn[j][:, k, :w], scalar1=A[:, k, mc:mc + 1],
                            scalar2=MOD[l][:, boff + k, mc:mc + 1], op0=ALU.mult, op1=ALU.add),
                            reads=[bn[j]], writes=[bht])
                    else:
                        P.op('act', lambda e, j=j, w=w, k=k, c0=c0, mc=mc: e.activation(
                            out=HT[:, k, c0:c0 + w], in_=xn[j][:, k, :w], func=AF.Identity,
                            scale=A[:, k, mc:mc + 1], bias=MOD[l][:, boff + k, mc:mc + 1]),
                            reads=[bn[j]], writes=[bht])

            front(0)
            for i in range(len(tiles)):
                if i + 1 < len(tiles):
                    front(i + 1)
                back(i)

        def inproj(l, HT, es):
            W = [sb(f"ipW{i}", [128, KD, 512], BF16, es) for i in range(2)]
            bW = [B_(), B_()]
            U = [sb(f"ipU{i}", [128, XW + 4], F32, es) for i in range(2)]
            bU = [B_(), B_()]
            ACC = [sb(f"ipA{i}", [128, XW], F32, es) for i in range(2)]
            bACC = [B_(), B_()]
            STG = [sb(f"ipS{i}", [128, XW], BF16, es) for i in range(2)]
            bSTG = [B_(), B_()]
            cw = sb("ipcw", [128, 32, 5], F32, es)
            cb = sb("ipcb", [128, 32], F32, es)
            b_cw, b_cb = B_(), B_()
            bPS = [B_() for _ in range(8)]
            wv = w_in[l].rearrange("(k p) n -> p k n", p=128)
            P.dma('sp', 'ld0', lambda e: e.dma_start(out=cw, in_=conv_w[l]), writes=[b_cw])
            P.dma('sp', 'ld1', lambda e: e.dma_start(out=cb, in_=conv_b[l]), writes=[b_cb])
            for i in range(2):
                P.op('dve', lambda e, i=i: e.memset(U[i], 0.0), writes=[bU[i]])
                P.op('dve', lambda e, i=i: e.memset(ACC[i], 0.0), writes=[bACC[i]])
            bHT = B_()
            nw = [0]

            def load_w(c0, ncol):
                j = nw[0] % 2
                nw[0] += 1
                P.dma('pool', f'w{j}', lambda e, j=j, c0=c0, ncol=ncol: e.dma_start(out=W[j][:, :, :ncol], in_=wv[:, :, c0:c0 + ncol]),
                      writes=[bW[j]])
                return j

            dtb = sb("ipdtb", [128, 64], F32, es)
            dx = sb("ipdx", [128, NCH * 64], F32, es)
            d1 = sb("ipd1", [128, NCH * 64], F32, es)
            b_dtb, b_dx, b_d1, b_dtt = B_(), B_(), B_(), B_()
            P.dma('sp', 'ld2', lambda e: e.dma_start(out=dtb, in_=dt_bias[l]), writes=[b_dtb])
            j = load_w(CDT, 64)
            for c in range(NCH):
                pb = 5 + c // 8
                po = (c % 8) * 64

                def mm(e, j=j, c=c, pb=pb, po=po):
                    ins = None
                    for k in range(KD):
                        ins = e.matmul(PS[pb][:, po:po + 64], lhsT=HT[:, k, c * 128:(c + 1) * 128], rhs=W[j][:, k, 0:64],
                                       start=(k == 0), stop=(k == KD - 1))
                    return ins
                P.op('pe', mm, reads=[bW[j], bHT], writes=[bPS[pb]])
            for pb, nchk in ((5, 8), (6, 8), (7, 2)):
                c0 = (pb - 5) * 8
                P.op('dve', lambda e, pb=pb, nchk=nchk, c0=c0: e.tensor_tensor(
                    out=dx[:, c0 * 64:(c0 + nchk) * 64].rearrange("p (c h) -> p c h", h=64),
                    in0=PS[pb][:, 0:nchk * 64].rearrange("p (c h) -> p c h", h=64),
                    in1=bc(dtb, [[0, nchk], [1, 64]]), op=ALU.add), reads=[bPS[pb], b_dtb], writes=[b_dx])
            P.op('act', lambda e: e.activation(out=d1, in_=dx, func=AF.Abs), reads=[b_dx], writes=[b_d1])
            P.op('act', lambda e: e.activation(out=d1, in_=d1, func=AF.Exp, scale=-1.0), reads=[b_d1], writes=[b_d1])
            P.op('act', lambda e: e.activation(out=d1, in_=d1, func=AF.Ln, bias=1.0), reads=[b_d1], writes=[b_d1])
            P.op('dve', lambda e: e.tensor_scalar_max(out=dx, in0=dx, scalar1=0.0), reads=[b_dx], writes=[b_dx])
            P.op('dve', lambda e: e.tensor_tensor(out=DTT.rearrange("p c h -> p (c h)"), in0=dx, in1=d1, op=ALU.add),
                 reads=[b_dx, b_d1], writes=[b_dtt])
            if debug and l == 0:
                P.dma('sp', 'st3', lambda e: e.dma_start(out=DBG["dtt"], in_=DTT), reads=[b_dtt])

            PMl = sb("ipPMl", [128, 4, 128], BF16, es)
            PMc = sb("ipPMc", [128, 2, 4, 2, 128], BF16, es)
            INV = sb("ipINV", [128, 4, 384], F32, es)
            PW = sb("ipPW", [128, 4, 2, 256], BF16, es)
            PSC = sb("ipPSC", [128, 8], F32, es)
            UT = [sb(f"ipUT{i}", [128, 2, 512], BF16, es) for i in range(2)]
            bUT = [B_(), B_()]
            PMT = [sb(f"ipPMT{i}", [128, 4, 256], BF16, es) for i in range(2)]
            bPMT = [B_(), B_()]
            PM2S = sb("ipPM2S", [128, 4, TT], BF16, es)
            b_pm2s = B_()
            b_pml, b_pmc, b_inv, b_pw, b_psc = B_(), B_(), B_(), B_(), B_()
            P.dma('pool', 'w2', lambda e: e.dma_start(out=PMl, in_=pm_lat), writes=[b_pml])
            P.dma('pool', 'w3', lambda e: e.dma_start(out=PMc, in_=pm_ctx), writes=[b_pmc])
            P.dma('sp', 'ld3', lambda e: e.dma_start(out=INV, in_=invc), writes=[b_inv])
            P.dma('pool', 'w4', lambda e: e.dma_start(out=PW, in_=pool_w[l].rearrange("g (ib p) o -> p g ib o", p=128)), writes=[b_pw])
            P.dma('sp', 'ld4', lambda e: e.dma_start(out=PSC, in_=pool_scale[l]), writes=[b_psc])
            it = 0
            for jp in range(2):
                j = load_w(CPL + jp * 512, 512)
                units = [(0, 2)] + [(c, 1) for c in range(2, NCH)]
                for (c0, nck) in units:
                    u = it % 2
                    it += 1
                    wtok = nck * 128
                    for cc in range(nck):
                        def mm(e, j=j, c=c0 + cc):
                            ins = None
                            for k in range(KD):
                                ins = e.matmul(PS[4], lhsT=HT[:, k, c * 128:(c + 1) * 128], rhs=W[j][:, k, :],
                                               start=(k == 0), stop=(k == KD - 1))
                            return ins
                        P.op('pe', mm, reads=[bW[j], bHT], writes=[bPS[4]])
                        P.op('act', lambda e, u=u, cc=cc: e.activation(out=UT[u][:, cc, :], in_=PS[4], func=AF.Copy),
                             reads=[bPS[4]], writes=[bUT[u]])
                    def pbk(banks, q, wtok=wtok):
                        if wtok == 128:
                            return banks[0], q * 128
                        return banks[q // 2], (q % 2) * 256
                    for q in range(4):
                        g = 2 * jp + q // 2
                        pbi, po = pbk((5, 7), q)

                        def mmp(e, u=u, q=q, g=g, nck=nck, pbi=pbi, po=po):
                            ins = None
                            if nck == 1:
                                ins = e.matmul(PS[pbi][:, po:po + 128], lhsT=UT[u][:, 0, q * 128:(q + 1) * 128],
                                               rhs=PMl[:, g, :], start=True, stop=True)
                            else:
                                for tb in range(2):
                                    for tb_ in range(2):
                                        ins = e.matmul(PS[pbi][:, po + tb * 128:po + (tb + 1) * 128],
                                                       lhsT=UT[u][:, tb_, q * 128:(q + 1) * 128],
                                                       rhs=PMc[:, tb_, g, tb, :], start=(tb_ == 0), stop=(tb_ == 1))
                            return ins
                        P.op('pe', mmp, reads=[bUT[u], b_pml, b_pmc], writes=[bPS[pbi]])
                    for gg in range(2):
                        g = 2 * jp + gg
                        ioff = 0 if nck == 2 else 256
                        pbi, po = pbk((5, 7), 2 * gg)
                        P.op('dve', lambda e, u=u, gg=gg, g=g, wtok=wtok, ioff=ioff, pbi=pbi, po=po: e.tensor_tensor(
                            out=PMT[u][:, 2 * gg:2 * gg + 2, :wtok],
                            in0=PS[pbi][:, po:po + 2 * wtok].rearrange("p (q t) -> p q t", t=wtok),
                            in1=bc(INV[:, g, ioff:ioff + wtok], [[0, 2], [1, wtok]]), op=ALU.mult),
                            reads=[bPS[pbi], b_inv], writes=[bPMT[u]])
                    for gg in range(2):
                        g = 2 * jp + gg
                        for ob in range(2):
                            q = 2 * gg + ob
                            pbi, po = pbk((6, 3), q)

                            def mmw(e, u=u, g=g, gg=gg, ob=ob, wtok=wtok, pbi=pbi, po=po):
                                ins = None
                                for ib in range(2):
                                    ins = e.matmul(PS[pbi][:, po:po + wtok], lhsT=PW[:, g, ib, ob * 128:(ob + 1) * 128],
                                                   rhs=PMT[u][:, 2 * gg + ib, :wtok], start=(ib == 0), stop=(ib == 1))
                                return ins
                            P.op('pe', mmw, reads=[bPMT[u], b_pw], writes=[bPS[pbi]])
                    for q in range(4):
                        blk = jp * 4 + q
                        pbi, po = pbk((6, 3), q)
                        P.op('act', lambda e, q=q, blk=blk, wtok=wtok, c0=c0, pbi=pbi, po=po: e.activation(
                            out=PM2S[:, q, c0 * 128:c0 * 128 + wtok], in_=PS[pbi][:, po:po + wtok],
                            func=AF.Copy, scale=PSC[:, blk:blk + 1]), reads=[bPS[pbi], b_psc], writes=[b_pm2s])
                P.dma('sp', 'st0', lambda e, jp=jp: e.dma_start(
                    out=PM2_d[jp * 512:(jp + 1) * 512, :].rearrange("(q p) t -> p q t", p=128), in_=PM2S),
                    reads=[b_pm2s])

            do_ada = (l + 1 < DEPTH)
            if do_ada:
                la = l + 1
                awa = [sb(f"ipwa{i}", [128, KD, 512], BF16, es) for i in range(2)]
                abada = sb("ipbada", [128, 48], F32, es)
                agm = sb("ipgm", [128, KD], F32, es)
                agf = sb("ipgf", [128, KD], F32, es)
                bawa = [B_(), B_()]
                ab_bada, ab_gm, ab_gf, ab_mod, ab_a = B_(), B_(), B_(), B_(), B_()
                P.dma('sp', 'ld5', lambda e: e.dma_start(out=abada, in_=b_ada[la]), writes=[ab_bada])
                P.dma('sp', 'ld6', lambda e: e.dma_start(out=agm, in_=g_mix[la]), writes=[ab_gm])
                P.dma('sp', 'ld7', lambda e: e.dma_start(out=agf, in_=g_ffn[la]), writes=[ab_gf])
                awv = w_ada[la].rearrange("(k p) n -> p k n", p=128)

            def ada_block(jb):
                j = jb % 2
                P.dma('pool', f'w{5 + j}', lambda e, j=j, jb=jb: e.dma_start(out=awa[j], in_=awv[:, :, jb * 512:(jb + 1) * 512]),
                      writes=[bawa[j]])
                for q in range(4):
                    cb_ = jb * 4 + q

                    def mm(e, j=j, q=q, cb_=cb_):
                        ins = None
                        for k in range(KD):
                            ins = e.matmul(PS[7][:, cb_ * 2:cb_ * 2 + 2], lhsT=awa[j][:, k, q * 128:(q + 1) * 128],
                                           rhs=CCB[:, k, :], start=(k == 0), stop=(k == KD - 1))
                        return ins
                    P.op('pe', mm, reads=[bawa[j]], writes=[bPS[7]])
                if jb == 11:
                    P.op('dve', lambda e: e.tensor_tensor(out=MOD[la], in0=PS[7][:, 0:96].rearrange("p (c t) -> p c t", t=2),
                                                          in1=bc(abada, [[1, 48], [0, 2]]), op=ALU.add),
                         reads=[bPS[7], ab_bada], writes=[ab_mod])
                    P.op('dve', lambda e: e.scalar_tensor_tensor(out=A1[la], in0=MOD[la][:, 8:16, :], scalar=1.0,
                                                                 in1=bc(agm, [[1, KD], [0, 2]]), op0=ALU.add, op1=ALU.mult),
                         reads=[ab_mod, ab_gm], writes=[ab_a])
                    P.op('dve', lambda e: e.scalar_tensor_tensor(out=A2[la], in0=MOD[la][:, 32:40, :], scalar=1.0,
                                                                 in1=bc(agf, [[1, KD], [0, 2]]), op0=ALU.add, op1=ALU.mult),
                         reads=[ab_mod, ab_gf], writes=[ab_a])

            tiles = [(0, 256)] + [(256 + 512 * i, 512) for i in range(4)]
            nps = [0]
            nstg = [0]
            nxb = [0]
            for jb in range(16):
                c0w = jb * 512
                fam = 'z' if jb < 4 else ('x' if jb < 12 else 'g')
                j = load_w(c0w, 512)
                if do_ada and jb < 12:
                    ada_block(jb)
                for q in range(4):
                    if fam == 'x':
                        blk = (jb - 4) * 4 + q
                        ui = nxb[0] % 2
                        nxb[0] += 1
                        tl = tiles + [(TT, 2)]
                    else:
                        si = nstg[0] % 2
                        nstg[0] += 1
                        tl = tiles
                    for (t0, w) in tl:
                        pb = nps[0] % 4
                        nps[0] += 1

                        def mm(e, j=j, q=q, t0=t0, w=w, pb=pb):
                            ins = None
                            for k in range(KD):
                                ins = e.matmul(PS[pb][:, :w], lhsT=W[j][:, k, q * 128:(q + 1) * 128], rhs=HT[:, k, t0:t0 + w],
                                               start=(k == 0), stop=(k == KD - 1))
                            return ins
                        P.op('pe', mm, reads=[bW[j], bHT], writes=[bPS[pb]])
                        if fam == 'x':
                            uo = 2 if t0 == 0 else (LOFF + 2 + (t0 - NCTX))
                            P.op('act', lambda e, ui=ui, uo=uo, w=w, pb=pb: e.activation(out=U[ui][:, uo:uo + w], in_=PS[pb][:, :w], func=AF.Copy),
                                 reads=[bPS[pb]], writes=[bU[ui]])
                            if w > 2:
                                P.op('act', lambda e, ui=ui, uo=uo, w=w, pb=pb, blk=blk: e.activation(
                                    out=ACC[ui][:, uo - 2:uo - 2 + w], in_=PS[pb][:, :w], func=AF.Copy, scale=cw[:, blk, 2:3]),
                                    reads=[bPS[pb], b_cw], writes=[bACC[ui]])
                        else:
                            fn = AF.Silu if fam == 'z' else AF.Sigmoid
                            P.op('act', lambda e, si=si, t0=t0, w=w, pb=pb, fn=fn: e.activation(out=STG[si][:, t0:t0 + w], in_=PS[pb][:, :w], func=fn),
                                 reads=[bPS[pb]], writes=[bSTG[si]])
                    if fam == 'x':
                        ai = ui
                        for tp in (0, 1, 3, 4):
                            P.op('dve', lambda e, ui=ui, ai=ai, blk=blk, tp=tp: e.scalar_tensor_tensor(
                                out=ACC[ai], in0=U[ui][:, tp:tp + XW], scalar=cw[:, blk, tp:tp + 1], in1=ACC[ai],
                                op0=ALU.mult, op1=ALU.add), reads=[bU[ui], b_cw, bACC[ai]], writes=[bACC[ai]])
                        si = nstg[0] % 2
                        nstg[0] += 1
                        P.op('act', lambda e, si=si, ai=ai, blk=blk: e.activation(out=STG[si], in_=ACC[ai], func=AF.Silu, bias=cb[:, blk:blk + 1]),
                             reads=[bACC[ai], b_cb], writes=[bSTG[si]])
                        P.dma('sp', f'st{si}', lambda e, si=si, blk=blk: e.dma_start(out=XBC_d[blk * 128:(blk + 1) * 128, :], in_=STG[si]),
                              reads=[bSTG[si]])
                    else:
                        dst = ZS_d if fam == 'z' else GS_d
                        blk = (jb * 4 + q) if fam == 'z' else ((jb - 12) * 4 + q)
                        P.dma('sp', f'st{si}', lambda e, si=si, blk=blk, dst=dst: e.dma_start(out=dst[blk * 128:(blk + 1) * 128, :], in_=STG[si][:, 0:TT]),
                              reads=[bSTG[si]])


        xbc_v = XBC_d.rearrange("(blk p) t -> p blk t", p=128)
        yd_v = Y_d.rearrange("(blk p) t -> p blk t", p=128)
        zs_v = ZS_d.rearrange("(blk p) t -> p blk t", p=128)
        ynd_v = YN_d.rearrange("(blk p) t -> p blk t", p=128)

        def chunk_col(c):
            return c * 128 if c < 2 else LOFF + (c - 2) * 128

        def pad_ap(X, h0, nh):
            return bass.AP(X.tensor, X.offset + h0 * 128, [list(X.ap[0]), [256, nh // 2], [192, 2], [1, 64]])

        def sweepA(l, es):
            XB = [sb(f"aXB{i}", [128, 32, 128], BF16, es) for i in range(2)]
            bXB = [B_(), B_()]
            XDA = sb("aXDA", [128, 32, 128], BF16, es)
            XDB = sb("aXDB", [128, 32, 128], BF16, es)
            XWA = sb("aXWA", [128, DIN], BF16, es)
            XWB = sb("aXWB", [128, DIN], BF16, es)
            BT = sb("aBT", [128, 1024], BF16, es)
            AN = sb("aAN", [128, 64], F32, es)
            DTA = sb("aDTA", [128, 64], F32, es)
            EXPO = sb("aEXPO", [128, 128], F32, es)
            FAC = sb("aFAC", [128, 64], F32, es)
            E = [sb(f"aE{i}", [128, 4, 128], F32, es) for i in range(2)]
            MP = [[sb(f"aMP{i}{d}", [128, 4, 128], BF16, es) for d in range(2)] for i in range(2)]
            CBTm = sb("aCBT", [128, 2, 8, 128], F32, es)
            DBH = sb("aDBH", [128, 2, DIN], BF16, es)
            DTH = sb("aDTH", [128, 2, 32], BF16, es)
            bDTH = B_()
            EA = sb("aEA", [128, DIN], F32, es)
            T1 = sb("aT1", [128, DIN], F32, es)
            YP = [sb(f"aYP{i}", [128, DIN], F32, es) for i in range(2)]
            SBL = [sb(f"aSBL{i}", [128, DIN], F32, es) for i in range(2)]
            DSK = sb("aDSK", [128, 16, 128], BF16, es)
            dsa = sb("adsa", [128, 16], F32, es)
            dsb = sb("adsb", [128, 16], F32, es)
            bXD, bXW, bBT, bAN, bDTA, bEXPO, bFAC = B_(), B_(), B_(), B_(), B_(), B_(), B_()
            bE = [B_(), B_()]
            bMP = [[B_(), B_()], [B_(), B_()]]
            bCBT, bDBC, bEA, bT1, bDSK, bds, bSA, bSAb, bDECB = B_(), B_(), B_(), B_(), B_(), B_(), B_(), B_(), B_()
            bYP, bSBL = [B_(), B_()], [B_(), B_()]
            bPS = [B_() for _ in range(8)]
            PSb = [PS[i].bitcast(BF16) for i in range(3)]
            P.dma('sp', 'ld0', lambda e: e.dma_start(out=AN, in_=a_log[l]), writes=[bAN])
            P.op('act', lambda e: e.activation(out=AN, in_=AN, func=AF.Exp), reads=[bAN], writes=[bAN])
            P.op('dve', lambda e: e.tensor_scalar_mul(out=AN, in0=AN, scalar1=-1.0), reads=[bAN], writes=[bAN])
            P.dma('sp', 'ld1', lambda e: e.dma_start(out=dsa, in_=dskA[l]), writes=[bds])
            P.dma('sp', 'ld2', lambda e: e.dma_start(out=dsb, in_=dskB[l]), writes=[bds])
            P.op('dve', lambda e: e.tensor_tensor(out=dsa, in0=dsa, in1=dsb, op=ALU.add), reads=[bds], writes=[bds])
            for pp in range(16):
                P.op('dve', lambda e, pp=pp: e.tensor_scalar(out=DSK[:, pp, :], in0=ident_f, scalar1=dsa[:, pp:pp + 1], scalar2=None, op0=ALU.mult),
                     reads=[bds], writes=[bDSK])
            P.op('dve', lambda e: e.memset(XDA, 0.0), writes=[bXD])
            P.op('dve', lambda e: e.memset(XDB, 0.0), writes=[bXD])
            P.op('dve', lambda e: e.memset(SA, 0.0), writes=[bSA])
            P.op('dve', lambda e: e.memset(SAb, 0.0), writes=[bSAb])

            R2 = [sb(f"aR2x{i}", [128, 4, 128], F32, es) for i in range(3)]
            bR2 = [B_(), B_(), B_()]
            P.dma('sp', 'ld3', lambda e: e.dma_start(out=XB[0], in_=xbc_v[:, :, 0:128]), writes=[bXB[0]])
            for c in range(NCH):
                i = c % 2
                co = chunk_col(c)
                yi = c % 2
                if c + 1 < NCH:
                    con = chunk_col(c + 1)
                    P.dma('sp', f'ld{3 + (1 - i)}', lambda e, i=i, con=con: e.dma_start(out=XB[1 - i], in_=xbc_v[:, :, con:con + 128]), writes=[bXB[1 - i]])
                P.op('dve', lambda e, c=c: e.tensor_tensor(out=DTA, in0=DTT[:, c, :], in1=AN, op=ALU.mult), reads=[bAN], writes=[bDTA])
                P.op('dve', lambda e: e.tensor_copy(out=DTH[:, 0, :], in_=DTA[:, 0:32]), reads=[bDTA], writes=[bDTH])
                P.op('dve', lambda e: e.tensor_tensor(out=DTH[:, 1, :], in0=DTA[:, 0:32], in1=DTH[:, 0, :], op=ALU.subtract), reads=[bDTA, bDTH], writes=[bDTH])
                for hl in range(2):
                    P.op('dve', lambda e, hl=hl: e.tensor_copy(out=DBH[:, hl, :].rearrange("p (h q) -> p h q", q=64), in_=bc(DTH[:, hl, :], [[1, 32], [0, 64]])),
                         reads=[bDTH], writes=[bDBC])

                def emit_R2(n):
                    g, d = n // 2, n % 2
                    Mk = TLE_f if d == 0 else TGE_f
                    rr = n % 3
                    P.op('pool' if n % 2 == 1 else 'dve', lambda e, rr=rr, d=d, g=g, Mk=Mk: e.tensor_tensor(
                        out=R2[rr], in0=bc(DTA[:, d * 32 + 4 * g:d * 32 + 4 * g + 4], [[1, 4], [0, 128]]),
                        in1=bc(Mk, [[0, 4], [1, 128]]), op=ALU.mult), reads=[bDTA], writes=[bR2[rr]])

                def emit_D(n):
                    d = n % 2
                    Uf = UA_f if d == 0 else UB_f
                    rr = n % 3
                    pb = 4 + n % 2
                    P.op('pe', lambda e, rr=rr, Uf=Uf, pb=pb: e.matmul(PS[pb], lhsT=Uf, rhs=R2[rr].rearrange("p h t -> p (h t)"), start=True, stop=True),
                         reads=[bR2[rr]], writes=[bPS[pb]])
                    P.op('act', lambda e, pb=pb, n=n: e.activation(out=E[n % 2].rearrange("p h t -> p (h t)"), in_=PS[pb], func=AF.Exp),
                         reads=[bPS[pb]], writes=[bE[n % 2]])

                def emit_M(n):
                    g, d = n // 2, n % 2
                    r = g % 2
                    P.op('dve', lambda e, n=n, r=r, d=d, g=g: e.tensor_tensor(
                        out=MP[r][d], in0=E[n % 2], in1=bc(CBTm[:, d, g, :], [[0, 4], [1, 128]]), op=ALU.mult),
                        reads=[bE[n % 2], bCBT], writes=[bMP[r][d]])

                def emit_y(g):
                    r = g % 2
                    for pe_ in range(2):
                        pp = 2 * g + pe_
                        bank = pp // 4
                        col = (pp % 4) * 128

                        def mmy(e, pp=pp, bank=bank, col=col, r=r, g=g, i=i):
                            o = PS[bank][:, col:col + 128]
                            e.matmul(o, lhsT=DSK[:, pp, :], rhs=XB[i][:, pp, :], start=True, stop=False)
                            ins = None
                            for ee in range(2):
                                h = 2 * pp + ee
                                hh = h - 4 * g
                                e.matmul(o, lhsT=XDA[:, h, :], rhs=MP[r][0][:, hh, :], start=False, stop=False)
                                ins = e.matmul(o, lhsT=XDB[:, h, :], rhs=MP[r][1][:, hh, :], start=False, stop=(ee == 1))
                            return ins
                        P.op('pe', mmy, reads=[bDSK, bXB[i], bXD, bMP[r][0], bMP[r][1]], writes=[bPS[bank]])

                emit_R2(0)
                emit_R2(1)

                def mm_small(e):
                    e.matmul(PS[3][:, 0:32], lhsT=UA_f, rhs=DTA[:, 0:32], start=True, stop=True)
                    e.matmul(PS[3][:, 32:64], lhsT=UB_f, rhs=DTA[:, 32:64], start=True, stop=True)
                    return e.matmul(PS[3][:, 64:128], lhsT=ONES, rhs=DTA[:, 0:64], start=True, stop=True)
                P.op('pe', mm_small, reads=[bDTA], writes=[bPS[3]])
                P.op('act', lambda e: e.activation(out=EXPO, in_=PS[3][:, 0:128], func=AF.Exp), reads=[bPS[3]], writes=[bEXPO])
                for b2 in range(2):
                    def tr(e, b2=b2, i=i):
                        ins = None
                        for q in range(8):
                            ins = e.transpose(PSb[b2][:, q * 128:(q + 1) * 128], XB[i][:, b2 * 8 + q, :], ident_b)
                        return ins
                    P.op('pe', tr, reads=[bXB[i]], writes=[bPS[b2]])

                def trb(e, i=i):
                    ins = None
                    for q in range(8):
                        ins = e.transpose(PSb[2][:, q * 128:(q + 1) * 128], XB[i][:, 16 + q, :], ident_b)
                    return ins
                P.op('pe', trb, reads=[bXB[i]], writes=[bPS[2]])
                for b2 in range(2):
                    def mmc(e, b2=b2, i=i):
                        ins = None
                        for gq in range(4):
                            g = b2 * 4 + gq
                            ins = e.matmul(PS[4 + b2][:, gq * 128:(gq + 1) * 128], lhsT=XB[i][:, 16 + g, :], rhs=XB[i][:, 24 + g, :],
                                           start=True, stop=True)
                        return ins
                    P.op('pe', mmc, reads=[bXB[i]], writes=[bPS[4 + b2]])
                for rd in range(2):
                    for b2 in range(2):
                        b4 = rd * 2 + b2

                        def mme(e, b4=b4, b2=b2):
                            ins = None
                            for q in range(4):
                                pp = b4 * 4 + q
                                e.matmul(PS[6 + b2][:, q * 128:(q + 1) * 128], lhsT=DBH[:, 0, pp * 128:(pp + 1) * 128], rhs=CMB[:, 3, :],
                                         start=True, stop=False)
                                ins = e.matmul(PS[6 + b2][:, q * 128:(q + 1) * 128], lhsT=DBH[:, 1, pp * 128:(pp + 1) * 128], rhs=CMB[:, 3, :],
                                               start=False, stop=True)
                            return ins
                        P.op('pe', mme, reads=[bDBC], writes=[bPS[6 + b2]])
                        P.op('act', lambda e, b4=b4, b2=b2: e.activation(out=EA[:, b4 * 512:(b4 + 1) * 512], in_=PS[6 + b2], func=AF.Exp),
                             reads=[bPS[6 + b2]], writes=[bEA])
                    if rd == 0:
                        for b2 in range(2):
                            for d, Mk in ((0, TLE_f), (1, TGE_f)):
                                P.op('dve', lambda e, b2=b2, d=d, Mk=Mk: e.tensor_tensor(
                                    out=CBTm[:, d, b2 * 4:(b2 + 1) * 4, :], in0=PS[4 + b2].rearrange("p (g t) -> p g t", t=128),
                                    in1=bc(Mk, [[0, 4], [1, 128]]), op=ALU.mult), reads=[bPS[4 + b2]], writes=[bCBT])
                        emit_D(0)
                        emit_D(1)
                P.op('dve', lambda e, c=c: e.tensor_tensor(out=FAC, in0=DTT[:, c, :], in1=EXPO[:, 0:64], op=ALU.mult), reads=[bEXPO], writes=[bFAC])
                P.op('dve', lambda e, c=c: e.tensor_copy(out=DECB[:, c, :], in_=EXPO[:, 96:128]), reads=[bEXPO], writes=[bDECB])
                for b2 in range(2):
                    src3 = bass.AP(PSb[b2].tensor, PSb[b2].offset, [list(PSb[b2].ap[0]), [128, 8], [64, 2], [1, 64]])
                    for d, X in ((0, XDA), (1, XDB)):
                        fac = bass.AP(DTT.tensor, DTT[:, c, :].offset + d * 32 + 16 * b2, [list(DTT.ap[0]), [2, 8], [1, 2], [0, 64]])
                        P.op('dve', lambda e, X=X, b2=b2, src3=src3, fac=fac: e.tensor_tensor(out=pad_ap(X, 16 * b2, 16), in0=src3, in1=fac, op=ALU.mult),
                             reads=[bPS[b2]], writes=[bXD])
                    for d, X in ((0, XWA), (1, XWB)):
                        P.op('dve', lambda e, X=X, b2=b2, d=d: e.tensor_tensor(
                            out=X[:, b2 * 1024:(b2 + 1) * 1024].rearrange("p (h q) -> p h q", q=64),
                            in0=PSb[b2].rearrange("p (h q) -> p h q", q=64),
                            in1=bc(FAC[:, d * 32 + 16 * b2:d * 32 + 16 * b2 + 16], [[1, 16], [0, 64]]), op=ALU.mult),
                            reads=[bPS[b2], bFAC], writes=[bXW])
                P.op('act', lambda e: e.activation(out=BT, in_=PSb[2], func=AF.Copy), reads=[bPS[2]], writes=[bBT])
                for n in range(2, 18):
                    emit_M(n - 2)
                    if n < 16:
                        emit_R2(n)
                        emit_D(n)
                    if (n - 2) % 2 == 1:
                        emit_y((n - 2) // 2)
                for hb in range(2):
                    for b2 in range(2):
                        def mmi(e, hb=hb, b2=b2, i=i):
                            ins = None
                            for q in range(4):
                                pp = hb * 8 + b2 * 4 + q
                                ins = e.matmul(PS[6 + b2][:, q * 128:(q + 1) * 128], lhsT=SAb[:, pp * 128:(pp + 1) * 128],
                                               rhs=XB[i][:, 24 + pp // 2, :], start=True, stop=True)
                            return ins
                        P.op('pe', mmi, reads=[bSAb, bXB[i]], writes=[bPS[6 + b2]])
                        cs = (hb * 2 + b2) * 512
                        P.op('dve', lambda e, b2=b2, cs=cs: e.tensor_tensor(out=T1[:, cs:cs + 512], in0=PS[6 + b2], in1=EA[:, cs:cs + 512], op=ALU.mult),
                             reads=[bPS[6 + b2], bEA], writes=[bT1])
                        yb = hb * 2 + b2
                        P.op('dve', lambda e, yb=yb, cs=cs, yi=yi: e.tensor_tensor(out=YP[yi][:, cs:cs + 512], in0=PS[yb], in1=T1[:, cs:cs + 512], op=ALU.add),
                             reads=[bPS[yb], bT1], writes=[bYP[yi]])
                P.dma('sp', f'st{yi}', lambda e, yi=yi, c=c: e.dma_start(out=yd_v[:, :, c * 128:(c + 1) * 128], in_=YP[yi].rearrange("p (b t) -> p b t", t=128)),
                      reads=[bYP[yi]])
                for b4 in range(4):
                    def mms(e, b4=b4):
                        ins = None
                        for q in range(2):
                            g = b4 * 2 + q
                            ins = e.matmul(PS[b4][:, q * 256:(q + 1) * 256], lhsT=BT[:, g * 128:(g + 1) * 128], rhs=XWA[:, g * 256:(g + 1) * 256],
                                           start=True, stop=True)
                        return ins
                    P.op('pe', mms, reads=[bBT, bXW], writes=[bPS[b4]])
                P.op('dve', lambda e: e.tensor_tensor(out=SA.rearrange("p (h q) -> p h q", q=64), in0=SA.rearrange("p (h q) -> p h q", q=64),
                                                      in1=bc(EXPO[:, 64:96], [[1, 32], [0, 64]]), op=ALU.mult), reads=[bEXPO, bSA], writes=[bSA])
                for b4 in range(4):
                    def mmsb(e, b4=b4):
                        ins = None
                        for q in range(2):
                            g = b4 * 2 + q
                            ins = e.matmul(PS[4 + b4][:, q * 256:(q + 1) * 256], lhsT=BT[:, g * 128:(g + 1) * 128], rhs=XWB[:, g * 256:(g + 1) * 256],
                                           start=True, stop=True)
                        return ins
                    P.op('pe', mmsb, reads=[bBT, bXW], writes=[bPS[4 + b4]])
                for b4 in range(4):
                    P.op('dve', lambda e, b4=b4: e.tensor_tensor(out=SA[:, b4 * 512:(b4 + 1) * 512], in0=SA[:, b4 * 512:(b4 + 1) * 512], in1=PS[b4], op=ALU.add),
                         reads=[bPS[b4], bSA], writes=[bSA])
                    P.op('act', lambda e, b4=b4, yi=yi: e.activation(out=SBL[yi][:, b4 * 512:(b4 + 1) * 512], in_=PS[4 + b4], func=AF.Copy),
                         reads=[bPS[4 + b4]], writes=[bSBL[yi]])
                P.op('act', lambda e: e.activation(out=SAb, in_=SA, func=AF.Copy), reads=[bSA], writes=[bSAb])
                P.dma('sp', f'st{2 + yi}', lambda e, yi=yi, c=c: e.dma_start(out=SB_d[c], in_=SBL[yi]), reads=[bSBL[yi]])
                if debug and l == 0 and c == 1:
                    P.dma('sp', 'st4', lambda e: e.dma_start(out=DBG["sa"], in_=SA), reads=[bSA])


        def sweepB(l, es, last):
            CT = [sb(f"bCT{i}", [128, 8, 128], BF16, es) for i in range(2)]
            YPl = [sb(f"bYP{i}", [128, DIN], F32, es) for i in range(2)]
            SBLl = [sb(f"bSBL{i}", [128, DIN], F32, es) for i in range(2)]
            ZSl = [sb(f"bZS{i}", [128, 16, 128], BF16, es) for i in range(2)]
            YNs = [sb(f"bYN{i}", [128, 16, 128], BF16, es) for i in range(2)]
            SBr = sb("bSBr", [128, DIN], F32, es)
            SBb = sb("bSBb", [128, DIN], BF16, es)
            CC2 = sb("bCC2", [128, 2, DIN], F32, es)
            AN = sb("bAN", [128, 32], F32, es)
            DTA = sb("bDTA", [128, 32], F32, es)
            DBH = sb("bDBH", [128, 2, DIN], BF16, es)
            DTH = sb("bDTH", [128, 2, 32], BF16, es)
            bDTH = B_()
            EA = sb("bEA", [128, DIN], F32, es)
            T1 = sb("bT1", [128, DIN], F32, es)
            YZ2 = [sb(f"bYZ{i}", [128, DIN], F32, es) for i in range(2)]
            SQ = sb("bSQ", [128, DIN], BF16, es)
            RST2 = [sb(f"bRST{i}", [128, 8, 128], F32, es) for i in range(2)]
            bYZ2, bRST2 = [B_(), B_()], [B_(), B_()]
            NW = sb("bNW", [128, 16], F32, es)
            bCT, bYPl, bSBLl, bZSl, bYNs = [B_(), B_()], [B_(), B_()], [B_(), B_()], [B_(), B_()], [B_(), B_()]
            bSBr, bSBb, bCC2, bAN, bDTA, bDBC, bEA, bT1, bYZ_, bSQ, bRST_, bNW = [B_() for _ in range(12)]
            bCCI, bCCO = B_(), B_()
            bPS = [B_() for _ in range(8)]
            P.dma('sp', 'st0', lambda e: e.dma_start(out=CCI_d, in_=SA), writes=[bCCI])
            P.dma('pool', 'cc', lambda e: e.collective_compute("AllGather", ALU.bypass, replica_groups=pairs,
                                                               ins=[CCI_d.opt()], outs=[CCO_d.opt()]),
                  reads=[bCCI], writes=[bCCO], inc=1)
            P.dma('sp', 'ld0', lambda e: e.dma_start(out=AN, in_=a_log[l][:, 32:64]), writes=[bAN])
            P.op('act', lambda e: e.activation(out=AN, in_=AN, func=AF.Exp), reads=[bAN], writes=[bAN])
            P.op('dve', lambda e: e.tensor_scalar_mul(out=AN, in0=AN, scalar1=-1.0), reads=[bAN], writes=[bAN])
            P.dma('sp', 'ld1', lambda e: e.dma_start(out=NW, in_=norm_w[l]), writes=[bNW])
            P.op('dve', lambda e: e.memset(SBr, 0.0), writes=[bSBr])
            P.op('dve', lambda e: e.memset(SBb, 0.0), writes=[bSBb])
            order = ([] if last else [1, 0]) + list(range(NCH - 1, 1, -1))
            EA2 = [EA, sb("bEA2", [128, DIN], F32, es)]
            bEA2 = [B_(), B_()]

            def emit_loads(it, c):
                i = it % 2
                co = chunk_col(c)
                P.dma('sp', f'ld{3 + i}', lambda e, i=i, co=co: e.dma_start(out=CT[i], in_=xbc_v[:, 24:32, co:co + 128]), writes=[bCT[i]])
                P.dma('sp', f'ld{5 + i}', lambda e, i=i, c=c: e.dma_start(out=YPl[i].rearrange("p (b t) -> p b t", t=128), in_=yd_v[:, :, c * 128:(c + 1) * 128]),
                      writes=[bYPl[i]])
                P.dma('sp', f'ld{7 + i}', lambda e, i=i, c=c: e.dma_start(out=SBLl[i], in_=SB_d[c]), writes=[bSBLl[i]])
                P.dma('sp', f'ld{9 + i}', lambda e, i=i, c=c: e.dma_start(out=ZSl[i], in_=zs_v[:, :, c * 128:(c + 1) * 128]), writes=[bZSl[i]])

            def emit_EA(it, c):
                sl = it % 2
                P.op('dve', lambda e, c=c: e.tensor_tensor(out=DTA, in0=DTT[:, c, 32:64], in1=AN, op=ALU.mult), reads=[bAN], writes=[bDTA])
                P.op('dve', lambda e: e.tensor_copy(out=DTH[:, 0, :], in_=DTA), reads=[bDTA], writes=[bDTH])
                P.op('dve', lambda e: e.tensor_tensor(out=DTH[:, 1, :], in0=DTA, in1=DTH[:, 0, :], op=ALU.subtract), reads=[bDTA, bDTH], writes=[bDTH])
                for hl in range(2):
                    P.op('dve', lambda e, hl=hl: e.tensor_copy(out=DBH[:, hl, :].rearrange("p (h q) -> p h q", q=64), in_=bc(DTH[:, hl, :], [[1, 32], [0, 64]])),
                         reads=[bDTH], writes=[bDBC])
                for b4 in range(4):
                    def mme(e, b4=b4):
                        ins = None
                        for q in range(4):
                            pp = b4 * 4 + q
                            e.matmul(PS[b4][:, q * 128:(q + 1) * 128], lhsT=DBH[:, 0, pp * 128:(pp + 1) * 128], rhs=CMB[:, 4, :],
                                     start=True, stop=False)
                            ins = e.matmul(PS[b4][:, q * 128:(q + 1) * 128], lhsT=DBH[:, 1, pp * 128:(pp + 1) * 128], rhs=CMB[:, 4, :],
                                           start=False, stop=True)
                        return ins
                    P.op('pe', mme, reads=[bDBC], writes=[bPS[b4]])
                    P.op('act', lambda e, b4=b4, sl=sl: e.activation(out=EA2[sl][:, b4 * 512:(b4 + 1) * 512], in_=PS[b4], func=AF.Exp),
                         reads=[bPS[b4]], writes=[bEA2[sl]])

            emit_loads(0, order[0])
            emit_EA(0, order[0])
            for it, c in enumerate(order):
                i = it % 2
                sl = it % 2
                if c == NCH - 1:
                    P.dma('sp', 'ld2', lambda e: e.dma_start(out=CC2, in_=CCO_d.rearrange("(r p) n -> p r n", p=128)),
                          reads=[bCCO], writes=[bCC2])
                    P.op('dve', lambda e: e.tensor_scalar(out=SBr, in0=CC2[:, 0, :], scalar1=SELM[:, 0:1], scalar2=None, op0=ALU.mult),
                         reads=[bCC2], writes=[bSBr])
                    P.op('dve', lambda e: e.scalar_tensor_tensor(out=SBr, in0=CC2[:, 1, :], scalar=SELM[:, 1:2], in1=SBr,
                                                                 op0=ALU.mult, op1=ALU.add), reads=[bCC2, bSBr], writes=[bSBr])
                    P.op('act', lambda e: e.activation(out=SBb, in_=SBr, func=AF.Copy), reads=[bSBr], writes=[bSBb])
                if it + 1 < len(order):
                    emit_loads(it + 1, order[it + 1])
                for b4 in range(4):
                    def mmi(e, b4=b4, i=i):
                        ins = None
                        for q in range(4):
                            pp = b4 * 4 + q
                            ins = e.matmul(PS[4 + b4][:, q * 128:(q + 1) * 128], lhsT=SBb[:, pp * 128:(pp + 1) * 128],
                                           rhs=CT[i][:, pp // 2, :], start=True, stop=True)
                        return ins
                    P.op('pe', mmi, reads=[bSBb, bCT[i]], writes=[bPS[4 + b4]])
                    P.op('dve', lambda e, b4=b4, sl=sl: e.tensor_tensor(out=T1[:, b4 * 512:(b4 + 1) * 512], in0=PS[4 + b4], in1=EA2[sl][:, b4 * 512:(b4 + 1) * 512], op=ALU.mult),
                         reads=[bPS[4 + b4], bEA2[sl]], writes=[bT1])
                P.op('dve', lambda e, i=i: e.tensor_tensor(out=T1, in0=T1, in1=YPl[i], op=ALU.add), reads=[bT1, bYPl[i]], writes=[bT1])
                if debug and l == 0:
                    P.dma('sp', 'st5', lambda e, c=c: e.dma_start(out=DBG["yt"].rearrange("(blk p) t -> p blk t", p=128)[:, :, c * 128:(c + 1) * 128],
                                                                 in_=T1.rearrange("p (b t) -> p b t", t=128)), reads=[bT1])
                P.op('dve', lambda e, c=c: e.tensor_tensor(out=SBr.rearrange("p (h q) -> p h q", q=64), in0=SBr.rearrange("p (h q) -> p h q", q=64),
                                                           in1=bc(DECB[:, c, :], [[1, 32], [0, 64]]), op=ALU.mult), reads=[bSBr], writes=[bSBr])
                P.op('dve', lambda e, i=i: e.tensor_tensor(out=SBr, in0=SBr, in1=SBLl[i], op=ALU.add), reads=[bSBr, bSBLl[i]], writes=[bSBr])
                P.op('act', lambda e: e.activation(out=SBb, in_=SBr, func=AF.Copy), reads=[bSBr], writes=[bSBb])
                if it + 1 < len(order) and order[it + 1] != NCH - 1:
                    emit_EA(it + 1, order[it + 1])
                YZ, RST, bYZ, bRST = YZ2[i], RST2[i], bYZ2[i], bRST2[i]
                P.op('dve', lambda e, i=i, YZ=YZ: e.tensor_tensor(out=YZ, in0=T1, in1=ZSl[i].rearrange("p b t -> p (b t)"), op=ALU.mult),
                     reads=[bT1, bZSl[i]], writes=[bYZ])
                P.op('act', lambda e, YZ=YZ: e.activation(out=bass.AP(SQ.tensor, SQ.offset, [list(SQ.ap[0]), [128, 8], [1024, 2], [1, 128]]),
                                                          in_=YZ.rearrange("p (g r t) -> p g r t", r=2, t=128), func=AF.Square), reads=[bYZ], writes=[bSQ])
                for b2 in range(2):
                    def mmn(e, b2=b2):
                        e.matmul(PS[4 + b2], lhsT=ONESB, rhs=SQ[:, b2 * 512:(b2 + 1) * 512], start=True, stop=False)
                        return e.matmul(PS[4 + b2], lhsT=ONESB, rhs=SQ[:, 1024 + b2 * 512:1024 + (b2 + 1) * 512], start=False, stop=True)
                    P.op('pe', mmn, reads=[bSQ], writes=[bPS[4 + b2]])
                    P.op('act', lambda e, b2=b2, RST=RST: e.activation(out=RST[:, b2 * 4:(b2 + 1) * 4, :].rearrange("p g t -> p (g t)"), in_=PS[4 + b2],
                                                                       func=AF.Ln, scale=1.0 / 256, bias=EPS), reads=[bPS[4 + b2]], writes=[bRST])
                P.op('act', lambda e, RST=RST: e.activation(out=RST, in_=RST, func=AF.Exp, scale=-0.5), reads=[bRST], writes=[bRST])
                P.op('pool', lambda e, i=i, YZ=YZ, RST=RST: e.tensor_tensor(out=YNs[i].rearrange("p (g r) t -> p g r t", r=2),
                                                                            in0=YZ.rearrange("p (g r t) -> p g r t", r=2, t=128),
                                                                            in1=bc(RST, [[128, 8], [0, 2], [1, 128]]), op=ALU.mult),
                     reads=[bYZ, bRST], writes=[bYNs[i]])
                if it + 1 < len(order) and order[it + 1] == NCH - 1:
                    emit_EA(it + 1, order[it + 1])
                P.dma('sp', f'st{1 + i}', lambda e, i=i, c=c: e.dma_start(out=ynd_v[:, :, c * 128:(c + 1) * 128], in_=YNs[i]), reads=[bYNs[i]])


        gs_v = GS_d.rearrange("(blk p) t -> p blk t", p=128)
        pm2_v = PM2_d.rearrange("(blk p) t -> p blk t", p=128)
        gd_v = G_d.rearrange("(blk p) t -> p blk t", p=128)

        def load_weight(dst, src_v, nk, ncols, keybase, bdst):
            n = 0
            for k0 in range(0, nk, 8):
                k1 = min(nk, k0 + 8)
                for c0 in range(0, ncols, 512):
                    c1 = min(ncols, c0 + 512)
                    P.dma('pool', f'w{keybase + n % 4}', lambda e, k0=k0, k1=k1, c0=c0, c1=c1: e.dma_start(
                        out=dst[:, k0:k1, c0:c1], in_=src_v[:, k0:k1, c0:c1]), writes=[bdst])
                    n += 1

        def mixer_dense(l, es, last):
            WSO = sb("dWSO", [128, 16, D], BF16, es)
            WPO = sb("dWPO", [128, 8, D], BF16, es)
            WO = sb("dWO", [128, 8, D], BF16, es)
            YNt = [sb(f"dYN{i}", [128, 16, 512], BF16, es) for i in range(2)]
            PMt = [sb(f"dPM{i}", [128, 8, 512], BF16, es) for i in range(2)]
            GSt = sb("dGS", [128, 16, 512], BF16, es)
            Xt = sb("dX", [128, KD, 512], F32, es)
            MT = sb("dMT", [128, KD, 512], BF16, es)
            Ta = [sb(f"dTa{i}", [128, 512], F32, es) for i in range(2)]
            Tb = [sb(f"dTb{i}", [128, 512], F32, es) for i in range(2)]
            bWSO, bWPO, bWO, bGS, bX, bMT = B_(), B_(), B_(), B_(), B_(), B_()
            bYN, bPM, bTa, bTb = [B_(), B_()], [B_(), B_()], [B_(), B_()], [B_(), B_()]
            bPS = [B_() for _ in range(8)]
            load_weight(WSO, w_ssd_out[l].rearrange("(k p) n -> p k n", p=128), 16, D, 0, bWSO)
            NWm = sb("dNW", [128, 16], F32, es)
            bNWm = B_()
            P.dma('sp', 'ld6', lambda e: e.dma_start(out=NWm, in_=norm_w[l]), writes=[bNWm])
            for kb in range(16):
                P.op('dve', lambda e, kb=kb: e.tensor_scalar(out=WSO[:, kb, :], in0=WSO[:, kb, :], scalar1=NWm[:, kb:kb + 1], scalar2=None, op0=ALU.mult),
                     reads=[bWSO, bNWm], writes=[bWSO])
            load_weight(WPO, w_pool_out[l].rearrange("(k p) n -> p k n", p=128), 8, D, 0, bWPO)
            load_weight(WO, w_out[l].rearrange("(k p) n -> p k n", p=128), 8, D, 0, bWO)
            tiles = ([] if last else [(0, 256, 1)]) + [(256 + 512 * i, 512, 0) for i in range(4)]
            np_ = 0
            for it, (t0, w, mc) in enumerate(tiles):
                i = it % 2
                P.dma('sp', f'ld{i}', lambda e, i=i, t0=t0, w=w: e.dma_start(out=YNt[i][:, :, :w], in_=ynd_v[:, :, t0:t0 + w]), writes=[bYN[i]])
                P.dma('sp', f'ld{2 + i}', lambda e, i=i, t0=t0, w=w: e.dma_start(out=PMt[i][:, :, :w], in_=pm2_v[:, :, t0:t0 + w]), writes=[bPM[i]])
                P.dma('sp', 'ld4', lambda e, t0=t0, w=w: e.dma_start(out=GSt[:, :, :w], in_=gs_v[:, :, t0:t0 + w]), writes=[bGS])
                P.dma('sp', 'ld5', lambda e, t0=t0, w=w: e.dma_start(out=Xt[:, :, :w], in_=xtd_v[:, :, t0:t0 + w]), writes=[bX])
                for db in range(KD):
                    pa = np_ % 4
                    pb = 4 + np_ % 4
                    r = np_ % 2
                    np_ += 1

                    def mma(e, db=db, i=i, w=w, pa=pa):
                        ins = None
                        for kb in range(16):
                            ins = e.matmul(PS[pa][:, :w], lhsT=WSO[:, kb, db * 128:(db + 1) * 128], rhs=YNt[i][:, kb, :w],
                                           start=(kb == 0), stop=(kb == 15))
                        return ins
                    P.op('pe', mma, reads=[bWSO, bYN[i]], writes=[bPS[pa]])

                    def mmb(e, db=db, i=i, w=w, pb=pb):
                        ins = None
                        for kb in range(8):
                            ins = e.matmul(PS[pb][:, :w], lhsT=WPO[:, kb, db * 128:(db + 1) * 128], rhs=PMt[i][:, kb, :w],
                                           start=(kb == 0), stop=(kb == 7))
                        return ins
                    P.op('pe', mmb, reads=[bWPO, bPM[i]], writes=[bPS[pb]])
                    P.op('dve', lambda e, r=r, db=db, w=w, pa=pa: e.tensor_tensor(out=Ta[r][:, :w], in0=PS[pa][:, :w], in1=GSt[:, db, :w], op=ALU.mult),
                         reads=[bPS[pa], bGS], writes=[bTa[r]])
                    P.op('dve', lambda e, r=r, db=db, w=w, pb=pb: e.tensor_tensor(out=Tb[r][:, :w], in0=PS[pb][:, :w], in1=GSt[:, 8 + db, :w], op=ALU.mult),
                         reads=[bPS[pb], bGS], writes=[bTb[r]])
                    P.op('pool', lambda e, r=r, db=db, w=w: e.tensor_tensor(out=MT[:, db, :w], in0=Ta[r][:, :w], in1=Tb[r][:, :w], op=ALU.add),
                         reads=[bTa[r], bTb[r]], writes=[bMT])
                for db in range(KD):
                    pa = np_ % 4
                    np_ += 1

                    def mmo(e, db=db, w=w, pa=pa):
                        ins = None
                        for kb in range(8):
                            ins = e.matmul(PS[pa][:, :w], lhsT=WO[:, kb, db * 128:(db + 1) * 128], rhs=MT[:, kb, :w],
                                           start=(kb == 0), stop=(kb == 7))
                        return ins
                    P.op('pe', mmo, reads=[bWO, bMT], writes=[bPS[pa]])
                    P.op('dve', lambda e, db=db, w=w, pa=pa, mc=mc: e.scalar_tensor_tensor(
                        out=Xt[:, db, :w], in0=PS[pa][:, :w], scalar=MOD[l][:, 16 + db, mc:mc + 1], in1=Xt[:, db, :w],
                        op0=ALU.mult, op1=ALU.add), reads=[bPS[pa], bX], writes=[bX])
                P.dma('sp', 'st0', lambda e, t0=t0, w=w: e.dma_start(out=xtd_v[:, :, t0:t0 + w], in_=Xt[:, :, :w]), reads=[bX])

        def ffn_up(l, HT, es, last):
            Wa = [sb(f"fWa{i}", [128, KD, 512], BF16, es) for i in range(2)]
            Wb = [sb(f"fWb{i}", [128, KD, 512], BF16, es) for i in range(2)]
            SI = [sb(f"fSI{i}", [128, 512], F32, es) for i in range(2)]
            GST = [sb(f"fG{i}", [128, TT], BF16, es) for i in range(2)]
            bWa, bWb, bSI, bGST = [B_(), B_()], [B_(), B_()], [B_(), B_()], [B_(), B_()]
            bPS = [B_() for _ in range(8)]
            bHT = B_()
            wv = w_gate_up[l].rearrange("(k p) n -> p k n", p=128)
            tiles = ([] if last else [(0, 256)]) + [(256 + 512 * i, 512) for i in range(4)]
            np_ = 0
            ng = 0
            for jb in range(6):
                j = jb % 2
                ncol = 512 if jb < 5 else 256
                P.dma('pool', f'w{j}', lambda e, j=j, jb=jb, ncol=ncol: e.dma_start(out=Wa[j][:, :, :ncol], in_=wv[:, :, jb * 512:jb * 512 + ncol]),
                      writes=[bWa[j]])
                P.dma('pool', f'w{2 + j}', lambda e, j=j, jb=jb, ncol=ncol: e.dma_start(out=Wb[j][:, :, :ncol], in_=wv[:, :, FFH + jb * 512:FFH + jb * 512 + ncol]),
                      writes=[bWb[j]])
                for q in range(ncol // 128):
                    hb = jb * 4 + q
                    gi = ng % 2
                    ng += 1
                    for (t0, w) in tiles:
                        pa = np_ % 4
                        pb = 4 + np_ % 4
                        r = np_ % 2
                        np_ += 1

                        def mma(e, j=j, q=q, t0=t0, w=w, pa=pa):
                            ins = None
                            for k in range(KD):
                                ins = e.matmul(PS[pa][:, :w], lhsT=Wa[j][:, k, q * 128:(q + 1) * 128], rhs=HT[:, k, t0:t0 + w],
                                               start=(k == 0), stop=(k == KD - 1))
                            return ins
                        P.op('pe', mma, reads=[bWa[j], bHT], writes=[bPS[pa]])

                        def mmb(e, j=j, q=q, t0=t0, w=w, pb=pb):
                            ins = None
                            for k in range(KD):
                                ins = e.matmul(PS[pb][:, :w], lhsT=Wb[j][:, k, q * 128:(q + 1) * 128], rhs=HT[:, k, t0:t0 + w],
                                               start=(k == 0), stop=(k == KD - 1))
                            return ins
                        P.op('pe', mmb, reads=[bWb[j], bHT], writes=[bPS[pb]])
                        P.op('act', lambda e, r=r, w=w, pa=pa: e.activation(out=SI[r][:, :w], in_=PS[pa][:, :w], func=AF.Silu),
                             reads=[bPS[pa]], writes=[bSI[r]])
                        P.op('dve', lambda e, r=r, w=w, pb=pb, gi=gi, t0=t0: e.tensor_tensor(out=GST[gi][:, t0:t0 + w], in0=PS[pb][:, :w], in1=SI[r][:, :w], op=ALU.mult),
                             reads=[bPS[pb], bSI[r]], writes=[bGST[gi]])
                    clo = NCTX if last else 0
                    P.dma('sp', f'st{gi}', lambda e, gi=gi, hb=hb, clo=clo: e.dma_start(out=G_d[hb * 128:(hb + 1) * 128, clo:TT], in_=GST[gi][:, clo:TT]),
                          reads=[bGST[gi]])

        def ffn_down(l, es, last):
            WD = sb("gWD", [128, 22, D], BF16, es)
            Gt = [sb(f"gG{i}", [128, 22, 512], BF16, es) for i in range(2)]
            Xt = [sb(f"gX{i}", [128, KD, 512], F32, es) for i in range(2)]
            SQ = sb("gSQ", [128, KD, 512], F32, es)
            RS = sb("gRS", [128, 512], F32, es)
            GF = sb("gGF", [128, KD], F32, es)
            HX = sb("gHX", [128, KD, 2], F32, es)
            HX2 = sb("gHX2", [128, 2, 16], F32, es)
            bWD, bSQ, bRS, bGF, bHX, bHX2, bHCI, bHCO = B_(), B_(), B_(), B_(), B_(), B_(), B_(), B_()
            bG, bX = [B_(), B_()], [B_(), B_()]
            bPS = [B_() for _ in range(8)]
            load_weight(WD, w_down[l].rearrange("(k p) n -> p k n", p=128), 22, D, 0, bWD)
            if last:
                P.dma('sp', 'ld6', lambda e: e.dma_start(out=GF, in_=g_final), writes=[bGF])
            tiles = ([] if last else [(0, 256, 1)]) + [(256 + 512 * i, 512, 0) for i in range(4)]
            np_ = 0
            for it, (t0, w, mc) in enumerate(tiles):
                i = it % 2
                P.dma('sp', f'ld{i}', lambda e, i=i, t0=t0, w=w: e.dma_start(out=Gt[i][:, :, :w], in_=gd_v[:, :, t0:t0 + w]), writes=[bG[i]])
                P.dma('sp', f'ld{2 + i}', lambda e, i=i, t0=t0, w=w: e.dma_start(out=Xt[i][:, :, :w], in_=xtd_v[:, :, t0:t0 + w]), writes=[bX[i]])
                for db in range(KD):
                    pa = np_ % 4
                    np_ += 1

                    def mmd(e, db=db, i=i, w=w, pa=pa):
                        ins = None
                        for kb in range(22):
                            ins = e.matmul(PS[pa][:, :w], lhsT=WD[:, kb, db * 128:(db + 1) * 128], rhs=Gt[i][:, kb, :w],
                                           start=(kb == 0), stop=(kb == 21))
                        return ins
                    P.op('pe', mmd, reads=[bWD, bG[i]], writes=[bPS[pa]])
                    P.op('dve', lambda e, db=db, i=i, w=w, pa=pa, mc=mc: e.scalar_tensor_tensor(
                        out=Xt[i][:, db, :w], in0=PS[pa][:, :w], scalar=MOD[l][:, 40 + db, mc:mc + 1], in1=Xt[i][:, db, :w],
                        op0=ALU.mult, op1=ALU.add), reads=[bPS[pa], bX[i]], writes=[bX[i]])
                if not last:
                    P.dma('sp', f'st{i}', lambda e, i=i, t0=t0, w=w: e.dma_start(out=xtd_v[:, :, t0:t0 + w], in_=Xt[i][:, :, :w]), reads=[bX[i]])
                    if t0 + w == TT:
                        for hh in range(2):
                            P.op('dve', lambda e, i=i, hh=hh: e.tensor_copy(out=HX[:, :, hh], in_=Xt[i][:, :, 511 - hh]), reads=[bX[i]], writes=[bHX])
                        P.dma('sp', 'st2', lambda e: e.dma_start(out=HCI_d, in_=HX.rearrange("p k t -> p (k t)")), reads=[bHX], writes=[bHCI])
                        P.dma('pool', 'cc', lambda e: e.collective_compute("AllGather", ALU.bypass, replica_groups=pairs,
                                                                           ins=[HCI_d.opt()], outs=[HCO_d.opt()]),
                              reads=[bHCI], writes=[bHCO], inc=1)
                        P.dma('sp', 'ld4', lambda e: e.dma_start(out=HX2, in_=HCO_d.rearrange("(r p) n -> p r n", p=128)), reads=[bHCO], writes=[bHX2])
                        P.op('dve', lambda e: e.tensor_scalar(out=HX.rearrange("p k t -> p (k t)"), in0=HX2[:, 0, :], scalar1=SELM[:, 0:1], scalar2=None, op0=ALU.mult),
                             reads=[bHX2], writes=[bHX])
                        P.op('dve', lambda e: e.scalar_tensor_tensor(out=HX.rearrange("p k t -> p (k t)"), in0=HX2[:, 1, :], scalar=SELM[:, 1:2],
                                                                     in1=HX.rearrange("p k t -> p (k t)"), op0=ALU.mult, op1=ALU.add),
                             reads=[bHX2, bHX], writes=[bHX])
                        P.dma('sp', 'st3', lambda e: e.dma_start(out=xtd_v[:, :, TT:TT + 2], in_=HX), reads=[bHX])
                else:
                    P.op('act', lambda e, i=i: e.activation(out=SQ, in_=Xt[i], func=AF.Square), reads=[bX[i]], writes=[bSQ])

                    def mmq(e):
                        ins = None
                        for k in range(KD):
                            ins = e.matmul(PS[4], lhsT=ONES, rhs=SQ[:, k, :], start=(k == 0), stop=(k == KD - 1))
                        return ins
                    P.op('pe', mmq, reads=[bSQ], writes=[bPS[4]])
                    P.op('act', lambda e: e.activation(out=RS, in_=PS[4], func=AF.Ln, scale=1.0 / D, bias=EPS), reads=[bPS[4]], writes=[bRS])
                    P.op('act', lambda e: e.activation(out=RS, in_=RS, func=AF.Exp, scale=-0.5), reads=[bRS], writes=[bRS])
                    for k in range(KD):
                        P.op('dve', lambda e, k=k, i=i: e.scalar_tensor_tensor(out=SQ[:, k, :], in0=Xt[i][:, k, :], scalar=GF[:, k:k + 1], in1=RS,
                                                                               op0=ALU.mult, op1=ALU.mult), reads=[bX[i], bRS, bGF, bSQ], writes=[bSQ])
                    P.dma('sp', f'st{i}', lambda e, t0=t0: e.dma_start(out=yT_out.rearrange("(k p) t -> p k t", p=128)[:, :, t0 - NCTX:t0 - NCTX + 512], in_=SQ),
                          reads=[bSQ])

        for l in range(DEPTH):
            last = (l == DEPTH - 1)
            with ExitStack() as es:
                HT = sb("HT", [128, KD, TTH], BF16, es)
                with ExitStack() as es1:
                    norm_mod(l, 1, HT, es1)
                    P.flush()
                if debug and l == 0:
                    P.dma('sp', 'st0', lambda e: e.dma_start(out=DBG["ht"], in_=HT))
                    P.flush()
                if stop_after == 1:
                    return nc
                with ExitStack() as es2:
                    inproj(l, HT, es2)
                    P.flush()
            if stop_after == 2:
                return nc
            with ExitStack() as es:
                sweepA(l, es)
                P.flush()
            if stop_after == 3:
                return nc
            with ExitStack() as es:
                sweepB(l, es, last)
                P.flush()
            if stop_after == 4:
                return nc
            with ExitStack() as es:
                mixer_dense(l, es, last)
                P.flush()
            if stop_after == 5:
                return nc
            with ExitStack() as es:
                HT = sb("HT2", [128, KD, TTH], BF16, es)
                with ExitStack() as es1:
                    norm_mod(l, 2, HT, es1)
                    P.flush()
                with ExitStack() as es1:
                    ffn_up(l, HT, es1, last)
                    P.flush()
            with ExitStack() as es:
                ffn_down(l, es, last)
                P.flush()
            if stop_after == 6:
                return nc
    return nc


def _pool_mats(L, k, rowlen):
    Pm = np.zeros((L, L), np.float32)
    inv = np.zeros(L, np.float32)
    for t in range(L):
        r0 = (t // rowlen) * rowlen
        tt = t - r0
        lo = min(max(tt - k // 2, 0), rowlen)
        hi = min(max(tt + k // 2, 0), rowlen)
        Pm[r0 + lo:r0 + hi, t] = 1.0
        Pm[t, t] -= (hi - lo)
        inv[t] = 1.0 / (hi - lo)
    return Pm, inv


def _consts(rev):
    k = np.arange(128)[:, None]
    j = np.arange(128)[None, :]
    cm = np.stack([(k == j), (k > j), (k < j), (k <= j), (k >= j)], axis=1).astype(np.float32)
    pml = np.zeros((128, 4, 128), np.float32)
    pmc = np.zeros((128, 2, 4, 2, 128), np.float32)
    invc = np.zeros((4, 384), np.float32)
    for g, kk in enumerate((2, 4, 8, 16)):
        Pm, inv = _pool_mats(128, kk, 64)
        Pc, invcx = _pool_mats(256, kk, 256)
        if rev:
            Pm = Pm[::-1, ::-1]
            inv = inv[::-1]
            Pc = Pc[::-1, ::-1]
            invcx = invcx[::-1]
        pml[:, g, :] = Pm
        for tb_ in range(2):
            for tb in range(2):
                pmc[:, tb_, g, tb, :] = Pc[tb_ * 128:(tb_ + 1) * 128, tb * 128:(tb + 1) * 128]
        invc[g, :NCTX] = invcx
        invc[g, NCTX:] = inv
    invc = np.ascontiguousarray(np.broadcast_to(invc[None], (128, 4, 384)))
    return cm, pml, pmc, invc


def prep_inputs(inputs, n_cores=8):
    f = lambda a: np.ascontiguousarray(np.asarray(a, dtype=np.float32))
    inp = {k: np.asarray(v) for k, v in inputs.items()}
    pm = lambda v: f(v.reshape(-1, 128).T)

    def stack(fn):
        return f(np.stack([fn(l) for l in range(DEPTH)]))
    shared = dict(
        w_ada=f(inp["w_ada"]), w_ssd_out=f(inp["w_ssd_out"]), pool_w=f(inp["pool_w"]),
        w_pool_out=f(inp["w_pool_out"]), w_out=f(inp["w_out"]), w_gate_up=f(inp["w_gate_up"]),
        w_down=f(inp["w_down"]),
        b_ada=stack(lambda l: pm(inp["b_ada"][l])), g_mix=stack(lambda l: pm(inp["g_mix"][l])),
        g_ffn=stack(lambda l: pm(inp["g_ffn"][l])), conv_b=stack(lambda l: pm(inp["conv_b"][l])),
        norm_w=stack(lambda l: pm(inp["ssd_norm_w"][l])), pool_scale=stack(lambda l: pm(inp["pool_scale"][l])),
        g_final=pm(inp["g_final"]),
    )
    per_hf = []
    for hf in range(2):
        rev = hf == 1
        dA, dB = (1, 0) if rev else (0, 1)
        dtc = lambda d: np.arange(6144 + d * 32, 6144 + d * 32 + 32)
        idx = np.concatenate([np.arange(0, 2048), np.arange(2048, 6144), np.arange(7232, 9280),
                              np.arange(6208, 7232), dtc(dA), dtc(dB)])
        cm, pml, pmc, invc = _consts(rev)

        def cw(l):
            w = inp["conv_w"][l]
            if rev:
                w = w[::-1]
            return w.T.reshape(32, 128, 5).transpose(1, 0, 2)

        def bc64(v):
            r = np.concatenate([v[dA], v[dB]])
            return np.broadcast_to(r[None], (128, 64))

        def dsk(l, d):
            v = inp["d_skip"][l][d]
            return np.repeat(v.reshape(16, 2, 1), 64, axis=2).reshape(16, 128).T
        sel = np.zeros((128, 2), np.float32)
        sel[:, 1 - hf] = 1.0
        per_hf.append(dict(
            w_in=f(inp["w_in"][:, :, idx]),
            conv_w=stack(cw), dt_bias=stack(lambda l: bc64(inp["dt_bias"][l])),
            a_log=stack(lambda l: bc64(inp["a_log"][l])),
            dskA=stack(lambda l: dsk(l, dA)), dskB=stack(lambda l: dsk(l, dB)),
            cmat=f(cm), pm_lat=f(pml), pm_ctx=f(pmc), invc=f(invc), selm=f(sel),
        ))
    in_maps = []
    for core in range(n_cores):
        b, hf = core // 2, core % 2
        xb = inp["x"][b]
        cx = inp["ctx"][b]
        if hf == 0:
            lat, halo, cl = xb[:NLAT], xb[NLAT:NLAT + 2], cx
        else:
            lat, halo, cl = xb[NLAT:][::-1], xb[NLAT - 2:NLAT][::-1], cx[::-1]
        xT = f(np.concatenate([cl, lat, halo], axis=0).T)
        cc = f(np.stack([pm(inp["c"][b]), pm(inp["c_ctx"])], axis=-1))
        m = dict(xT=xT, cc=cc)
        m.update(shared)
        m.update(per_hf[hf])
        in_maps.append(m)
    return in_maps


def assemble(results, n_cores=8):
    B = n_cores // 2
    out = np.zeros((B, 2 * NLAT, D), np.float32)
    for core in range(n_cores):
        b, hf = core // 2, core % 2
        y = np.asarray(results[core]["yT"]).T
        if hf == 0:
            out[b, :NLAT] = y
        else:
            out[b, NLAT:] = y[::-1]
    return out


_NC_CACHE = {}


def kernel(**inputs):
    n = 8
    if "nc" not in _NC_CACHE:
        _NC_CACHE["nc"] = build(n)
    nc = _NC_CACHE["nc"]
    in_maps = prep_inputs(inputs, n)
    res = run_bass_kernel_spmd(nc, in_maps, core_ids=list(range(n)))
    return assemble(res.results, n)
```
